# Optimizing a Trainium2 kernel written in Bass

```python
import math
import jax, jax.numpy as jnp
from jax import lax
import numpy as np

D_MODEL = 1024
BATCH = 8
SEQ = 2048
DEPTH = 4

MEM_LEN = 256
HEAD_DIM = 64
N_HEADS = D_MODEL // HEAD_DIM
N_KV_GROUPS = 4
HEADS_PER_GROUP = N_HEADS // N_KV_GROUPS
CMP_BLOCK = 32
CMP_STRIDE = 16
SEL_BLOCK = 64
SEL_TOP_N = 16
WINDOW = 512
Q_BLOCK = 128
SEL_Q_CHUNK = 16
CONV_CH = D_MODEL
CONV_WIDTH = 31
X_HEADS = 4
X_HEAD_DIM = D_MODEL // X_HEADS
D_FF = int(math.ceil(8 * D_MODEL / 3 / 256)) * 256
REL_BUCKETS = 32
REL_MAX_DIST = 128
EPS = 1e-6

SPLIT_SIZES = (
    2 * CONV_CH,
    N_HEADS * HEAD_DIM,
    N_KV_GROUPS * HEAD_DIM,
    N_KV_GROUPS * HEAD_DIM,
    N_KV_GROUPS * HEAD_DIM,
    N_KV_GROUPS * HEAD_DIM,
    N_KV_GROUPS * HEAD_DIM,
    N_KV_GROUPS * HEAD_DIM,
    3 * N_HEADS,
    D_MODEL,
    D_MODEL,
)
N_IN = sum(SPLIT_SIZES)
SPLIT_POINTS = tuple(int(v) for v in np.cumsum(SPLIT_SIZES)[:-1])

kernel_name = "hybrid_conformer_nsa_block"


def rms_norm(x, g):
    xf = x.astype(jnp.float32)
    y = xf * lax.rsqrt(jnp.mean(xf * xf, axis=-1, keepdims=True) + EPS)
    return (y * g.astype(jnp.float32)).astype(x.dtype)


def layer_norm(x, g, b):
    xf = x.astype(jnp.float32)
    mu = jnp.mean(xf, axis=-1, keepdims=True)
    xc = xf - mu
    y = xc * lax.rsqrt(jnp.mean(xc * xc, axis=-1, keepdims=True) + EPS)
    return (y * g.astype(jnp.float32) + b.astype(jnp.float32)).astype(x.dtype)


def t5_bucket(dist):
    n = jnp.maximum(dist, 0)
    max_exact = REL_BUCKETS // 2
    nf = jnp.maximum(n, 1).astype(jnp.float32)
    large = max_exact + (jnp.log(nf / max_exact) / math.log(REL_MAX_DIST / max_exact)
                         * (REL_BUCKETS - max_exact)).astype(jnp.int32)
    large = jnp.minimum(large, REL_BUCKETS - 1)
    return jnp.where(n < max_exact, n, large)


def masked_softmax(s, mask):
    s = jnp.where(mask, s, -1e30)
    m = jnp.max(s, axis=-1, keepdims=True)
    p = jnp.where(mask, jnp.exp(s - m), 0.0)
    return p / jnp.maximum(jnp.sum(p, axis=-1, keepdims=True), 1e-30)


def conv_module(u, dw_w, dw_b, ln_g, ln_b, pw_w):
    a, gt = jnp.split(u, 2, axis=-1)
    v = a * jax.nn.sigmoid(gt)
    v = lax.conv_general_dilated(
        v, dw_w[:, None, :], window_strides=(1,),
        padding=((CONV_WIDTH - 1, 0),),
        dimension_numbers=("NWC", "WIO", "NWC"),
        feature_group_count=CONV_CH) + dw_b
    v = jax.nn.silu(layer_norm(v, ln_g, ln_b))
    return v @ pw_w


def compress_blocks(blocks, pos, w1, w2):
    z = blocks + pos[None, None, :, None, :]
    z = jnp.einsum("bnlgd,lde->bnge", z, w1)
    return jax.nn.silu(z) @ w2


def nsa_attention(q, kc, vc, ks, vs, kw, vw, gate_logits, cmp_pos, cmp_w1, cmp_w2, rel_bias):
    B, S = q.shape[0], q.shape[1]
    G, R, Dh = N_KV_GROUPS, HEADS_PER_GROUP, HEAD_DIM
    scale = 1.0 / math.sqrt(Dh)
    qg = q.reshape(B, S, G, R, Dh)
    t = jnp.arange(S)

    n_cmp = (S - CMP_BLOCK) // CMP_STRIDE + 1
    cidx = np.arange(n_cmp)[:, None] * CMP_STRIDE + np.arange(CMP_BLOCK)[None, :]
    k_cmp = compress_blocks(kc[:, cidx], cmp_pos[0], cmp_w1[0], cmp_w2[0])
    v_cmp = compress_blocks(vc[:, cidx], cmp_pos[1], cmp_w1[1], cmp_w2[1])
    c_end = jnp.arange(n_cmp) * CMP_STRIDE + CMP_BLOCK - 1
    dist_c = t[:, None] - c_end[None, :]
    bias_c = rel_bias[t5_bucket(dist_c)].reshape(S, n_cmp, G, R).transpose(2, 3, 0, 1)
    s_c = jnp.einsum("bsgrd,bngd->bgrsn", qg, k_cmp).astype(jnp.float32) * scale + bias_c
    p_cmp = masked_softmax(s_c, dist_c >= 0)
    o_cmp = jnp.einsum("bgrsn,bngd->bsgrd", p_cmp.astype(v_cmp.dtype), v_cmp)

    n_sel = S // SEL_BLOCK
    top_n = min(SEL_TOP_N, n_sel)
    jj = np.arange(n_cmp)[:, None] * CMP_STRIDE
    mm0 = np.arange(n_sel)[None, :] * SEL_BLOCK
    overlap = ((jj < mm0 + SEL_BLOCK) & (jj + CMP_BLOCK > mm0)).astype(np.float32)
    imp = jnp.einsum("bgrsn,nm->bsgm", p_cmp, jnp.asarray(overlap))
    blk_t = (t // SEL_BLOCK)[:, None]
    mm = jnp.arange(n_sel)[None, :]
    forced = (mm == 0) | (mm == blk_t) | (mm == blk_t - 1)
    valid = mm <= blk_t
    score = jnp.where(forced[None, :, None, :], jnp.inf,
                      jnp.where(valid[None, :, None, :], imp, -jnp.inf))
    _, sel_idx = lax.top_k(score, top_n)

    ksg = ks.reshape(B, n_sel, SEL_BLOCK, G, Dh).transpose(0, 3, 1, 2, 4)
    vsg = vs.reshape(B, n_sel, SEL_BLOCK, G, Dh).transpose(0, 3, 1, 2, 4)
    bias_g = rel_bias.reshape(REL_BUCKETS, G, R)
    nc = S // SEL_Q_CHUNK
    q_ch = qg.reshape(B, nc, SEL_Q_CHUNK, G, R, Dh).swapaxes(0, 1)
    i_ch = sel_idx.reshape(B, nc, SEL_Q_CHUNK, G, top_n).swapaxes(0, 1)
    starts = jnp.arange(nc) * SEL_Q_CHUNK
    b_ix = jnp.arange(B)[:, None, None, None]
    g_ix = jnp.arange(G)[None, None, :, None]
    n_keys = top_n * SEL_BLOCK

    def sel_chunk(args):
        qc, ic, st = args
        kg = ksg[b_ix, g_ix, ic].reshape(B, SEL_Q_CHUNK, G, n_keys, Dh)
        vg = vsg[b_ix, g_ix, ic].reshape(B, SEL_Q_CHUNK, G, n_keys, Dh)
        kpos = (ic[..., None] * SEL_BLOCK + jnp.arange(SEL_BLOCK)).reshape(B, SEL_Q_CHUNK, G, n_keys)
        tq = st + jnp.arange(SEL_Q_CHUNK)
        dist = tq[None, :, None, None] - kpos
        bias = bias_g[t5_bucket(dist), g_ix]
        s = jnp.einsum("bqgrd,bqgkd->bqgrk", qc, kg).astype(jnp.float32) * scale \
            + bias.swapaxes(-1, -2)
        p = masked_softmax(s, (dist >= 0)[:, :, :, None, :])
        return jnp.einsum("bqgrk,bqgkd->bqgrd", p.astype(vg.dtype), vg)

    o_slc = lax.map(sel_chunk, (q_ch, i_ch, starts)).swapaxes(0, 1).reshape(B, S, G, R, Dh)

    nq = S // Q_BLOCK
    kwp = jnp.pad(kw, ((0, 0), (WINDOW, 0), (0, 0), (0, 0)))
    vwp = jnp.pad(vw, ((0, 0), (WINDOW, 0), (0, 0), (0, 0)))
    q_bl = qg.reshape(B, nq, Q_BLOCK, G, R, Dh).swapaxes(0, 1)

    def win_block(args):
        qb, i = args
        start = i * Q_BLOCK
        kb = lax.dynamic_slice_in_dim(kwp, start, Q_BLOCK + WINDOW, axis=1)
        vb = lax.dynamic_slice_in_dim(vwp, start, Q_BLOCK + WINDOW, axis=1)
        tq = start + jnp.arange(Q_BLOCK)
        kpos = start - WINDOW + jnp.arange(Q_BLOCK + WINDOW)
        dist = tq[:, None] - kpos[None, :]
        mask = (dist >= 0) & (dist < WINDOW) & (kpos[None, :] >= 0)
        bias = rel_bias[t5_bucket(dist)].reshape(Q_BLOCK, Q_BLOCK + WINDOW, G, R).transpose(2, 3, 0, 1)
        s = jnp.einsum("bqgrd,bkgd->bgrqk", qb, kb).astype(jnp.float32) * scale + bias
        p = masked_softmax(s, mask)
        return jnp.einsum("bgrqk,bkgd->bqgrd", p.astype(vb.dtype), vb)

    o_win = lax.map(win_block, (q_bl, jnp.arange(nq))).swapaxes(0, 1).reshape(B, S, G, R, Dh)

    g = jax.nn.sigmoid(gate_logits).reshape(B, S, 3, G, R, 1)
    o = g[:, :, 0] * o_cmp + g[:, :, 1] * o_slc + g[:, :, 2] * o_win
    return o.reshape(B, S, N_HEADS * HEAD_DIM)


def hybrid_mixer(h, w_in, conv_dw_w, conv_dw_b, conv_ln_g, conv_ln_b, conv_pw_w,
                 cmp_pos, cmp_w1, cmp_w2, w_out, rel_bias):
    B, S = h.shape[0], h.shape[1]
    (u_conv, q, kc, vc, ks, vs, kw, vw, g_nsa_heads, g_conv, g_attn) = jnp.split(
        h @ w_in, SPLIT_POINTS, axis=-1)
    kv_shape = (B, S, N_KV_GROUPS, HEAD_DIM)
    conv_out = conv_module(u_conv, conv_dw_w, conv_dw_b, conv_ln_g, conv_ln_b, conv_pw_w)
    nsa_out = nsa_attention(
        q.reshape(B, S, N_HEADS, HEAD_DIM),
        kc.reshape(kv_shape), vc.reshape(kv_shape), ks.reshape(kv_shape),
        vs.reshape(kv_shape), kw.reshape(kv_shape), vw.reshape(kv_shape),
        g_nsa_heads.reshape(B, S, 3, N_HEADS), cmp_pos, cmp_w1, cmp_w2, rel_bias)
    y = jax.nn.sigmoid(g_conv) * conv_out + jax.nn.sigmoid(g_attn) * nsa_out
    return y @ w_out


def cross_attention(h, mem, wq, wkv, wo):
    B, S = h.shape[0], h.shape[1]
    M = mem.shape[1]
    q = (h @ wq).reshape(B, S, X_HEADS, X_HEAD_DIM)
    k, v = jnp.split(mem @ wkv, 2, axis=-1)
    k = k.reshape(B, M, X_HEADS, X_HEAD_DIM)
    v = v.reshape(B, M, X_HEADS, X_HEAD_DIM)
    s = jnp.einsum("bshd,bmhd->bhsm", q, k).astype(jnp.float32) / math.sqrt(X_HEAD_DIM)
    p = jax.nn.softmax(s, axis=-1).astype(v.dtype)
    o = jnp.einsum("bhsm,bmhd->bshd", p, v).reshape(B, S, X_HEADS * X_HEAD_DIM)
    return o @ wo


def swiglu(h, w_in, w_out):
    a, b = jnp.split(h @ w_in, 2, axis=-1)
    return (jax.nn.silu(a) * b) @ w_out


def setup_inputs(seed: int = 0) -> dict:
    key = jax.random.key(seed)
    ks = jax.random.split(key, 24)
    f32 = jnp.float32

    def nrm(k, shape, scale):
        return jax.random.normal(k, shape, f32) * scale

    def gain(k, shape):
        return 1.0 + 0.02 * jax.random.normal(k, shape, f32)

    return {
        "x": jax.random.normal(ks[0], (BATCH, SEQ, D_MODEL), f32),
        "mem": jax.random.normal(ks[1], (BATCH, MEM_LEN, D_MODEL), f32),
        "norm_mix_g": gain(ks[2], (DEPTH, D_MODEL)),
        "w_in": nrm(ks[3], (DEPTH, D_MODEL, N_IN), D_MODEL ** -0.5),
        "conv_dw_w": nrm(ks[4], (DEPTH, CONV_WIDTH, CONV_CH), CONV_WIDTH ** -0.5),
        "conv_dw_b": nrm(ks[5], (DEPTH, CONV_CH), 0.02),
        "conv_ln_g": gain(ks[6], (DEPTH, CONV_CH)),
        "conv_ln_b": nrm(ks[7], (DEPTH, CONV_CH), 0.02),
        "conv_pw_w": nrm(ks[8], (DEPTH, CONV_CH, D_MODEL), CONV_CH ** -0.5),
        "cmp_pos": nrm(ks[9], (DEPTH, 2, CMP_BLOCK, HEAD_DIM), 0.1),
        "cmp_w1": nrm(ks[10], (DEPTH, 2, CMP_BLOCK, HEAD_DIM, HEAD_DIM), (CMP_BLOCK * HEAD_DIM) ** -0.5),
        "cmp_w2": nrm(ks[11], (DEPTH, 2, HEAD_DIM, HEAD_DIM), HEAD_DIM ** -0.5),
        "w_out": nrm(ks[12], (DEPTH, D_MODEL, D_MODEL), D_MODEL ** -0.5),
        "norm_x_g": gain(ks[13], (DEPTH, D_MODEL)),
        "xq_w": nrm(ks[14], (DEPTH, D_MODEL, X_HEADS * X_HEAD_DIM), D_MODEL ** -0.5),
        "xkv_w": nrm(ks[15], (DEPTH, D_MODEL, 2 * X_HEADS * X_HEAD_DIM), D_MODEL ** -0.5),
        "xo_w": nrm(ks[16], (DEPTH, X_HEADS * X_HEAD_DIM, D_MODEL), (X_HEADS * X_HEAD_DIM) ** -0.5),
        "norm_ffn_g": gain(ks[17], (DEPTH, D_MODEL)),
        "ffn_in_w": nrm(ks[18], (DEPTH, D_MODEL, 2 * D_FF), D_MODEL ** -0.5),
        "ffn_out_w": nrm(ks[19], (DEPTH, D_FF, D_MODEL), D_FF ** -0.5),
        "rel_bias": nrm(ks[20], (REL_BUCKETS, N_HEADS), 0.5),
        "final_norm_g": gain(ks[21], (D_MODEL,)),
    }


def reference(x, mem, norm_mix_g, w_in, conv_dw_w, conv_dw_b, conv_ln_g, conv_ln_b,
              conv_pw_w, cmp_pos, cmp_w1, cmp_w2, w_out, norm_x_g, xq_w, xkv_w, xo_w,
              norm_ffn_g, ffn_in_w, ffn_out_w, rel_bias, final_norm_g):
    for l in range(DEPTH):
        h = rms_norm(x, norm_mix_g[l])
        x = x + hybrid_mixer(h, w_in[l], conv_dw_w[l], conv_dw_b[l], conv_ln_g[l], conv_ln_b[l],
                             conv_pw_w[l], cmp_pos[l], cmp_w1[l], cmp_w2[l], w_out[l], rel_bias)
        h = rms_norm(x, norm_x_g[l])
        x = x + cross_attention(h, mem, xq_w[l], xkv_w[l], xo_w[l])
        h = rms_norm(x, norm_ffn_g[l])
        x = x + swiglu(h, ffn_in_w[l], ffn_out_w[l])
    return rms_norm(x, final_norm_g)
```

```python
import numpy as np
import ml_dtypes
from contextlib import ExitStack
import concourse.bass as bass
import concourse.mybir as mybir
from concourse.bass_utils import run_bass_kernel_spmd

F32 = mybir.dt.float32
BF16 = mybir.dt.bfloat16
ALU = mybir.AluOpType
AF = mybir.ActivationFunctionType
AX = mybir.AxisListType

SEQ = 2048
D = 1024
NT = 16
DEPTH = 4
NIN = 6704
DFF = 2816
MEM = 256
O_A, O_GT, O_Q, O_KC, O_VC, O_KS, O_VS, O_KW, O_VW, O_GH, O_GC, O_GA = (
    0, 1024, 2048, 3072, 3328, 3584, 3840, 4096, 4352, 4608, 4656, 5680)
EPS = 1e-6
NEG = -30000.0


class View:
    __slots__ = ("bufs", "ap")

    def __init__(self, bufs, ap):
        self.bufs = bufs
        self.ap = ap

    def __getitem__(self, idx):
        return View(self.bufs, self.ap[idx])

    def map(self, f):
        return View(self.bufs, f(self.ap))


class Buf:
    __slots__ = ("name", "ap", "lastw", "readers")

    def __init__(self, name, ap):
        self.name = name
        self.ap = ap
        self.lastw = None
        self.readers = []

    def __getitem__(self, idx):
        return View((self,), self.ap[idx])

    @property
    def v(self):
        return View((self,), self.ap)


def multi(bufs, ap):
    return View(tuple(bufs), ap)


class _Eng:
    def __init__(self, name):
        self.name = name
        self.key = name
        self.epoch = 0
        self.cnt = 0
        self.clock = {}
        self.ops = []
        self.pending = False


class Sched:
    NDMA = 12

    def __init__(self, nc, es):
        self.nc = nc
        self.es = es
        self.eng = {n: _Eng(n) for n in ("pe", "act", "dve", "pool", "sp")}
        self.sems = {}
        for n in self.eng:
            self.sems[n] = es.enter_context(nc.semaphore("s_" + n))
        self.nslots = {"sp": 8, "pool": 6, "act": 0}
        for q, n in self.nslots.items():
            for i in range(n):
                self.sems["d%s%d" % (q, i)] = es.enter_context(nc.semaphore("s_d%s%d" % (q, i)))
        self.ndma_q = {"sp": 0, "pool": 0}
        self.ndma = 0
        self.dlast = {}
        self.tclock = {}
        self.nwaits = 0
        self.nops = 0

    def sbuf(self, name, shape, dt, es=None):
        self.nops += 1
        name = "%s_%d" % (name, self.nops)
        t = (es or self.es).enter_context(self.nc.sbuf_tensor(name, list(shape), dt))
        return Buf(name, t[:])

    def psum(self, name, shape, dt, es=None):
        t = (es or self.es).enter_context(self.nc.psum_tensor(name, list(shape), dt))
        return Buf(name, t[:])

    def _need(self, E, tok, waits):
        s, v = tok
        if E.name == "pe" and s == E.key:
            return
        if E.clock.get(s, 0) >= v:
            return
        if waits.get(s, 0) < v:
            waits[s] = v

    def _apply(self, E, waits):
        for s, v in waits.items():
            tc = self.tclock.get((s, v))
            assert tc is not None, ("wait on unfinished token", s, v, E.name)
            ck = E.clock
            for k2, v2 in tc.items():
                if ck.get(k2, 0) < v2:
                    ck[k2] = v2
        self.nwaits += len(waits)
        return list(waits.items())

    def _deps(self, E, rb, wb):
        waits = {}
        for b in rb:
            if b.lastw is not None:
                self._need(E, b.lastw, waits)
        for b in wb:
            if b.lastw is not None:
                self._need(E, b.lastw, waits)
            for r in b.readers:
                self._need(E, r, waits)
        return self._apply(E, waits)

    def _mark(self, tok, rb, wb):
        for b in rb:
            if tok not in b.readers[-2:]:
                b.readers.append(tok)
        for b in wb:
            b.lastw = tok
            b.readers = []

    def op(self, ename, fn, reads=(), writes=(), inc=True):
        E = self.eng[ename]
        rb = [b for v in reads for b in v.bufs]
        wb = [b for v in writes for b in v.bufs]
        waits = self._deps(E, rb, wb)
        tok = (E.key, E.cnt + 1)
        if inc:
            E.cnt += 1
            c = dict(E.clock)
            c[E.key] = E.cnt
            self.tclock[tok] = c
            E.pending = False
        else:
            E.pending = True
        self._mark(tok, rb, wb)
        self.nops += 1
        E.ops.append((waits, fn, E.key if inc else None, 1))

    def dma(self, qname, out, in_, **kw):
        E = self.eng[qname]
        rb = list(in_.bufs)
        wb = list(out.bufs)
        i = self.ndma_q[qname]
        self.ndma_q[qname] += 1
        self.ndma += 1
        ns = self.nslots[qname]
        s = "d%s%d" % (qname, i % ns)
        val = 16 * (i // ns + 1)
        waits = {}
        for b in rb:
            if b.lastw is not None:
                self._need(E, b.lastw, waits)
        for b in wb:
            if b.lastw is not None:
                self._need(E, b.lastw, waits)
            for r in b.readers:
                self._need(E, r, waits)
        if val > 16:
            self._need(E, (s, val - 16), waits)
        wl = self._apply(E, waits)
        tok = (s, val)
        c = dict(E.clock)
        c[s] = val
        self.tclock[tok] = c
        self.dlast[s] = val
        self._mark(tok, rb, wb)
        oap, iap = out.ap, in_.ap
        E.ops.append((wl, lambda e: e.dma_start(out=oap, in_=iap, **kw), s, 16))
        return tok

    def wait_toks(self, ename, toks):
        E = self.eng[ename]
        waits = {}
        for t in toks:
            self._need(E, t, waits)
        wl = self._apply(E, waits)
        E.ops.append((wl, None, None, 0))

    def barrier(self, skip_pool=False):
        toks = [(E.key, E.cnt) for E in self.eng.values() if E.cnt > 0]
        toks += list(self.dlast.items())
        for n in self.eng:
            if skip_pool and n == "pool":
                continue
            self.wait_toks(n, toks)
        for E in self.eng.values():
            if E.cnt > 1500:
                E.epoch += 1
                E.key = "%s#%d" % (E.name, E.epoch)
                E.cnt = 0
                self.sems[E.key] = self.es.enter_context(self.nc.semaphore("s_%s_%d" % (E.name, E.epoch)))

    def matmul(self, out, lhsT, rhs, start=True, stop=True, inc=None, **kw):
        o, l, r = out.ap, lhsT.ap, rhs.ap
        if inc is None:
            inc = stop
        rd = [lhsT, rhs] + ([] if start else [out])
        self.op("pe", lambda e: e.matmul(o, l, r, start=start, stop=stop, **kw), rd, [out], inc=inc)

    def transpose(self, out, in_, ident, inc=True):
        o, i, d = out.ap, in_.ap, ident.ap
        self.op("pe", lambda e: e.transpose(o, i, d), [in_, ident], [out], inc=inc)

    def act(self, out, in_, func, bias=None, scale=None, accum_out=None):
        kw = {}
        rd = [in_]
        wr = [out]
        if bias is not None:
            if isinstance(bias, View):
                kw["bias"] = bias.ap
                rd.append(bias)
            else:
                kw["bias"] = bias
        if scale is not None:
            if isinstance(scale, View):
                kw["scale"] = scale.ap
                rd.append(scale)
            else:
                kw["scale"] = scale
        if accum_out is not None:
            kw["accum_out"] = accum_out.ap
            wr.append(accum_out)
        o, i = out.ap, in_.ap
        self.op("act", lambda e: e.activation(o, i, func, **kw), rd, wr)

    def tt(self, eng, out, in0, in1, op):
        o, a, b = out.ap, in0.ap, in1.ap
        self.op(eng, lambda e: e.tensor_tensor(o, a, b, op), [in0, in1], [out])

    def ts(self, eng, out, in0, s1, op0, s2=None, op1=None):
        rd = [in0]
        a1, a2 = s1, s2
        if isinstance(s1, View):
            rd.append(s1)
            a1 = s1.ap
        if isinstance(s2, View):
            rd.append(s2)
            a2 = s2.ap
        kw = {}
        if op1 is not None:
            kw["op1"] = op1
        o, i = out.ap, in0.ap
        self.op(eng, lambda e: e.tensor_scalar(o, i, a1, a2, op0, **kw), rd, [out])

    def stt(self, eng, out, in0, scalar, in1, op0, op1):
        rd = [in0, in1]
        sc = scalar
        if isinstance(scalar, View):
            rd.append(scalar)
            sc = scalar.ap
        o, a, b = out.ap, in0.ap, in1.ap
        self.op(eng, lambda e: e.scalar_tensor_tensor(o, a, sc, b, op0, op1), rd, [out])

    def copy(self, eng, out, in_):
        o, i = out.ap, in_.ap
        if eng == "act":
            self.op(eng, lambda e: e.copy(o, i), [in_], [out])
        else:
            self.op(eng, lambda e: e.tensor_copy(o, i), [in_], [out])

    def memset(self, eng, out, val):
        o = out.ap
        self.op(eng, lambda e: e.memset(o, val), [], [out])

    def reduce(self, eng, out, in_, op, axis=AX.X):
        o, i = out.ap, in_.ap
        self.op(eng, lambda e: e.tensor_reduce(o, i, axis, op), [in_], [out])

    def recip(self, out, in_):
        o, i = out.ap, in_.ap
        self.op("dve", lambda e: e.reciprocal(o, i), [in_], [out])

    def emit(self):
        nc = self.nc
        sems = self.sems
        for E in self.eng.values():
            assert not E.pending, E.name
        with nc.Block() as block:
            def mk(E):
                def body(eng):
                    for waits, fn, inc, amt in E.ops:
                        for s, v in waits:
                            eng.wait_ge(sems[s], v)
                        if fn is not None:
                            ins = fn(eng)
                            if inc is not None:
                                ins.then_inc(sems[inc], amt)
                return body
            block.tensor(mk(self.eng["pe"]))
            block.scalar(mk(self.eng["act"]))
            block.vector(mk(self.eng["dve"]))
            block.gpsimd(mk(self.eng["pool"]))
            block.sync(mk(self.eng["sp"]))


def _t5_bucket(n):
    n = np.maximum(n, 0)
    nf = np.maximum(n, 1).astype(np.float32)
    large = 16 + (np.log(nf / np.float32(16)) / np.float32(np.log(8.0)) * 16).astype(np.int32)
    large = np.minimum(large, 31)
    return np.where(n < 16, n, large)


def host_consts():
    bf = ml_dtypes.bfloat16
    c = {}
    c["c_ident"] = np.eye(128, dtype=np.float32).astype(bf)
    c["c_antij"] = np.eye(128, dtype=np.float32)[::-1].copy().astype(bf)
    i = np.arange(768)
    d = i - 127
    ok = (d >= 0) & (d < 512)
    oh = np.zeros((32, 768), np.float32)
    oh[_t5_bucket(d)[ok], i[ok]] = 1.0
    c["c_ohw"] = oh
    c["c_validw"] = np.broadcast_to(ok.astype(np.float32)[None], (16, 768)).copy()
    c["c_maskw"] = ((c["c_validw"] - 1.0) * 30000.0).astype(np.float32)
    dist = np.arange(128)
    oc = np.zeros((32, 128), np.float32)
    oc[_t5_bucket(dist), dist] = 1.0
    c["c_ohc"] = oc
    n = np.arange(128)[:, None]
    m = np.arange(32)[None, :]
    ov = ((16 * n < 64 * m + 64) & (16 * n + 32 > 64 * m)).astype(np.float32)
    ov[127] = 0
    c["c_ov"] = ov.astype(bf)
    ex = (np.arange(2048)[None, :] // 64 == np.arange(32)[:, None]).astype(np.float32)
    c["c_expand"] = np.tile(ex, (4, 1)).astype(bf)
    cand = np.zeros((128, 8, 32), np.float32)
    fval = np.zeros((128, 8, 32), np.float32)
    for qb in range(8, 16):
        t = qb * 128 + np.arange(128)
        blk = (t // 64)[:, None]
        mm = np.arange(32)[None, :]
        forced = (mm == 0) | (mm == blk) | (mm == blk - 1)
        valid = mm <= blk
        cand[:, qb - 8] = (valid & ~forced)
        fval[:, qb - 8] = np.where(forced, 100.0 + mm, np.where(valid, 0.0, -1.0))
    c["c_cand"] = cand
    c["c_fval"] = fval
    return c


CONST_SPECS = [("c_ident", [128, 128], BF16), ("c_antij", [128, 128], BF16), ("c_ohw", [32, 768], F32),
               ("c_validw", [16, 768], F32), ("c_maskw", [16, 768], F32), ("c_ohc", [32, 128], F32), ("c_ov", [128, 32], BF16),
               ("c_expand", [128, 2048], BF16), ("c_cand", [128, 8, 32], F32), ("c_fval", [128, 8, 32], F32)]

IN_SPECS = [("x", [SEQ, D]), ("mem", [MEM, D]), ("norm_mix_g", [DEPTH, D]), ("w_in", [DEPTH, D, NIN]),
            ("conv_dw_w", [DEPTH, 31, D]), ("conv_dw_b", [DEPTH, D]), ("conv_ln_g", [DEPTH, D]),
            ("conv_ln_b", [DEPTH, D]), ("conv_pw_w", [DEPTH, D, D]), ("cmp_pos", [DEPTH, 2, 32, 64]),
            ("cmp_w1", [DEPTH, 2, 32, 64, 64]), ("cmp_w2", [DEPTH, 2, 64, 64]), ("w_out", [DEPTH, D, D]),
            ("norm_x_g", [DEPTH, D]), ("xq_w", [DEPTH, D, D]), ("xkv_w", [DEPTH, D, 2 * D]),
            ("xo_w", [DEPTH, D, D]), ("norm_ffn_g", [DEPTH, D]), ("ffn_in_w", [DEPTH, D, 2 * DFF]),
            ("ffn_out_w", [DEPTH, DFF, D]), ("rel_bias", [32, 16]), ("final_norm_g", [1, D])]


def build(n_layers=DEPTH, stages=("mix", "xattn", "ffn"), dumps=()):
    nc = bass.Bass("TRN2", target_bir_lowering=False)
    I = {}
    for name, shp in IN_SPECS:
        I[name] = Buf(name, nc.dram_tensor(name, shp, F32, kind="ExternalInput").ap())
    for name, shp, dt in CONST_SPECS:
        I[name] = Buf(name, nc.dram_tensor(name, shp, dt, kind="ExternalInput").ap())
    OUT = nc.dram_tensor("out", [SEQ, D], F32, kind="ExternalOutput").ap()
    out_t = [Buf("out%d" % i, OUT[i * 128:(i + 1) * 128, :]) for i in range(NT)]

    def dram(name, shape, dt):
        return nc.dram_tensor(name, list(shape), dt, kind="Internal").ap()

    xs_ap = dram("xs", [SEQ, D], F32)
    xs = [Buf("xs%d" % i, xs_ap[i * 128:(i + 1) * 128, :]) for i in range(NT)]
    xin_t = [Buf("xin%d" % i, I["x"].ap[i * 128:(i + 1) * 128, :]) for i in range(NT)]
    yc_ap = dram("yc", [SEQ, D], F32)
    YC = [Buf("yc%d" % i, yc_ap[i * 128:(i + 1) * 128, :]) for i in range(NT)]
    sga_ap = dram("sga", [SEQ, D], F32)
    SGA = [Buf("sga%d" % i, sga_ap[i * 128:(i + 1) * 128, :]) for i in range(NT)]
    conv_ap = dram("convs", [8, 128, SEQ], F32)
    CONV = [[Buf("cv%d_%d" % (c, t), conv_ap[c, :, t * 512:(t + 1) * 512]) for t in range(4)] for c in range(8)]
    fw_ap = dram("fw", [16, 768], BF16)
    FW = Buf("fw", fw_ap)
    fc_ap = dram("fc", [16, 4096], BF16)
    FC = Buf("fc", fc_ap)
    ec_ap = dram("ecd", [16, 128, 16 * 128], BF16)
    ECD = [Buf("ecd%d" % q, ec_ap[q]) for q in range(16)]

    dbg_toks = []
    dbg_outs = []

    with ExitStack() as es:
        S = Sched(nc, es)

        def dump(name, view, shape, dt=F32):
            if name not in dumps:
                return
            t = nc.dram_tensor("dbg_" + name, list(shape), dt, kind="ExternalOutput").ap()
            dbg_outs.append("dbg_" + name)
            dbg_toks.append(S.dma("sp", Buf("dbg_" + name, t).v, view))

        ident = S.sbuf("ident", [128, 128], BF16)
        antij = S.sbuf("antij", [128, 128], BF16)
        ones = S.sbuf("ones", [128, 128], BF16)
        expand = S.sbuf("expand", [128, 2048], BF16)
        ewin = S.sbuf("ewin", [128, 3, 16, 128], BF16)
        cand = S.sbuf("cand", [128, 8, 32], F32)
        fval = S.sbuf("fval", [128, 8, 32], F32)
        dwT = S.sbuf("dwT", [128, DEPTH, 8, 31], F32)
        dwb = S.sbuf("dwb", [128, DEPTH, 8], F32)
        lng = S.sbuf("lng", [128, DEPTH, 8], F32)
        lnb = S.sbuf("lnb", [128, DEPTH, 8], F32)
        memT = S.sbuf("memT", [128, 8, MEM], BF16)
        gbuf = [S.sbuf("gbuf0", [128, D], F32)] * 2
        xt = [S.sbuf("xt%d" % i, [128, D], F32) for i in range(2)]
        hb = S.sbuf("hb", [128, D], BF16)
        hb2 = [hb, S.sbuf("hbb", [128, D], BF16)]
        junk = S.sbuf("junk", [128, D], BF16)
        ss = [S.sbuf("ss%d" % i, [128, 1], F32) for i in range(2)]
        rstd = [S.sbuf("rstd%d" % i, [128, 1], F32) for i in range(2)]
        ssn_t = S.sbuf("ssn", [128, 2, NT], F32)
        ssnb = [[Buf("ssn%d_%d" % (h_, i_), ssn_t.ap[:, h_, i_:i_ + 1]) for i_ in range(NT)] for h_ in range(2)]
        ssn_all = multi([b_ for r_ in ssnb for b_ in r_], ssn_t.ap)
        ssn_hi = multi(ssnb[1], ssn_t.ap[:, 1, :])
        rsn = S.sbuf("rsn", [128, NT], F32)
        NWB = 4
        wb = [S.sbuf("wb%d" % i, [128, 8, 512], BF16) for i in range(NWB)]
        banks = [S.psum("bk%d" % i, [128, 512], F32) for i in range(6)]
        tbank = [S.psum("tb%d" % i, [128, 8, 128], BF16) for i in range(2)]
        st = {"wb": 0, "bk": 0, "ev": 0, "g": 0, "xt": 0}

        st["nwb"] = NWB

        def next_wb():
            b = wb[st["wb"] % st["nwb"]]
            st["wb"] += 1
            return b

        def next_bank():
            b = banks[st["bk"] % 6]
            st["bk"] += 1
            return b

        def ev_eng():
            st["ev"] += 1
            return "act" if st["ev"] % 2 else "dve"

        def wload(dst, W, l, r0, nr, c0, ncols):
            src = W.ap[l, r0:r0 + nr, c0:c0 + ncols].rearrange("(c p) n -> p c n", p=128)
            S.dma("pool", dst, View((W,), src))

        def bcast_row(dst, src_buf, row_ap):
            S.dma("sp", dst, View((src_buf,), row_ap.partition_broadcast(128)))

        S.dma("sp", ident.v, I["c_ident"].v)
        S.dma("sp", antij.v, I["c_antij"].v)
        S.dma("sp", expand.v, I["c_expand"].v)
        S.dma("sp", cand.v, I["c_cand"].v)
        S.dma("sp", fval.v, I["c_fval"].v)
        S.memset("dve", ones.v, 1.0)
        for l_ in range(DEPTH):
            for c_ in range(8):
                S.dma("sp", dwT[:, l_, c_, :], I["conv_dw_w"].v.map(lambda a: a[l_, :, c_ * 128:(c_ + 1) * 128].rearrange("k p -> p k")),
                      allow_slow_non_contiguous=True)
            for dst, nm in ((dwb, "conv_dw_b"), (lng, "conv_ln_g"), (lnb, "conv_ln_b")):
                S.dma("sp", dst[:, l_, :], I[nm].v.map(lambda a: a[l_].rearrange("(c p) -> p c", p=128)),
                      allow_slow_non_contiguous=True)

        with ExitStack() as es0:
            rbp = S.sbuf("rbp", [32, 16], F32, es0)
            ohw = S.sbuf("ohw", [32, 768], F32, es0)
            ohc = S.sbuf("ohc", [32, 128], F32, es0)
            validw = S.sbuf("validw", [16, 768], F32, es0)
            maskw = S.sbuf("maskw", [16, 768], F32, es0)
            neg31 = S.sbuf("neg31", [16, 1], F32, es0)
            fwf = S.sbuf("fwf", [16, 768], F32, es0)
            fwb = S.sbuf("fwb", [16, 768], BF16, es0)
            fcb = S.sbuf("fcb", [16, 4096], BF16, es0)
            tmph = S.sbuf("tmph", [128, 16, 128], BF16, es0)
            tmpe = [S.sbuf("tmpe%d" % i, [128, 16, 128], BF16, es0) for i in range(2)]
            memf = S.sbuf("memf", [128, D], F32, es0)
            S.dma("sp", rbp.v, I["rel_bias"].v)
            S.dma("sp", ohw.v, I["c_ohw"].v)
            S.dma("sp", ohc.v, I["c_ohc"].v)
            S.dma("sp", validw.v, I["c_validw"].v)
            S.dma("sp", maskw.v, I["c_maskw"].v)
            S.dma("sp", neg31.v, I["rel_bias"].v.map(lambda a: a[31:32, :].rearrange("o h -> h o")),
                  allow_slow_non_contiguous=True)
            S.ts("dve", neg31.v, neg31.v, -1.0, ALU.mult)
            b0, b1, b2 = banks[0], banks[1], banks[2]
            S.matmul(b0[0:16, 0:384], rbp.v, ohw[:, 0:384])
            S.matmul(b1[0:16, 0:384], rbp.v, ohw[:, 384:768])
            S.matmul(b2[0:16, 0:128], rbp.v, ohc.v)
            S.act(fwf[:, 0:384], b0[0:16, 0:384], AF.Identity, bias=neg31.v)
            S.act(fwf[:, 384:768], b1[0:16, 0:384], AF.Identity, bias=neg31.v)
            S.tt("dve", fwf.v, fwf.v, validw.v, ALU.mult)
            S.tt("dve", fwb.v, fwf.v, maskw.v, ALU.add)
            S.memset("dve", fcb[:, 0:2063], NEG)
            S.memset("dve", fcb[:, 2063 + 128:4096], 0.0)
            S.act(fcb[:, 2063:2063 + 128], b2[0:16, 0:128], AF.Identity, bias=neg31.v)
            S.dma("sp", FW.v, fwb.v)
            S.dma("sp", FC.v, fcb.v)
            dump("fw", fwf.v, [16, 768])
            for ti, delta in enumerate((0, 128, 512)):
                src = bass.AP(tensor=fw_ap.tensor, offset=delta, ap=[[1, 128], [768, 16], [1, 128]])
                S.dma("sp", tmph.v, View((FW,), src))
                for g in range(4):
                    bk = next_bank()
                    S.matmul(bk.v, antij.v, tmph[:, 4 * g:4 * g + 4, :])
                    S.copy(ev_eng(), ewin[:, ti, 4 * g:4 * g + 4, :],
                           bk.v.map(lambda a: a.rearrange("p (r q) -> p r q", r=4)))
            for qb in range(16):
                src = bass.AP(tensor=fc_ap.tensor, offset=128 * qb, ap=[[16, 128], [4096, 16], [1, 128]])
                S.dma("sp", tmph.v, View((FC,), src))
                te = tmpe[qb % 2]
                for g in range(4):
                    bk = next_bank()
                    S.matmul(bk.v, antij.v, tmph[:, 4 * g:4 * g + 4, :])
                    S.copy(ev_eng(), te[:, 4 * g:4 * g + 4, :],
                           bk.v.map(lambda a: a.rearrange("p (r q) -> p r q", r=4)))
                S.dma("sp", ECD[qb].v, te.v.map(lambda a: a.rearrange("p h q -> p (h q)")))
            for mt in range(2):
                S.dma("sp", memf.v, I["mem"][mt * 128:(mt + 1) * 128, :])
                S.copy("dve", hb.v, memf.v)
                tb = tbank[mt % 2]
                for c in range(8):
                    S.transpose(tb[:, c, :], hb[:, c * 128:(c + 1) * 128], ident.v, inc=(c == 7))
                S.copy("act", memT[:, :, mt * 128:(mt + 1) * 128], tb.v)
            S.barrier(skip_pool=True)
        dump("ewin", ewin.v, [128, 3, 16, 128], BF16)

        def batch_rstd():
            S.tt("dve", rsn.v, ssn_all.map(lambda a: a[:, 0, :]), ssn_all.map(lambda a: a[:, 1, :]), ALU.add)
            S.ts("dve", rsn.v, rsn.v, 1.0 / D, ALU.mult, EPS, ALU.add)
            S.act(rsn.v, rsn.v, AF.Sqrt)
            S.recip(rsn.v, rsn.v)

        def norm_T(hT, src_tiles, gname, l, pre=False):
            gb = gbuf[st["g"] % 2]
            st["g"] += 1
            if l is None:
                bcast_row(gb.v, I[gname], I[gname].ap[0:1, :])
            else:
                bcast_row(gb.v, I[gname], I[gname].ap[l:l + 1, :])
            if pre:
                batch_rstd()
                for i in range(NT):
                    x = xt[st["xt"] % 2]
                    st["xt"] += 1
                    S.dma("sp", x.v, src_tiles[i].v)
                    hbi = hb2[i % 2]
                    S.stt("dve", hbi.v, x.v, rsn[:, i:i + 1], gb.v, ALU.mult, ALU.mult)
                    tb = tbank[i % 2]
                    for c in range(8):
                        S.transpose(tb[:, c, :], hbi[:, c * 128:(c + 1) * 128], ident.v, inc=(c == 7))
                    S.copy(ev_eng(), hT[:, :, i * 128:(i + 1) * 128], tb.v)
                return
            for i in range(NT):
                x = xt[st["xt"] % 2]
                sq = ss[st["xt"] % 2]
                rs = rstd[st["xt"] % 2]
                st["xt"] += 1
                S.dma("sp", x.v, src_tiles[i].v)
                S.act(junk.v, x.v, AF.Square, accum_out=sq.v)
                S.ts("dve", rs.v, sq.v, 1.0 / D, ALU.mult, EPS, ALU.add)
                S.act(rs.v, rs.v, AF.Sqrt)
                S.recip(rs.v, rs.v)
                S.stt("dve", hb.v, x.v, rs.v, gb.v, ALU.mult, ALU.mult)
                tb = tbank[i % 2]
                for c in range(8):
                    S.transpose(tb[:, c, :], hb[:, c * 128:(c + 1) * 128], ident.v, inc=(c == 7))
                S.copy(ev_eng(), hT[:, :, i * 128:(i + 1) * 128], tb.v)

        def fm_proj(hT, lhs_of_k, evac, nk=8):
            for tb in range(4):
                bk = next_bank()
                for k in range(nk):
                    S.matmul(bk.v, lhs_of_k(k), hT[:, k, tb * 512:(tb + 1) * 512], start=(k == 0), stop=(k == nk - 1))
                evac(tb, bk)

        for l in range(n_layers):
            src_tiles = xin_t if l == 0 else xs
            if "mix" in stages:
                with ExitStack() as esm:
                  if True:
                    esh = esm
                    arena = S.sbuf("arena", [128, 8 * SEQ], BF16, esh)
                    hT = Buf("hT%d" % l, arena.ap.rearrange("p (c t) -> p c t", c=8))
                    norm_T(hT, src_tiles, "norm_mix_g", l, pre=(l > 0))
                    if l == 0:
                        dump("hT", hT.v, [128, 8, SEQ], BF16)
                    with ExitStack() as esa:
                        vTc = [S.sbuf("vTc%d" % i, [128, 30 + SEQ], BF16, esa) for i in range(2)]
                        diag = [S.sbuf("diag%d" % i, [128, 31, 128], BF16, esa) for i in range(2)]
                        sgs = [S.sbuf("sgs%d" % i, [128, 512], F32, esa) for i in range(2)]
                        cvo = [S.sbuf("cvo%d" % i, [128, 512], F32, esa) for i in range(2)]
                        for i in range(2):
                            S.memset("dve", vTc[i][:, 0:30], 0.0)
                        for cc in range(8):
                            w = next_wb()
                            wload(w[:, :, 0:128], I["w_in"], l, 0, D, O_A + cc * 128, 128)
                            wload(w[:, :, 128:256], I["w_in"], l, 0, D, O_GT + cc * 128, 128)
                            vt = vTc[cc % 2]
                            dg = diag[cc % 2]
                            S.tt("dve", dg.v,
                                 ident.v.map(lambda a: a.unsqueeze(1).to_broadcast([128, 31, 128])),
                                 dwT[:, l, cc, :].map(lambda a: a.unsqueeze(2).to_broadcast([128, 31, 128])),
                                 ALU.mult)
                            for tb in range(4):
                                ba = next_bank()
                                bg = next_bank()
                                for k in range(8):
                                    S.matmul(ba.v, w[:, k, 0:128], hT[:, k, tb * 512:(tb + 1) * 512], start=(k == 0), stop=(k == 7))
                                for k in range(8):
                                    S.matmul(bg.v, w[:, k, 128:256], hT[:, k, tb * 512:(tb + 1) * 512], start=(k == 0), stop=(k == 7))
                                sg = sgs[tb % 2]
                                S.act(sg.v, bg.v, AF.Sigmoid)
                                S.tt("dve", vt[:, 30 + tb * 512:30 + (tb + 1) * 512], ba.v, sg.v, ALU.mult)
                            for tb in range(4):
                                bk = next_bank()
                                for k in range(31):
                                    S.matmul(bk.v, dg[:, k, :], vt[:, tb * 512 + k:tb * 512 + k + 512], start=(k == 0), stop=(k == 30))
                                co = cvo[tb % 2]
                                S.act(co.v, bk.v, AF.Identity, bias=dwb[:, l, cc:cc + 1])
                                S.dma("sp", CONV[cc][tb].v, co.v)
                        S.barrier(skip_pool=True)
                    with ExitStack() as esb:
                        cvall2 = [S.sbuf("cvall%d" % i, [128, 8, 512], F32, esb) for i in range(2)]
                        cbf2 = [S.sbuf("cbf%d" % i, [128, 8, 512], BF16, esb) for i in range(2)]
                        sqb2 = [S.sbuf("sqb%d" % i, [128, 8, 512], BF16, esb) for i in range(2)]
                        zT2 = [S.sbuf("zT%d" % i, [128, 8, 512], BF16, esb) for i in range(2)]
                        mean = S.sbuf("mean", [128, 512], F32, esb)
                        msq = S.sbuf("msq", [128, 512], F32, esb)
                        rsd = S.sbuf("rsd", [128, 512], F32, esb)
                        sgc = [S.sbuf("sgc%d" % i, [128, 512], F32, esb) for i in range(2)]
                        yco = [S.sbuf("yco%d" % i, [128, 512], F32, esb) for i in range(2)]
                        wpw = [next_wb(), next_wb()]
                        wgc = [next_wb(), next_wb()]
                        for nb in range(2):
                            wload(wpw[nb].v, I["conv_pw_w"], l, 0, D, nb * 512, 512)
                            wload(wgc[nb].v, I["w_in"], l, 0, D, O_GC + nb * 512, 512)

                        def prep_a(tb):
                            cvall, cbf, sqb = cvall2[tb % 2], cbf2[tb % 2], sqb2[tb % 2]
                            bufs = [CONV[c][tb] for c in range(8)]
                            S.dma("sp", cvall.v, multi(bufs, conv_ap[:, :, tb * 512:(tb + 1) * 512].rearrange("c p t -> p c t")))
                            S.copy("dve", cbf.v, cvall.v)
                            S.act(sqb.v, cvall.v, AF.Square)

                        def prep_b(tb):
                            cvall, cbf, sqb, zT = cvall2[tb % 2], cbf2[tb % 2], sqb2[tb % 2], zT2[tb % 2]
                            b1_ = next_bank()
                            b2_ = next_bank()
                            for c in range(8):
                                S.matmul(b1_.v, ones.v, cbf[:, c, :], start=(c == 0), stop=(c == 7))
                            for c in range(8):
                                S.matmul(b2_.v, ones.v, sqb[:, c, :], start=(c == 0), stop=(c == 7))
                            S.act(mean.v, b1_.v, AF.Copy, scale=1.0 / D)
                            S.tt("dve", msq.v, mean.v, mean.v, ALU.mult)
                            S.stt("dve", rsd.v, b2_.v, 1.0 / D, msq.v, ALU.mult, ALU.subtract)
                            S.ts("dve", rsd.v, rsd.v, EPS, ALU.add)
                            S.act(rsd.v, rsd.v, AF.Sqrt)
                            S.recip(rsd.v, rsd.v)
                            S.tt("dve", cvall.v, cvall.v, mean.v.map(lambda a: a.unsqueeze(1).to_broadcast([128, 8, 512])), ALU.subtract)
                            S.tt("dve", cvall.v, cvall.v, rsd.v.map(lambda a: a.unsqueeze(1).to_broadcast([128, 8, 512])), ALU.mult)
                            for c in range(8):
                                S.act(zT[:, c, :], cvall[:, c, :], AF.Silu, scale=lng[:, l, c:c + 1], bias=lnb[:, l, c:c + 1])

                        def mm_tile(tb, j):
                            zT = zT2[tb % 2]
                            i = tb * 4 + j
                            for nb in range(2):
                                bc = next_bank()
                                bg = next_bank()
                                for c in range(8):
                                    S.matmul(bc.v, zT[:, c, j * 128:(j + 1) * 128], wpw[nb][:, c, :], start=(c == 0), stop=(c == 7))
                                for k in range(8):
                                    S.matmul(bg.v, hT[:, k, i * 128:(i + 1) * 128], wgc[nb][:, k, :], start=(k == 0), stop=(k == 7))
                                sg = sgc[nb]
                                yo = yco[nb]
                                S.act(sg.v, bg.v, AF.Sigmoid)
                                S.tt("dve", yo.v, bc.v, sg.v, ALU.mult)
                                S.dma("sp", YC[i][:, nb * 512:(nb + 1) * 512], yo.v)

                        prep_a(0)
                        prep_b(0)
                        for tb in range(4):
                            if tb + 1 < 4:
                                prep_a(tb + 1)
                            mm_tile(tb, 0)
                            if tb + 1 < 4:
                                prep_b(tb + 1)
                            for j in range(1, 4):
                                mm_tile(tb, j)
                        S.barrier(skip_pool=True)
                    if l == 0:
                        dump("yc", multi(YC, yc_ap), [SEQ, D])
                    esn = esm
                    qT = S.sbuf("qT", [128, 8, SEQ], BF16, esn)
                    kwT = S.sbuf("kwT", [128, 4, SEQ], BF16, esn)
                    ksT = S.sbuf("ksT", [128, 4, SEQ], BF16, esn)
                    S.memset("dve", kwT.v, 0.0)
                    S.memset("dve", ksT.v, 0.0)
                    vse = S.sbuf("vse", [128, NT, 4, 65], BF16, esn)
                    vwe = S.sbuf("vwe", [128, NT, 4, 65], BF16, esn)
                    gsig = S.sbuf("gsig", [128, NT, 48], F32, esn)
                    kcmpT = S.sbuf("kcmpT", [128, 4, 128], BF16, esn)
                    vce = S.sbuf("vce", [128, 4, 97], BF16, esn)
                    with ExitStack() as esp:
                        st["nwb"] = 2
                        kcT = Buf("kcT%d" % l, wb[2].ap.rearrange("p c t -> p (c t)").rearrange("p (a t) -> p a t", a=2))
                        vcT = Buf("vcT%d" % l, wb[3].ap.rearrange("p c t -> p (c t)").rearrange("p (a t) -> p a t", a=2))
                        for gp in range(2):
                            w = next_wb()
                            for e_ in range(2):
                                for r_ in range(4):
                                    wload(w[:, :, r_ * 128 + e_ * 64:r_ * 128 + e_ * 64 + 64], I["w_in"], l, 0, D,
                                          O_Q + gp * 512 + e_ * 256 + r_ * 64, 64)
                            for r in range(4):
                                cidx = gp * 4 + r

                                def lhs(k, w=w, r=r):
                                    return w[:, k, r * 128:(r + 1) * 128]

                                def evq(tb, bk, cidx=cidx):
                                    if ev_eng() == "act":
                                        S.act(qT[:, cidx, tb * 512:(tb + 1) * 512], bk.v, AF.Copy, scale=0.125)
                                    else:
                                        S.ts("dve", qT[:, cidx, tb * 512:(tb + 1) * 512], bk.v, 0.125, ALU.mult)
                                fm_proj(hT, lhs, evq)
                        w = next_wb()
                        wload(w.v, I["w_in"], l, 0, D, O_KC, 512)
                        w2_ = next_wb()
                        wload(w2_[:, :, 0:256], I["w_in"], l, 0, D, O_KS, 256)
                        wload(w2_[:, :, 256:512], I["w_in"], l, 0, D, O_KW, 256)
                        for (wt, c0, dst) in ((w, 0, kcT), (w, 256, vcT), (w2_, 0, ksT), (w2_, 256, kwT)):
                            for gp in range(2):
                                def lhs(k, wt=wt, c0=c0, gp=gp):
                                    return wt[:, k, c0 + gp * 128:c0 + (gp + 1) * 128]

                                def evk(tb, bk, dst=dst, gp=gp):
                                    if dst is kwT or dst is ksT:
                                        en_ = ev_eng()
                                        S.copy(en_, dst[0:64, 2 * gp, tb * 512:(tb + 1) * 512], bk[0:64, :])
                                        S.copy(en_, dst[64:128, 2 * gp + 1, tb * 512:(tb + 1) * 512], bk[64:128, :])
                                    else:
                                        S.copy(ev_eng(), dst[:, gp, tb * 512:(tb + 1) * 512], bk.v)
                                fm_proj(hT, lhs, evk)
                        wtm = [next_wb(), next_wb()]
                        wload(wtm[0][:, :, 0:256], I["w_in"], l, 0, D, O_VS, 256)
                        wload(wtm[1][:, :, 0:304], I["w_in"], l, 0, D, O_VW, 304)
                        S.memset("dve", vse[:, :, :, 64:65], 1.0)
                        S.memset("dve", vwe[:, :, :, 64:65], 1.0)
                        for i in range(NT):
                            ba = next_bank()
                            bb = next_bank()
                            for k in range(8):
                                S.matmul(ba[:, 0:256], hT[:, k, i * 128:(i + 1) * 128], wtm[0][:, k, 0:256], start=(k == 0), stop=(k == 7))
                            for k in range(8):
                                S.matmul(bb[:, 0:304], hT[:, k, i * 128:(i + 1) * 128], wtm[1][:, k, 0:304], start=(k == 0), stop=(k == 7))
                            S.copy("dve", vse[:, i, :, 0:64], ba[:, 0:256].map(lambda a: a.rearrange("p (g d) -> p g d", g=4)))
                            S.copy("act", vwe[:, i, :, 0:64], bb[:, 0:256].map(lambda a: a.rearrange("p (g d) -> p g d", g=4)))
                            S.act(gsig[:, i, :], bb[:, 256:304], AF.Sigmoid)
                        with ExitStack() as esg:
                            sgt = [S.sbuf("sgt%d" % i, [128, 512], F32, esg) for i in range(2)]
                            for nb in range(2):
                                w = next_wb()
                                wload(w.v, I["w_in"], l, 0, D, O_GA + nb * 512, 512)
                                for i in range(NT):
                                    bk = next_bank()
                                    for k in range(8):
                                        S.matmul(bk.v, hT[:, k, i * 128:(i + 1) * 128], w[:, k, :], start=(k == 0), stop=(k == 7))
                                    sg = sgt[i % 2]
                                    S.act(sg.v, bk.v, AF.Sigmoid)
                                    S.dma("sp", SGA[i][:, nb * 512:(nb + 1) * 512], sg.v)
                            S.barrier()
                        with ExitStack() as esc:
                            w1d = S.sbuf("w1d", [128, 2, 32, 64], BF16, esc)
                            w2d = S.sbuf("w2d", [64, 2, 128], BF16, esc)
                            posd = S.sbuf("posd", [128, 2, 32], BF16, esc)
                            cbias = S.sbuf("cbias", [64, 2], F32, esc)
                            sz = S.sbuf("sz", [64, 128], BF16, esc)
                            for hf in range(2):
                                for j_ in range(2):
                                    S.dma("pool", w1d[hf * 64:(hf + 1) * 64, j_], I["cmp_w1"].v.map(lambda a: a[l, j_].rearrange("t d e -> d t e")))
                                S.dma("pool", w2d[:, :, hf * 64:(hf + 1) * 64], I["cmp_w2"].v.map(lambda a: a[l].rearrange("j e f -> e j f")))
                            for j_ in range(2):
                                for hf in range(2):
                                    S.dma("pool", posd[hf * 64:(hf + 1) * 64, j_, :], I["cmp_pos"].v.map(lambda a: a[l, j_].rearrange("t d -> d t")), allow_slow_non_contiguous=True)
                            S.memset("dve", kcmpT.v, 0.0)
                            S.memset("dve", vce.v, 0.0)
                            S.memset("dve", vce[:, :, 64:65], 1.0)
                            S.memset("dve", sz.v, 0.0)
                            for g in range(4):
                                S.dma("sp", vce[:, g, 65:97], I["c_ov"].v)
                            dump("w2d", w2d.v, [64, 2, 128], BF16)
                            dump("w1d", w1d.v, [128, 2, 32, 64], BF16)
                            dump("kcT", kcT.v, [128, 2, SEQ], BF16)
                            for j, srcT in ((0, kcT), (1, vcT)):
                                for g in range(4):
                                    gp, e = divmod(g, 2)
                                    hs = slice(e * 64, e * 64 + 64)
                                    bk = next_bank()
                                    for t in range(32):
                                        S.matmul(bk[0:64, 0:127], w1d[hs, j, t, :], srcT[hs, gp, t:t + 16 * 126 + 1:16], start=(t == 0), stop=False)
                                        S.matmul(bk[0:64, 0:127], w1d[hs, j, t, :], posd[hs, j, t:t + 1].map(lambda a: a.to_broadcast([64, 127])), start=False, stop=(t == 31))
                                    S.act(sz[:, 0:127], bk[0:64, 0:127], AF.Silu)
                                    bo = next_bank()
                                    if j == 0:
                                        if g == 0:
                                            dump("sz0", sz.v, [64, 128], BF16)
                                        S.matmul(bo[:, 0:127], w2d[:, 0, :], sz[:, 0:127])
                                        S.copy("dve", kcmpT[hs, g, 0:127], bo[hs, 0:127])
                                    else:
                                        S.matmul(bo[0:127, 0:64], sz[:, 0:127], w2d[:, 1, 0:64])
                                        S.copy("dve", vce[0:127, g, 0:64], bo[0:127, 0:64])
                            S.barrier()
                        S.barrier()
                        st["nwb"] = NWB
                    if l == 0:
                        dump("qT", qT.v, [128, 8, SEQ], BF16)
                        dump("kwT", kwT.v, [128, 4, SEQ], BF16)
                        dump("vwe", vwe.v, [128, NT, 4, 65], BF16)
                        dump("gsig", gsig.v, [128, NT, 48])
                        dump("kcmpT", kcmpT.v, [128, 4, 128], BF16)
                        dump("vce", vce.v, [128, 4, 97], BF16)
                    S.barrier()

                  with ExitStack() as esw:
                    ar = {"off": 0}

                    def take(name, shape, dt):
                        n = 1
                        for d_ in shape[1:]:
                            n *= d_
                        nb = n * (4 if dt == F32 else 2)
                        nb = (nb + 31) // 32 * 32
                        o0 = ar["off"]
                        ar["off"] += nb // 2
                        assert ar["off"] <= 8 * SEQ
                        ap = arena.ap[:, o0:o0 + nb // 2]
                        if dt == F32:
                            ap = ap.bitcast(F32)[:, 0:n]
                        else:
                            ap = ap[:, 0:n]
                        if len(shape) == 3:
                            ap = ap.rearrange("p (a b) -> p a b", a=shape[1])
                        elif len(shape) == 4:
                            ap = ap.rearrange("p (a b c) -> p a b c", a=shape[1], b=shape[2])
                        return Buf("%s_%d" % (name, l), ap)

                    PT = [take("PT%d" % i, [128, 512], BF16) for i in range(3)]
                    ect = [take("ect0", [128, 16 * 128], BF16)] * 2
                    nsa = take("nsa", [128, 16, 64], F32)
                    tmpo = take("tmpo", [128, 4, 64], F32)
                    rl = take("rl", [128, 4], F32)
                    ccf = take("ccf", [128, 4], F32)
                    impn = take("impn", [128, 4, 32], F32)
                    imp = take("imp", [128, 32], F32)
                    score = take("score", [128, 32], F32)
                    work = take("work", [128, 32], F32)
                    m8a = take("m8a", [128, 8], F32)
                    m8b = take("m8b", [128, 8], F32)
                    selb4 = take("selb4", [128, 4, 128], BF16)
                    selbT = take("selbT", [128, 4, 128], BF16)
                    sgat = take("sgat", [128, D], F32)
                    yct = take("yct", [128, D], F32)
                    S.memset("dve", selb4.v, 0.0)
                    yb = hb
                    yT = take("yT", [128, 8, 128], BF16)
                    xo = yct
                    wout = [next_wb(), next_wb()]
                    for nb in range(2):
                        wload(wout[nb].v, I["w_out"], l, 0, D, nb * 512, 512)
                    SB = [banks[0], banks[1], banks[5]]
                    osel, owin, ocmp = banks[2], banks[3], banks[4]
                    LA = 2

                    def v4(view):
                        return view.map(lambda a: a.rearrange("p (r q) -> p r q", r=4))

                    nsa2 = [nsa, S.sbuf("nsa2", [128, 16, 64], F32, esw)]
                    ect = [ect[0], S.sbuf("ect1", [128, 16 * 128], BF16, esw)]
                    selbT2 = [selbT, S.sbuf("selbT2", [128, 4, 128], BF16, esw)]
                    rlc = S.sbuf("rlc", [128, 4], F32, esw)
                    ccc = S.sbuf("ccc", [128, 4], F32, esw)
                    pending = {}
                    cur = {"j": 0}

                    def defer(fn, k):
                        pending.setdefault(cur["j"] + k, []).append(fn)

                    def mk_cmp(qb, g):
                        gp = g // 2
                        ec = ect[qb % 2]
                        qv = qT[:, gp * 4:gp * 4 + 4, qb * 128:(qb + 1) * 128]
                        nsab = nsa2[qb % 2]
                        sbT = selbT2[qb % 2]

                        def qk_c(sb):
                            if g == 0:
                                S.dma("sp", ec.v, ECD[qb].v)
                            S.matmul(sb.v, kcmpT[:, g, :], qv, start=True, stop=False)
                            S.matmul(sb.v, ident.v, ec[:, g * 512:(g + 1) * 512], start=False, stop=True)

                        def ex_c(sb, pt):
                            S.act(pt.v, sb.v, AF.Exp)

                        def pv_c(pt):
                            for r in range(4):
                                S.matmul(ocmp[:, r * 97:(r + 1) * 97], pt[:, r * 128:(r + 1) * 128], vce[:, g, :],
                                         start=True, stop=True, inc=(r == 3))

                        def post_c():
                            oc3 = ocmp.v.map(lambda a: a[:, 0:388].rearrange("p (r c) -> p r c", c=97))
                            S.ts("dve", rlc.v, oc3[:, :, 64], 1e-30, ALU.max)
                            S.recip(rlc.v, rlc.v)
                            S.tt("dve", ccc.v, gsig[:, qb, 4 * g:4 * g + 4], rlc.v, ALU.mult)
                            S.tt("dve", nsab[:, 4 * g:4 * g + 4, :], oc3[:, :, 0:64],
                                 ccc.v.map(lambda a: a.unsqueeze(2).to_broadcast([128, 4, 64])), ALU.mult)
                            if qb >= 8:
                                S.tt("dve", impn.v, oc3[:, :, 65:97], rlc.v.map(lambda a: a.unsqueeze(2).to_broadcast([128, 4, 32])), ALU.mult)
                                S.reduce("dve", imp.v, impn.v.map(lambda a: a.rearrange("p r m -> p m r")), ALU.add)
                                S.tt("dve", score.v, imp.v, cand[:, qb - 8, :], ALU.mult)
                                S.tt("dve", score.v, score.v, fval[:, qb - 8, :], ALU.add)
                                S.op("dve", lambda en: en.max(m8a.ap, score.ap), [score.v], [m8a.v])
                                S.op("dve", lambda en: en.match_replace(work.ap, m8a.ap, score.ap, -2.0), [m8a.v, score.v], [work.v])
                                S.op("dve", lambda en: en.max(m8b.ap, work.ap), [work.v], [m8b.v])
                                S.ts("dve", selb4[:, g, 32 * g:32 * g + 32], score.v, m8b[:, 7:8], ALU.is_lt, NEG, ALU.mult)

                                def pe_part():
                                    tbk = tbank[0]
                                    S.transpose(tbk[:, 0, :], selb4[:, g, :], ident.v)
                                    S.copy("dve", sbT[:, g, :], tbk[:, 0, :])
                                defer(pe_part, 3)

                        return (qk_c, ex_c, pv_c, post_c)

                    def mk_branch(qb, g, kts, kT, ve, obank, tabs, goff, mask):
                        gp = g // 2
                        qv = qT[:, gp * 4:gp * 4 + 4, qb * 128:(qb + 1) * 128]
                        nsab = nsa2[qb % 2]
                        sbT = selbT2[qb % 2]
                        out = []
                        for idx, kt in enumerate(kts):
                            last = idx == len(kts) - 1

                            def qk(sb, kt=kt):
                                tab = (qb - kt) in tabs
                                S.matmul(sb.v, kT[:, g, kt * 128:(kt + 1) * 128], qv, start=True, stop=(not mask and not tab))
                                if mask:
                                    S.matmul(sb.v, expand[:, kt * 128:(kt + 1) * 128],
                                             sbT[:, g, :].map(lambda a: a.unsqueeze(1).to_broadcast([128, 4, 128])), start=False, stop=(not tab))
                                if tab:
                                    S.matmul(sb.v, ident.v, ewin[:, tabs[qb - kt], 4 * g:4 * g + 4, :], start=False, stop=True)

                            def ex(sb, pt, kt=kt):
                                S.act(pt.v, sb.v, AF.Exp)

                            def pv(pt, kt=kt, idx=idx, last=last):
                                for r in range(4):
                                    S.matmul(obank[:, r * 65:(r + 1) * 65], pt[:, r * 128:(r + 1) * 128], ve[:, kt, g, :],
                                             start=(idx == 0 and r == 0), stop=last, inc=(r == 3), skip_group_check=True)

                            def post():
                                o3 = obank.v.map(lambda a: a[:, 0:260].rearrange("p (r c) -> p r c", c=65))
                                S.recip(rl.v, o3[:, :, 64])
                                S.tt("dve", ccf.v, gsig[:, qb, goff + 4 * g:goff + 4 * g + 4], rl.v, ALU.mult)
                                S.tt("dve", tmpo.v, o3[:, :, 0:64], ccf.v.map(lambda a: a.unsqueeze(2).to_broadcast([128, 4, 64])), ALU.mult)
                                S.tt("dve", nsab[:, 4 * g:4 * g + 4, :], nsab[:, 4 * g:4 * g + 4, :], tmpo.v, ALU.add)

                            out.append((qk, ex, pv, post if last else None))
                        return out

                    def mk_asm(qb):
                        nsab = nsa2[qb % 2]

                        def assemble():
                            if l == 0:
                                dump("nsa%d" % qb, nsab.v, [128, 16, 64])
                            x = xt[qb % 2]
                            S.tt("dve", sgat.v, sgat.v, nsab.v.map(lambda a: a.rearrange("p h d -> p (h d)")), ALU.mult)
                            S.tt("dve", yb.v, sgat.v, yct.v, ALU.add)

                            def assemble_b():
                                tbk = tbank[1]
                                for c in range(8):
                                    S.transpose(tbk[:, c, :], yb[:, c * 128:(c + 1) * 128], ident.v, inc=(c == 7))
                                S.copy("act", yT.v, tbk.v)
                                for nb in range(2):
                                    bk = ocmp
                                    for c in range(8):
                                        S.matmul(bk.v, yT[:, c, :], wout[nb][:, c, :], start=(c == 0), stop=(c == 7))
                                    S.tt("dve", xo[:, nb * 512:(nb + 1) * 512], bk.v, x[:, nb * 512:(nb + 1) * 512], ALU.add)
                                S.dma("sp", xs[qb].v, xo.v)

                                def stats_and_prefetch():
                                    S.act(junk.v, xo.v, AF.Square, accum_out=ssnb[0][qb].v)
                                    if qb + 1 < NT:
                                        prefetch_asm(qb + 1)
                                defer(stats_and_prefetch, 3)
                            defer(assemble_b, 4)
                        return assemble

                    def prefetch_asm(qb):
                        S.dma("sp", sgat.v, SGA[qb].v)
                        S.dma("sp", yct.v, YC[qb].v)
                        S.dma("sp", xt[qb % 2].v, src_tiles[qb].v)

                    prefetch_asm(0)
                    S.memset("dve", ssn_hi, 0.0)

                    tiles = []
                    for g in range(4):
                        tiles.append(mk_cmp(0, g))
                    for qb in range(NT):
                        for g in range(4):
                            tiles += mk_branch(qb, g, [kt for kt in range(qb - 4, qb + 1) if kt >= 0], kwT, vwe, owin, {0: 0, 1: 1, 4: 2}, 32, False)
                            if qb + 1 < NT:
                                tiles.append(mk_cmp(qb + 1, g))
                            br = mk_branch(qb, g, list(range(qb + 1)), ksT, vse, osel, {0: 0, 1: 1}, 16, qb >= 8)
                            if g == 3:
                                qk_, ex_, pv_, post_ = br[-1]
                                asm = mk_asm(qb)

                                def post_and_asm(post_=post_, asm=asm):
                                    post_()
                                    asm()
                                br[-1] = (qk_, ex_, pv_, post_and_asm)
                            tiles += br

                    nt = len(tiles)
                    for i in range(nt + LA):
                        if i < nt:
                            tiles[i][0](SB[i % 3])
                        j = i - LA
                        if j >= 0:
                            cur["j"] = j
                            qk_, ex_, pv_, post_ = tiles[j]
                            pt = PT[j % 3]
                            ex_(SB[j % 3], pt)
                            pv_(pt)
                            if post_ is not None:
                                post_()
                            for fn in pending.pop(j, []):
                                fn()
                    while pending:
                        j = min(pending)
                        cur["j"] = j
                        for fn in pending.pop(j):
                            fn()
                    S.barrier(skip_pool=True)
                  if l == 0:
                      dump("x1", multi(xs, xs_ap), [SEQ, D])
                  S.barrier(skip_pool=True)

            if "xattn" in stages:
                with ExitStack() as esx:
                    hT = S.sbuf("hTx", [128, 8, SEQ], BF16, esx)
                    norm_T(hT, xs, "norm_x_g", l, pre=True)
                    q2T = S.sbuf("q2T", [128, 8, SEQ], BF16, esx)
                    kxT = S.sbuf("kxT", [128, 8, MEM], BF16, esx)
                    vxe = S.sbuf("vxe", [128, 2, 4, 257], BF16, esx)
                    PTx = [S.sbuf("PTx%d" % i, [128, 512], BF16, esx) for i in range(4)]
                    on = [S.sbuf("on%d" % i, [128, D], BF16, esx) for i in range(4)]
                    onT = S.sbuf("onT", [128, 8, 128], BF16, esx)
                    xo = S.sbuf("xox", [128, D], F32, esx)
                    rlx = S.sbuf("rlx", [128, 1], F32, esx)
                    for cb in range(2):
                        w = next_wb()
                        wload(w.v, I["xq_w"], l, 0, D, cb * 512, 512)
                        for c4 in range(4):
                            c = cb * 4 + c4

                            def lhs(k, w=w, c4=c4):
                                return w[:, k, c4 * 128:(c4 + 1) * 128]

                            def evq(tb, bk, c=c):
                                if ev_eng() == "act":
                                    S.act(q2T[:, c, tb * 512:(tb + 1) * 512], bk.v, AF.Copy, scale=0.0625)
                                else:
                                    S.ts("dve", q2T[:, c, tb * 512:(tb + 1) * 512], bk.v, 0.0625, ALU.mult)
                            fm_proj(hT, lhs, evq)
                    for cb in range(2):
                        w = next_wb()
                        wload(w.v, I["xkv_w"], l, 0, D, cb * 512, 512)
                        for c4 in range(4):
                            c = cb * 4 + c4
                            bk = next_bank()
                            for k in range(8):
                                S.matmul(bk[:, 0:MEM], w[:, k, c4 * 128:(c4 + 1) * 128], memT[:, k, :], start=(k == 0), stop=(k == 7))
                            S.copy(ev_eng(), kxT[:, c, :], bk[:, 0:MEM])
                    S.memset("dve", vxe[:, :, :, 256:257], 1.0)
                    for cb in range(2):
                        w = next_wb()
                        wload(w.v, I["xkv_w"], l, 0, D, D + cb * 512, 512)
                        for mt in range(2):
                            bk = next_bank()
                            for k in range(8):
                                S.matmul(bk.v, memT[:, k, mt * 128:(mt + 1) * 128], w[:, k, :], start=(k == 0), stop=(k == 7))
                            S.copy(ev_eng(), vxe[:, mt, 2 * cb:2 * cb + 2, 0:256], bk.v.map(lambda a: a.rearrange("p (h d) -> p h d", h=2)))
                    wxo = [next_wb(), next_wb()]
                    for nb in range(2):
                        wload(wxo[nb].v, I["xo_w"], l, 0, D, nb * 512, 512)
                    npt = 0
                    for tb in range(4):
                        for hh in range(4):
                            pts = []
                            for mt in range(2):
                                sb = next_bank()
                                for cch in range(2):
                                    S.matmul(sb.v, kxT[:, 2 * hh + cch, mt * 128:(mt + 1) * 128], q2T[:, 2 * hh + cch, tb * 512:(tb + 1) * 512],
                                             start=(cch == 0), stop=(cch == 1))
                                pt = PTx[npt % 4]
                                npt += 1
                                S.act(pt.v, sb.v, AF.Exp)
                                pts.append(pt)
                            for j in range(4):
                                ob = next_bank()
                                for mt in range(2):
                                    S.matmul(ob[:, 0:257], pts[mt][:, j * 128:(j + 1) * 128], vxe[:, mt, hh, :], start=(mt == 0), stop=(mt == 1))
                                S.recip(rlx.v, ob[:, 256:257])
                                S.ts("dve", on[j][:, hh * 256:(hh + 1) * 256], ob[:, 0:256], rlx.v, ALU.mult)
                        for j in range(4):
                            i = tb * 4 + j
                            tbk = tbank[j % 2]
                            for c in range(8):
                                S.transpose(tbk[:, c, :], on[j][:, c * 128:(c + 1) * 128], ident.v, inc=(c == 7))
                            S.copy("act", onT.v, tbk.v)
                            x = xt[st["xt"] % 2]
                            st["xt"] += 1
                            S.dma("sp", x.v, xs[i].v)
                            for nb in range(2):
                                bk = next_bank()
                                for c in range(8):
                                    S.matmul(bk.v, onT[:, c, :], wxo[nb][:, c, :], start=(c == 0), stop=(c == 7))
                                S.tt("dve", xo[:, nb * 512:(nb + 1) * 512], bk.v, x[:, nb * 512:(nb + 1) * 512], ALU.add)
                            S.act(junk.v, xo.v, AF.Square, accum_out=ssnb[0][i].v)
                            S.dma("sp", xs[i].v, xo.v)
                    S.barrier(skip_pool=True)
                if l == 0:
                    dump("x2", multi(xs, xs_ap), [SEQ, D])

            if "ffn" in stages:
                with ExitStack() as esf:
                    uT = S.sbuf("uT", [128, 22, SEQ], BF16, esf)
                    with ExitStack() as esf1:
                        hT = S.sbuf("hTf", [128, 8, SEQ], BF16, esf1)
                        norm_T(hT, xs, "norm_ffn_g", l, pre=True)
                        sa = [S.sbuf("sa%d" % i, [128, 512], F32, esf1) for i in range(2)]
                        for blk in range(6):
                            c0 = blk * 512
                            ncols = min(512, DFF - c0)
                            wa = next_wb()
                            wbb = next_wb()
                            wload(wa[:, :, 0:ncols], I["ffn_in_w"], l, 0, D, c0, ncols)
                            wload(wbb[:, :, 0:ncols], I["ffn_in_w"], l, 0, D, DFF + c0, ncols)
                            for c4 in range(ncols // 128):
                                c = blk * 4 + c4
                                for tb in range(4):
                                    ba = next_bank()
                                    bb = next_bank()
                                    for k in range(8):
                                        S.matmul(ba.v, wa[:, k, c4 * 128:(c4 + 1) * 128], hT[:, k, tb * 512:(tb + 1) * 512], start=(k == 0), stop=(k == 7))
                                    for k in range(8):
                                        S.matmul(bb.v, wbb[:, k, c4 * 128:(c4 + 1) * 128], hT[:, k, tb * 512:(tb + 1) * 512], start=(k == 0), stop=(k == 7))
                                    sv = sa[tb % 2]
                                    S.act(sv.v, ba.v, AF.Silu)
                                    S.tt("dve", uT[:, c, tb * 512:(tb + 1) * 512], sv.v, bb.v, ALU.mult)
                        S.barrier()
                    wfo = S.sbuf("wfo", [128, 22, 512], BF16, esf)
                    xh = [S.sbuf("xh%d" % i, [128, 512], F32, esf) for i in range(2)]
                    xoh = [S.sbuf("xoh%d" % i, [128, 512], F32, esf) for i in range(2)]
                    for nb in range(2):
                        for (c0, nch) in ((0, 8), (8, 8), (16, 6)):
                            wload(wfo[:, c0:c0 + nch, :], I["ffn_out_w"], l, c0 * 128, nch * 128, nb * 512, 512)
                        for i in range(NT):
                            bk = next_bank()
                            for c in range(22):
                                S.matmul(bk.v, uT[:, c, i * 128:(i + 1) * 128], wfo[:, c, :], start=(c == 0), stop=(c == 21))
                            x = xh[i % 2]
                            S.dma("sp", x.v, xs[i][:, nb * 512:(nb + 1) * 512])
                            xq = xoh[i % 2]
                            S.tt("dve", xq.v, bk.v, x.v, ALU.add)
                            S.act(junk[:, 0:512], xq.v, AF.Square, accum_out=ssnb[nb][i].v)
                            S.dma("sp", xs[i][:, nb * 512:(nb + 1) * 512], xq.v)
                    S.barrier(skip_pool=True)
                if l == 0:
                    dump("x3", multi(xs, xs_ap), [SEQ, D])

        out_toks = []
        with ExitStack() as esz:
            fo = [S.sbuf("fo%d" % i, [128, D], F32, esz) for i in range(2)]
            gb = gbuf[st["g"] % 2]
            st["g"] += 1
            bcast_row(gb.v, I["final_norm_g"], I["final_norm_g"].ap[0:1, :])
            fsrc = xs if n_layers > 0 else xin_t
            full = ("ffn" in stages) and n_layers > 0
            if full:
                batch_rstd()
            for i in range(NT):
                x = xt[st["xt"] % 2]
                sq = ss[st["xt"] % 2]
                rs = rstd[st["xt"] % 2]
                st["xt"] += 1
                S.dma("sp", x.v, fsrc[i].v)
                if full:
                    rs = rsn[:, i:i + 1]
                else:
                    S.act(junk.v, x.v, AF.Square, accum_out=sq.v)
                    S.ts("dve", rs.v, sq.v, 1.0 / D, ALU.mult, EPS, ALU.add)
                    S.act(rs.v, rs.v, AF.Sqrt)
                    S.recip(rs.v, rs.v)
                    rs = rs.v
                S.stt("dve", fo[i % 2].v, x.v, rs, gb.v, ALU.mult, ALU.mult)
                out_toks.append(S.dma("sp", out_t[i].v, fo[i % 2].v))
            S.wait_toks("sp", out_toks)
            S.barrier()
        S.wait_toks("sp", dbg_toks)
        S.emit()
        print("ops", {k: len(v.ops) for k, v in S.eng.items()}, "waits", S.nwaits, "dmas", S.ndma, flush=True)
    return nc, dbg_outs


_CONSTS = None


def make_in_map(inputs, b):
    global _CONSTS
    if _CONSTS is None:
        _CONSTS = host_consts()
    m = {}
    for name, shp in IN_SPECS:
        a = np.asarray(inputs[name])
        if name in ("x", "mem"):
            a = a[b]
        m[name] = np.ascontiguousarray(a, dtype=np.float32).reshape(shp)
    m.update(_CONSTS)
    return m


def kernel(**inputs):
    nc, _ = build()
    in_maps = [make_in_map(inputs, b) for b in range(8)]
    res = run_bass_kernel_spmd(nc, in_maps, core_ids=list(range(8)))
    return np.stack([np.asarray(r["out"], dtype=np.float32) for r in res.results], axis=0)
```

```python
import numpy as np
import ml_dtypes
from contextlib import ExitStack
import concourse.bass as bass
import concourse.mybir as mybir
from concourse.bass_utils import run_bass_kernel_spmd

F32 = mybir.dt.float32
BF16 = mybir.dt.bfloat16
ALU = mybir.AluOpType
AF = mybir.ActivationFunctionType
AX = mybir.AxisListType

SEQ = 2048
D = 1024
NT = 16
DEPTH = 4
NIN = 6704
DFF = 2816
MEM = 256
O_A, O_GT, O_Q, O_KC, O_VC, O_KS, O_VS, O_KW, O_VW, O_GH, O_GC, O_GA = (
    0, 1024, 2048, 3072, 3328, 3584, 3840, 4096, 4352, 4608, 4656, 5680)
EPS = 1e-6
NEG = -30000.0


class View:
    __slots__ = ("bufs", "ap")

    def __init__(self, bufs, ap):
        self.bufs = bufs
        self.ap = ap

    def __getitem__(self, idx):
        return View(self.bufs, self.ap[idx])

    def map(self, f):
        return View(self.bufs, f(self.ap))


class Buf:
    __slots__ = ("name", "ap", "lastw", "readers")

    def __init__(self, name, ap):
        self.name = name
        self.ap = ap
        self.lastw = None
        self.readers = []

    def __getitem__(self, idx):
        return View((self,), self.ap[idx])

    @property
    def v(self):
        return View((self,), self.ap)


def multi(bufs, ap):
    return View(tuple(bufs), ap)


class _Eng:
    def __init__(self, name):
        self.name = name
        self.key = name
        self.epoch = 0
        self.cnt = 0
        self.clock = {}
        self.ops = []
        self.pending = False


class Sched:
    NDMA = 12

    def __init__(self, nc, es):
        self.nc = nc
        self.es = es
        self.eng = {n: _Eng(n) for n in ("pe", "act", "dve", "pool", "sp")}
        self.sems = {}
        for n in self.eng:
            self.sems[n] = es.enter_context(nc.semaphore("s_" + n))
        self.nslots = {"sp": 8, "pool": 6, "act": 0}
        for q, n in self.nslots.items():
            for i in range(n):
                self.sems["d%s%d" % (q, i)] = es.enter_context(nc.semaphore("s_d%s%d" % (q, i)))
        self.ndma_q = {"sp": 0, "pool": 0}
        self.ndma = 0
        self.dlast = {}
        self.tclock = {}
        self.nwaits = 0
        self.nops = 0

    def sbuf(self, name, shape, dt, es=None):
        self.nops += 1
        name = "%s_%d" % (name, self.nops)
        t = (es or self.es).enter_context(self.nc.sbuf_tensor(name, list(shape), dt))
        return Buf(name, t[:])

    def psum(self, name, shape, dt, es=None):
        t = (es or self.es).enter_context(self.nc.psum_tensor(name, list(shape), dt))
        return Buf(name, t[:])

    def _need(self, E, tok, waits):
        s, v = tok
        if E.name == "pe" and s == E.key:
            return
        if E.clock.get(s, 0) >= v:
            return
        if waits.get(s, 0) < v:
            waits[s] = v

    def _apply(self, E, waits):
        for s, v in waits.items():
            tc = self.tclock.get((s, v))
            assert tc is not None, ("wait on unfinished token", s, v, E.name)
            ck = E.clock
            for k2, v2 in tc.items():
                if ck.get(k2, 0) < v2:
                    ck[k2] = v2
        self.nwaits += len(waits)
        return list(waits.items())

    def _deps(self, E, rb, wb):
        waits = {}
        for b in rb:
            if b.lastw is not None:
                self._need(E, b.lastw, waits)
        for b in wb:
            if b.lastw is not None:
                self._need(E, b.lastw, waits)
            for r in b.readers:
                self._need(E, r, waits)
        return self._apply(E, waits)

    def _mark(self, tok, rb, wb):
        for b in rb:
            if tok not in b.readers[-2:]:
                b.readers.append(tok)
        for b in wb:
            b.lastw = tok
            b.readers = []

    def op(self, ename, fn, reads=(), writes=(), inc=True):
        E = self.eng[ename]
        rb = [b for v in reads for b in v.bufs]
        wb = [b for v in writes for b in v.bufs]
        waits = self._deps(E, rb, wb)
        tok = (E.key, E.cnt + 1)
        if inc:
            E.cnt += 1
            c = dict(E.clock)
            c[E.key] = E.cnt
            self.tclock[tok] = c
            E.pending = False
        else:
            E.pending = True
        self._mark(tok, rb, wb)
        self.nops += 1
        E.ops.append((waits, fn, E.key if inc else None, 1))

    def dma(self, qname, out, in_, **kw):
        E = self.eng[qname]
        rb = list(in_.bufs)
        wb = list(out.bufs)
        i = self.ndma_q[qname]
        self.ndma_q[qname] += 1
        self.ndma += 1
        ns = self.nslots[qname]
        s = "d%s%d" % (qname, i % ns)
        val = 16 * (i // ns + 1)
        waits = {}
        for b in rb:
            if b.lastw is not None:
                self._need(E, b.lastw, waits)
        for b in wb:
            if b.lastw is not None:
                self._need(E, b.lastw, waits)
            for r in b.readers:
                self._need(E, r, waits)
        if val > 16:
            self._need(E, (s, val - 16), waits)
        wl = self._apply(E, waits)
        tok = (s, val)
        c = dict(E.clock)
        c[s] = val
        self.tclock[tok] = c
        self.dlast[s] = val
        self._mark(tok, rb, wb)
        oap, iap = out.ap, in_.ap
        E.ops.append((wl, lambda e: e.dma_start(out=oap, in_=iap, **kw), s, 16))
        return tok

    def wait_toks(self, ename, toks):
        E = self.eng[ename]
        waits = {}
        for t in toks:
            self._need(E, t, waits)
        wl = self._apply(E, waits)
        E.ops.append((wl, None, None, 0))

    def barrier(self, skip_pool=False):
        toks = [(E.key, E.cnt) for E in self.eng.values() if E.cnt > 0]
        toks += list(self.dlast.items())
        for n in self.eng:
            if skip_pool and n == "pool":
                continue
            self.wait_toks(n, toks)
        for E in self.eng.values():
            if E.cnt > 1500:
                E.epoch += 1
                E.key = "%s#%d" % (E.name, E.epoch)
                E.cnt = 0
                self.sems[E.key] = self.es.enter_context(self.nc.semaphore("s_%s_%d" % (E.name, E.epoch)))

    def matmul(self, out, lhsT, rhs, start=True, stop=True, inc=None, **kw):
        o, l, r = out.ap, lhsT.ap, rhs.ap
        if inc is None:
            inc = stop
        rd = [lhsT, rhs] + ([] if start else [out])
        self.op("pe", lambda e: e.matmul(o, l, r, start=start, stop=stop, **kw), rd, [out], inc=inc)

    def transpose(self, out, in_, ident, inc=True):
        o, i, d = out.ap, in_.ap, ident.ap
        self.op("pe", lambda e: e.transpose(o, i, d), [in_, ident], [out], inc=inc)

    def act(self, out, in_, func, bias=None, scale=None, accum_out=None):
        kw = {}
        rd = [in_]
        wr = [out]
        if bias is not None:
            if isinstance(bias, View):
                kw["bias"] = bias.ap
                rd.append(bias)
            else:
                kw["bias"] = bias
        if scale is not None:
            if isinstance(scale, View):
                kw["scale"] = scale.ap
                rd.append(scale)
            else:
                kw["scale"] = scale
        if accum_out is not None:
            kw["accum_out"] = accum_out.ap
            wr.append(accum_out)
        o, i = out.ap, in_.ap
        self.op("act", lambda e: e.activation(o, i, func, **kw), rd, wr)

    def tt(self, eng, out, in0, in1, op):
        o, a, b = out.ap, in0.ap, in1.ap
        self.op(eng, lambda e: e.tensor_tensor(o, a, b, op), [in0, in1], [out])

    def ts(self, eng, out, in0, s1, op0, s2=None, op1=None):
        rd = [in0]
        a1, a2 = s1, s2
        if isinstance(s1, View):
            rd.append(s1)
            a1 = s1.ap
        if isinstance(s2, View):
            rd.append(s2)
            a2 = s2.ap
        kw = {}
        if op1 is not None:
            kw["op1"] = op1
        o, i = out.ap, in0.ap
        self.op(eng, lambda e: e.tensor_scalar(o, i, a1, a2, op0, **kw), rd, [out])

    def stt(self, eng, out, in0, scalar, in1, op0, op1):
        rd = [in0, in1]
        sc = scalar
        if isinstance(scalar, View):
            rd.append(scalar)
            sc = scalar.ap
        o, a, b = out.ap, in0.ap, in1.ap
        self.op(eng, lambda e: e.scalar_tensor_tensor(o, a, sc, b, op0, op1), rd, [out])

    def copy(self, eng, out, in_):
        o, i = out.ap, in_.ap
        if eng == "act":
            self.op(eng, lambda e: e.copy(o, i), [in_], [out])
        else:
            self.op(eng, lambda e: e.tensor_copy(o, i), [in_], [out])

    def memset(self, eng, out, val):
        o = out.ap
        self.op(eng, lambda e: e.memset(o, val), [], [out])

    def reduce(self, eng, out, in_, op, axis=AX.X):
        o, i = out.ap, in_.ap
        self.op(eng, lambda e: e.tensor_reduce(o, i, axis, op), [in_], [out])

    def recip(self, out, in_):
        o, i = out.ap, in_.ap
        self.op("dve", lambda e: e.reciprocal(o, i), [in_], [out])

    def emit(self):
        nc = self.nc
        sems = self.sems
        for E in self.eng.values():
            assert not E.pending, E.name
        with nc.Block() as block:
            def mk(E):
                def body(eng):
                    for waits, fn, inc, amt in E.ops:
                        for s, v in waits:
                            eng.wait_ge(sems[s], v)
                        if fn is not None:
                            ins = fn(eng)
                            if inc is not None:
                                ins.then_inc(sems[inc], amt)
                return body
            block.tensor(mk(self.eng["pe"]))
            block.scalar(mk(self.eng["act"]))
            block.vector(mk(self.eng["dve"]))
            block.gpsimd(mk(self.eng["pool"]))
            block.sync(mk(self.eng["sp"]))


def _t5_bucket(n):
    n = np.maximum(n, 0)
    nf = np.maximum(n, 1).astype(np.float32)
    large = 16 + (np.log(nf / np.float32(16)) / np.float32(np.log(8.0)) * 16).astype(np.int32)
    large = np.minimum(large, 31)
    return np.where(n < 16, n, large)


def host_consts():
    bf = ml_dtypes.bfloat16
    c = {}
    c["c_ident"] = np.eye(128, dtype=np.float32).astype(bf)
    c["c_antij"] = np.eye(128, dtype=np.float32)[::-1].copy().astype(bf)
    i = np.arange(768)
    d = i - 127
    ok = (d >= 0) & (d < 512)
    oh = np.zeros((32, 768), np.float32)
    oh[_t5_bucket(d)[ok], i[ok]] = 1.0
    c["c_ohw"] = oh
    c["c_validw"] = np.broadcast_to(ok.astype(np.float32)[None], (16, 768)).copy()
    c["c_maskw"] = ((c["c_validw"] - 1.0) * 30000.0).astype(np.float32)
    dist = np.arange(128)
    oc = np.zeros((32, 128), np.float32)
    oc[_t5_bucket(dist), dist] = 1.0
    c["c_ohc"] = oc
    n = np.arange(128)[:, None]
    m = np.arange(32)[None, :]
    ov = ((16 * n < 64 * m + 64) & (16 * n + 32 > 64 * m)).astype(np.float32)
    ov[127] = 0
    c["c_ov"] = ov.astype(bf)
    ex = (np.arange(2048)[None, :] // 64 == np.arange(32)[:, None]).astype(np.float32)
    c["c_expand"] = np.tile(ex, (4, 1)).astype(bf)
    cand = np.zeros((128, 8, 32), np.float32)
    fval = np.zeros((128, 8, 32), np.float32)
    for qb in range(8, 16):
        t = qb * 128 + np.arange(128)
        blk = (t // 64)[:, None]
        mm = np.arange(32)[None, :]
        forced = (mm == 0) | (mm == blk) | (mm == blk - 1)
        valid = mm <= blk
        cand[:, qb - 8] = (valid & ~forced)
        fval[:, qb - 8] = np.where(forced, 100.0 + mm, np.where(valid, 0.0, -1.0))
    c["c_cand"] = cand
    c["c_fval"] = fval
    return c


CONST_SPECS = [("c_ident", [128, 128], BF16), ("c_antij", [128, 128], BF16), ("c_ohw", [32, 768], F32),
               ("c_validw", [16, 768], F32), ("c_maskw", [16, 768], F32), ("c_ohc", [32, 128], F32), ("c_ov", [128, 32], BF16),
               ("c_expand", [128, 2048], BF16), ("c_cand", [128, 8, 32], F32), ("c_fval", [128, 8, 32], F32)]

IN_SPECS = [("x", [SEQ, D]), ("mem", [MEM, D]), ("norm_mix_g", [DEPTH, D]), ("w_in", [DEPTH, D, NIN]),
            ("conv_dw_w", [DEPTH, 31, D]), ("conv_dw_b", [DEPTH, D]), ("conv_ln_g", [DEPTH, D]),
            ("conv_ln_b", [DEPTH, D]), ("conv_pw_w", [DEPTH, D, D]), ("cmp_pos", [DEPTH, 2, 32, 64]),
            ("cmp_w1", [DEPTH, 2, 32, 64, 64]), ("cmp_w2", [DEPTH, 2, 64, 64]), ("w_out", [DEPTH, D, D]),
            ("norm_x_g", [DEPTH, D]), ("xq_w", [DEPTH, D, D]), ("xkv_w", [DEPTH, D, 2 * D]),
            ("xo_w", [DEPTH, D, D]), ("norm_ffn_g", [DEPTH, D]), ("ffn_in_w", [DEPTH, D, 2 * DFF]),
            ("ffn_out_w", [DEPTH, DFF, D]), ("rel_bias", [32, 16]), ("final_norm_g", [1, D])]


def build(n_layers=DEPTH, stages=("mix", "xattn", "ffn"), dumps=()):
    nc = bass.Bass("TRN2", target_bir_lowering=False)
    I = {}
    for name, shp in IN_SPECS:
        I[name] = Buf(name, nc.dram_tensor(name, shp, F32, kind="ExternalInput").ap())
    for name, shp, dt in CONST_SPECS:
        I[name] = Buf(name, nc.dram_tensor(name, shp, dt, kind="ExternalInput").ap())
    OUT = nc.dram_tensor("out", [SEQ, D], F32, kind="ExternalOutput").ap()
    out_t = [Buf("out%d" % i, OUT[i * 128:(i + 1) * 128, :]) for i in range(NT)]

    def dram(name, shape, dt):
        return nc.dram_tensor(name, list(shape), dt, kind="Internal").ap()

    xs_ap = dram("xs", [SEQ, D], F32)
    xs = [Buf("xs%d" % i, xs_ap[i * 128:(i + 1) * 128, :]) for i in range(NT)]
    xin_t = [Buf("xin%d" % i, I["x"].ap[i * 128:(i + 1) * 128, :]) for i in range(NT)]
    yc_ap = dram("yc", [SEQ, D], F32)
    YC = [Buf("yc%d" % i, yc_ap[i * 128:(i + 1) * 128, :]) for i in range(NT)]
    sga_ap = dram("sga", [SEQ, D], F32)
    SGA = [Buf("sga%d" % i, sga_ap[i * 128:(i + 1) * 128, :]) for i in range(NT)]
    conv_ap = dram("convs", [8, 128, SEQ], F32)
    CONV = [[Buf("cv%d_%d" % (c, t), conv_ap[c, :, t * 512:(t + 1) * 512]) for t in range(4)] for c in range(8)]
    fw_ap = dram("fw", [16, 768], BF16)
    FW = Buf("fw", fw_ap)
    fc_ap = dram("fc", [16, 4096], BF16)
    FC = Buf("fc", fc_ap)
    ec_ap = dram("ecd", [16, 128, 16 * 128], BF16)
    ECD = [Buf("ecd%d" % q, ec_ap[q]) for q in range(16)]

    dbg_toks = []
    dbg_outs = []

    with ExitStack() as es:
        S = Sched(nc, es)

        def dump(name, view, shape, dt=F32):
            if name not in dumps:
                return
            t = nc.dram_tensor("dbg_" + name, list(shape), dt, kind="ExternalOutput").ap()
            dbg_outs.append("dbg_" + name)
            dbg_toks.append(S.dma("sp", Buf("dbg_" + name, t).v, view))

        ident = S.sbuf("ident", [128, 128], BF16)
        antij = S.sbuf("antij", [128, 128], BF16)
        ones = S.sbuf("ones", [128, 128], BF16)
        expand = S.sbuf("expand", [128, 2048], BF16)
        ewin = S.sbuf("ewin", [128, 3, 16, 128], BF16)
        cand = S.sbuf("cand", [128, 8, 32], F32)
        fval = S.sbuf("fval", [128, 8, 32], F32)
        dwT = S.sbuf("dwT", [128, DEPTH, 8, 31], F32)
        dwb = S.sbuf("dwb", [128, DEPTH, 8], F32)
        lng = S.sbuf("lng", [128, DEPTH, 8], F32)
        lnb = S.sbuf("lnb", [128, DEPTH, 8], F32)
        memT = S.sbuf("memT", [128, 8, MEM], BF16)
        gbuf = [S.sbuf("gbuf0", [128, D], F32)] * 2
        xt = [S.sbuf("xt%d" % i, [128, D], F32) for i in range(2)]
        hb = S.sbuf("hb", [128, D], BF16)
        hb2 = [hb, S.sbuf("hbb", [128, D], BF16)]
        junk = S.sbuf("junk", [128, D], BF16)
        ss = [S.sbuf("ss%d" % i, [128, 1], F32) for i in range(2)]
        rstd = [S.sbuf("rstd%d" % i, [128, 1], F32) for i in range(2)]
        ssn_t = S.sbuf("ssn", [128, 2, NT], F32)
        ssnb = [[Buf("ssn%d_%d" % (h_, i_), ssn_t.ap[:, h_, i_:i_ + 1]) for i_ in range(NT)] for h_ in range(2)]
        ssn_all = multi([b_ for r_ in ssnb for b_ in r_], ssn_t.ap)
        ssn_hi = multi(ssnb[1], ssn_t.ap[:, 1, :])
        rsn = S.sbuf("rsn", [128, NT], F32)
        NWB = 4
        wb = [S.sbuf("wb%d" % i, [128, 8, 512], BF16) for i in range(NWB)]
        banks = [S.psum("bk%d" % i, [128, 512], F32) for i in range(6)]
        tbank = [S.psum("tb%d" % i, [128, 8, 128], BF16) for i in range(2)]
        st = {"wb": 0, "bk": 0, "ev": 0, "g": 0, "xt": 0}

        st["nwb"] = NWB

        def next_wb():
            b = wb[st["wb"] % st["nwb"]]
            st["wb"] += 1
            return b

        def next_bank():
            b = banks[st["bk"] % 6]
            st["bk"] += 1
            return b

        def ev_eng():
            st["ev"] += 1
            return "act" if st["ev"] % 2 else "dve"

        def wload(dst, W, l, r0, nr, c0, ncols):
            src = W.ap[l, r0:r0 + nr, c0:c0 + ncols].rearrange("(c p) n -> p c n", p=128)
            S.dma("pool", dst, View((W,), src))

        def bcast_row(dst, src_buf, row_ap):
            S.dma("sp", dst, View((src_buf,), row_ap.partition_broadcast(128)))

        S.dma("sp", ident.v, I["c_ident"].v)
        S.dma("sp", antij.v, I["c_antij"].v)
        S.dma("sp", expand.v, I["c_expand"].v)
        S.dma("sp", cand.v, I["c_cand"].v)
        S.dma("sp", fval.v, I["c_fval"].v)
        S.memset("dve", ones.v, 1.0)
        for l_ in range(DEPTH):
            for c_ in range(8):
                S.dma("sp", dwT[:, l_, c_, :], I["conv_dw_w"].v.map(lambda a: a[l_, :, c_ * 128:(c_ + 1) * 128].rearrange("k p -> p k")),
                      allow_slow_non_contiguous=True)
            for dst, nm in ((dwb, "conv_dw_b"), (lng, "conv_ln_g"), (lnb, "conv_ln_b")):
                S.dma("sp", dst[:, l_, :], I[nm].v.map(lambda a: a[l_].rearrange("(c p) -> p c", p=128)),
                      allow_slow_non_contiguous=True)

        with ExitStack() as es0:
            rbp = S.sbuf("rbp", [32, 16], F32, es0)
            ohw = S.sbuf("ohw", [32, 768], F32, es0)
            ohc = S.sbuf("ohc", [32, 128], F32, es0)
            validw = S.sbuf("validw", [16, 768], F32, es0)
            maskw = S.sbuf("maskw", [16, 768], F32, es0)
            neg31 = S.sbuf("neg31", [16, 1], F32, es0)
            fwf = S.sbuf("fwf", [16, 768], F32, es0)
            fwb = S.sbuf("fwb", [16, 768], BF16, es0)
            fcb = S.sbuf("fcb", [16, 4096], BF16, es0)
            tmph = S.sbuf("tmph", [128, 16, 128], BF16, es0)
            tmpe = [S.sbuf("tmpe%d" % i, [128, 16, 128], BF16, es0) for i in range(2)]
            memf = S.sbuf("memf", [128, D], F32, es0)
            S.dma("sp", rbp.v, I["rel_bias"].v)
            S.dma("sp", ohw.v, I["c_ohw"].v)
            S.dma("sp", ohc.v, I["c_ohc"].v)
            S.dma("sp", validw.v, I["c_validw"].v)
            S.dma("sp", maskw.v, I["c_maskw"].v)
            S.dma("sp", neg31.v, I["rel_bias"].v.map(lambda a: a[31:32, :].rearrange("o h -> h o")),
                  allow_slow_non_contiguous=True)
            S.ts("dve", neg31.v, neg31.v, -1.0, ALU.mult)
            b0, b1, b2 = banks[0], banks[1], banks[2]
            S.matmul(b0[0:16, 0:384], rbp.v, ohw[:, 0:384])
            S.matmul(b1[0:16, 0:384], rbp.v, ohw[:, 384:768])
            S.matmul(b2[0:16, 0:128], rbp.v, ohc.v)
            S.act(fwf[:, 0:384], b0[0:16, 0:384], AF.Identity, bias=neg31.v)
            S.act(fwf[:, 384:768], b1[0:16, 0:384], AF.Identity, bias=neg31.v)
            S.tt("dve", fwf.v, fwf.v, validw.v, ALU.mult)
            S.tt("dve", fwb.v, fwf.v, maskw.v, ALU.add)
            S.memset("dve", fcb[:, 0:2063], NEG)
            S.memset("dve", fcb[:, 2063 + 128:4096], 0.0)
            S.act(fcb[:, 2063:2063 + 128], b2[0:16, 0:128], AF.Identity, bias=neg31.v)
            S.dma("sp", FW.v, fwb.v)
            S.dma("sp", FC.v, fcb.v)
            dump("fw", fwf.v, [16, 768])
            for ti, delta in enumerate((0, 128, 512)):
                src = bass.AP(tensor=fw_ap.tensor, offset=delta, ap=[[1, 128], [768, 16], [1, 128]])
                S.dma("sp", tmph.v, View((FW,), src))
                for g in range(4):
                    bk = next_bank()
                    S.matmul(bk.v, antij.v, tmph[:, 4 * g:4 * g + 4, :])
                    S.copy(ev_eng(), ewin[:, ti, 4 * g:4 * g + 4, :],
                           bk.v.map(lambda a: a.rearrange("p (r q) -> p r q", r=4)))
            for qb in range(16):
                src = bass.AP(tensor=fc_ap.tensor, offset=128 * qb, ap=[[16, 128], [4096, 16], [1, 128]])
                S.dma("sp", tmph.v, View((FC,), src))
                te = tmpe[qb % 2]
                for g in range(4):
                    bk = next_bank()
                    S.matmul(bk.v, antij.v, tmph[:, 4 * g:4 * g + 4, :])
                    S.copy(ev_eng(), te[:, 4 * g:4 * g + 4, :],
                           bk.v.map(lambda a: a.rearrange("p (r q) -> p r q", r=4)))
                S.dma("sp", ECD[qb].v, te.v.map(lambda a: a.rearrange("p h q -> p (h q)")))
            for mt in range(2):
                S.dma("sp", memf.v, I["mem"][mt * 128:(mt + 1) * 128, :])
                S.copy("dve", hb.v, memf.v)
                tb = tbank[mt % 2]
                for c in range(8):
                    S.transpose(tb[:, c, :], hb[:, c * 128:(c + 1) * 128], ident.v, inc=(c == 7))
                S.copy("act", memT[:, :, mt * 128:(mt + 1) * 128], tb.v)
            S.barrier(skip_pool=True)
        dump("ewin", ewin.v, [128, 3, 16, 128], BF16)

        def batch_rstd():
            S.tt("dve", rsn.v, ssn_all.map(lambda a: a[:, 0, :]), ssn_all.map(lambda a: a[:, 1, :]), ALU.add)
            S.ts("dve", rsn.v, rsn.v, 1.0 / D, ALU.mult, EPS, ALU.add)
            S.act(rsn.v, rsn.v, AF.Sqrt)
            S.recip(rsn.v, rsn.v)

        def norm_T(hT, src_tiles, gname, l, pre=False):
            gb = gbuf[st["g"] % 2]
            st["g"] += 1
            if l is None:
                bcast_row(gb.v, I[gname], I[gname].ap[0:1, :])
            else:
                bcast_row(gb.v, I[gname], I[gname].ap[l:l + 1, :])
            if pre:
                batch_rstd()
                for i in range(NT):
                    x = xt[st["xt"] % 2]
                    st["xt"] += 1
                    S.dma("sp", x.v, src_tiles[i].v)
                    hbi = hb2[i % 2]
                    S.stt("dve", hbi.v, x.v, rsn[:, i:i + 1], gb.v, ALU.mult, ALU.mult)
                    tb = tbank[i % 2]
                    for c in range(8):
                        S.transpose(tb[:, c, :], hbi[:, c * 128:(c + 1) * 128], ident.v, inc=(c == 7))
                    S.copy(ev_eng(), hT[:, :, i * 128:(i + 1) * 128], tb.v)
                return
            for i in range(NT):
                x = xt[st["xt"] % 2]
                sq = ss[st["xt"] % 2]
                rs = rstd[st["xt"] % 2]
                st["xt"] += 1
                S.dma("sp", x.v, src_tiles[i].v)
                S.act(junk.v, x.v, AF.Square, accum_out=sq.v)
                S.ts("dve", rs.v, sq.v, 1.0 / D, ALU.mult, EPS, ALU.add)
                S.act(rs.v, rs.v, AF.Sqrt)
                S.recip(rs.v, rs.v)
                S.stt("dve", hb.v, x.v, rs.v, gb.v, ALU.mult, ALU.mult)
                tb = tbank[i % 2]
                for c in range(8):
                    S.transpose(tb[:, c, :], hb[:, c * 128:(c + 1) * 128], ident.v, inc=(c == 7))
                S.copy(ev_eng(), hT[:, :, i * 128:(i + 1) * 128], tb.v)

        def fm_proj(hT, lhs_of_k, evac, nk=8):
            for tb in range(4):
                bk = next_bank()
                for k in range(nk):
                    S.matmul(bk.v, lhs_of_k(k), hT[:, k, tb * 512:(tb + 1) * 512], start=(k == 0), stop=(k == nk - 1))
                evac(tb, bk)

        for l in range(n_layers):
            src_tiles = xin_t if l == 0 else xs
            if "mix" in stages:
                with ExitStack() as esm:
                  if True:
                    esh = esm
                    arena = S.sbuf("arena", [128, 8 * SEQ], BF16, esh)
                    hT = Buf("hT%d" % l, arena.ap.rearrange("p (c t) -> p c t", c=8))
                    norm_T(hT, src_tiles, "norm_mix_g", l, pre=(l > 0))
                    if l == 0:
                        dump("hT", hT.v, [128, 8, SEQ], BF16)
                    with ExitStack() as esa:
                        vTc = [S.sbuf("vTc%d" % i, [128, 30 + SEQ], BF16, esa) for i in range(2)]
                        diag = [S.sbuf("diag%d" % i, [128, 31, 128], BF16, esa) for i in range(2)]
                        sgs = [S.sbuf("sgs%d" % i, [128, 512], F32, esa) for i in range(2)]
                        cvo = [S.sbuf("cvo%d" % i, [128, 512], F32, esa) for i in range(2)]
                        for i in range(2):
                            S.memset("dve", vTc[i][:, 0:30], 0.0)
                        for cc in range(8):
                            w = next_wb()
                            wload(w[:, :, 0:128], I["w_in"], l, 0, D, O_A + cc * 128, 128)
                            wload(w[:, :, 128:256], I["w_in"], l, 0, D, O_GT + cc * 128, 128)
                            vt = vTc[cc % 2]
                            dg = diag[cc % 2]
                            for tb in range(4):
                                ba = next_bank()
                                bg = next_bank()
                                for k in range(8):
                                    S.matmul(ba.v, w[:, k, 0:128], hT[:, k, tb * 512:(tb + 1) * 512], start=(k == 0), stop=(k == 7))
                                for k in range(8):
                                    S.matmul(bg.v, w[:, k, 128:256], hT[:, k, tb * 512:(tb + 1) * 512], start=(k == 0), stop=(k == 7))
                                sg = sgs[tb % 2]
                                S.act(sg.v, bg.v, AF.Sigmoid)
                                S.tt("dve", vt[:, 30 + tb * 512:30 + (tb + 1) * 512], ba.v, sg.v, ALU.mult)
                            S.tt("dve", dg.v,
                                 ident.v.map(lambda a: a.unsqueeze(1).to_broadcast([128, 31, 128])),
                                 dwT[:, l, cc, :].map(lambda a: a.unsqueeze(2).to_broadcast([128, 31, 128])),
                                 ALU.mult)
                            for tb in range(4):
                                bk = next_bank()
                                for k in range(31):
                                    S.matmul(bk.v, dg[:, k, :], vt[:, tb * 512 + k:tb * 512 + k + 512], start=(k == 0), stop=(k == 30))
                                co = cvo[tb % 2]
                                S.act(co.v, bk.v, AF.Identity, bias=dwb[:, l, cc:cc + 1])
                                S.dma("sp", CONV[cc][tb].v, co.v)
                        S.barrier(skip_pool=True)
                    with ExitStack() as esb:
                        cvall2 = [S.sbuf("cvall%d" % i, [128, 8, 512], F32, esb) for i in range(2)]
                        cbf2 = [S.sbuf("cbf%d" % i, [128, 8, 512], BF16, esb) for i in range(2)]
                        sqb2 = [S.sbuf("sqb%d" % i, [128, 8, 512], BF16, esb) for i in range(2)]
                        zT2 = [S.sbuf("zT%d" % i, [128, 8, 512], BF16, esb) for i in range(2)]
                        mean = S.sbuf("mean", [128, 512], F32, esb)
                        msq = S.sbuf("msq", [128, 512], F32, esb)
                        rsd = S.sbuf("rsd", [128, 512], F32, esb)
                        sgc = [S.sbuf("sgc%d" % i, [128, 512], F32, esb) for i in range(2)]
                        yco = [S.sbuf("yco%d" % i, [128, 512], F32, esb) for i in range(2)]
                        wpw = [next_wb(), next_wb()]
                        wgc = [next_wb(), next_wb()]
                        for nb in range(2):
                            wload(wpw[nb].v, I["conv_pw_w"], l, 0, D, nb * 512, 512)
                            wload(wgc[nb].v, I["w_in"], l, 0, D, O_GC + nb * 512, 512)

                        def prep_a(tb):
                            cvall, cbf, sqb = cvall2[tb % 2], cbf2[tb % 2], sqb2[tb % 2]
                            bufs = [CONV[c][tb] for c in range(8)]
                            S.dma("sp", cvall.v, multi(bufs, conv_ap[:, :, tb * 512:(tb + 1) * 512].rearrange("c p t -> p c t")))
                            S.copy("dve", cbf.v, cvall.v)
                            S.act(sqb.v, cvall.v, AF.Square)

                        def prep_b(tb):
                            cvall, cbf, sqb, zT = cvall2[tb % 2], cbf2[tb % 2], sqb2[tb % 2], zT2[tb % 2]
                            b1_ = next_bank()
                            b2_ = next_bank()
                            for c in range(8):
                                S.matmul(b1_.v, ones.v, cbf[:, c, :], start=(c == 0), stop=(c == 7))
                            for c in range(8):
                                S.matmul(b2_.v, ones.v, sqb[:, c, :], start=(c == 0), stop=(c == 7))
                            S.act(mean.v, b1_.v, AF.Copy, scale=1.0 / D)
                            S.tt("dve", msq.v, mean.v, mean.v, ALU.mult)
                            S.stt("dve", rsd.v, b2_.v, 1.0 / D, msq.v, ALU.mult, ALU.subtract)
                            S.ts("dve", rsd.v, rsd.v, EPS, ALU.add)
                            S.act(rsd.v, rsd.v, AF.Sqrt)
                            S.recip(rsd.v, rsd.v)
                            S.tt("dve", cvall.v, cvall.v, mean.v.map(lambda a: a.unsqueeze(1).to_broadcast([128, 8, 512])), ALU.subtract)
                            S.tt("dve", cvall.v, cvall.v, rsd.v.map(lambda a: a.unsqueeze(1).to_broadcast([128, 8, 512])), ALU.mult)
                            for c in range(8):
                                S.act(zT[:, c, :], cvall[:, c, :], AF.Silu, scale=lng[:, l, c:c + 1], bias=lnb[:, l, c:c + 1])

                        def mm_tile(tb, j):
                            zT = zT2[tb % 2]
                            i = tb * 4 + j
                            for nb in range(2):
                                bc = next_bank()
                                bg = next_bank()
                                for c in range(8):
                                    S.matmul(bc.v, zT[:, c, j * 128:(j + 1) * 128], wpw[nb][:, c, :], start=(c == 0), stop=(c == 7))
                                for k in range(8):
                                    S.matmul(bg.v, hT[:, k, i * 128:(i + 1) * 128], wgc[nb][:, k, :], start=(k == 0), stop=(k == 7))
                                sg = sgc[nb]
                                yo = yco[nb]
                                S.act(sg.v, bg.v, AF.Sigmoid)
                                S.tt("dve", yo.v, bc.v, sg.v, ALU.mult)
                                S.dma("sp", YC[i][:, nb * 512:(nb + 1) * 512], yo.v)

                        prep_a(0)
                        prep_b(0)
                        for tb in range(4):
                            if tb + 1 < 4:
                                prep_a(tb + 1)
                            mm_tile(tb, 0)
                            if tb + 1 < 4:
                                prep_b(tb + 1)
                            for j in range(1, 4):
                                mm_tile(tb, j)
                        S.barrier(skip_pool=True)
                    if l == 0:
                        dump("yc", multi(YC, yc_ap), [SEQ, D])
                    esn = esm
                    qT = S.sbuf("qT", [128, 8, SEQ], BF16, esn)
                    kwT = S.sbuf("kwT", [128, 4, SEQ], BF16, esn)
                    ksT = S.sbuf("ksT", [128, 4, SEQ], BF16, esn)
                    S.memset("dve", kwT.v, 0.0)
                    S.memset("dve", ksT.v, 0.0)
                    vse = S.sbuf("vse", [128, NT, 4, 65], BF16, esn)
                    vwe = S.sbuf("vwe", [128, NT, 4, 65], BF16, esn)
                    gsig = S.sbuf("gsig", [128, NT, 48], F32, esn)
                    kcmpT = S.sbuf("kcmpT", [128, 4, 128], BF16, esn)
                    vce = S.sbuf("vce", [128, 4, 97], BF16, esn)
                    with ExitStack() as esp:
                        st["nwb"] = 2
                        kcT = Buf("kcT%d" % l, wb[2].ap.rearrange("p c t -> p (c t)").rearrange("p (a t) -> p a t", a=2))
                        vcT = Buf("vcT%d" % l, wb[3].ap.rearrange("p c t -> p (c t)").rearrange("p (a t) -> p a t", a=2))
                        for gp in range(2):
                            w = next_wb()
                            for e_ in range(2):
                                for r_ in range(4):
                                    wload(w[:, :, r_ * 128 + e_ * 64:r_ * 128 + e_ * 64 + 64], I["w_in"], l, 0, D,
                                          O_Q + gp * 512 + e_ * 256 + r_ * 64, 64)
                            for r in range(4):
                                cidx = gp * 4 + r

                                def lhs(k, w=w, r=r):
                                    return w[:, k, r * 128:(r + 1) * 128]

                                def evq(tb, bk, cidx=cidx):
                                    if ev_eng() == "act":
                                        S.act(qT[:, cidx, tb * 512:(tb + 1) * 512], bk.v, AF.Copy, scale=0.125)
                                    else:
                                        S.ts("dve", qT[:, cidx, tb * 512:(tb + 1) * 512], bk.v, 0.125, ALU.mult)
                                fm_proj(hT, lhs, evq)
                        w = next_wb()
                        wload(w.v, I["w_in"], l, 0, D, O_KC, 512)
                        w2_ = next_wb()
                        wload(w2_[:, :, 0:256], I["w_in"], l, 0, D, O_KS, 256)
                        wload(w2_[:, :, 256:512], I["w_in"], l, 0, D, O_KW, 256)
                        for (wt, c0, dst) in ((w, 0, kcT), (w, 256, vcT), (w2_, 0, ksT), (w2_, 256, kwT)):
                            for gp in range(2):
                                def lhs(k, wt=wt, c0=c0, gp=gp):
                                    return wt[:, k, c0 + gp * 128:c0 + (gp + 1) * 128]

                                def evk(tb, bk, dst=dst, gp=gp):
                                    if dst is kwT or dst is ksT:
                                        en_ = ev_eng()
                                        S.copy(en_, dst[0:64, 2 * gp, tb * 512:(tb + 1) * 512], bk[0:64, :])
                                        S.copy(en_, dst[64:128, 2 * gp + 1, tb * 512:(tb + 1) * 512], bk[64:128, :])
                                    else:
                                        S.copy(ev_eng(), dst[:, gp, tb * 512:(tb + 1) * 512], bk.v)
                                fm_proj(hT, lhs, evk)
                        wtm = [next_wb(), next_wb()]
                        wload(wtm[0][:, :, 0:256], I["w_in"], l, 0, D, O_VS, 256)
                        wload(wtm[1][:, :, 0:304], I["w_in"], l, 0, D, O_VW, 304)
                        S.memset("dve", vse[:, :, :, 64:65], 1.0)
                        S.memset("dve", vwe[:, :, :, 64:65], 1.0)
                        for i in range(NT):
                            ba = next_bank()
                            bb = next_bank()
                            for k in range(8):
                                S.matmul(ba[:, 0:256], hT[:, k, i * 128:(i + 1) * 128], wtm[0][:, k, 0:256], start=(k == 0), stop=(k == 7))
                            for k in range(8):
                                S.matmul(bb[:, 0:304], hT[:, k, i * 128:(i + 1) * 128], wtm[1][:, k, 0:304], start=(k == 0), stop=(k == 7))
                            S.copy("dve", vse[:, i, :, 0:64], ba[:, 0:256].map(lambda a: a.rearrange("p (g d) -> p g d", g=4)))
                            S.copy("act", vwe[:, i, :, 0:64], bb[:, 0:256].map(lambda a: a.rearrange("p (g d) -> p g d", g=4)))
                            S.act(gsig[:, i, :], bb[:, 256:304], AF.Sigmoid)
                        with ExitStack() as esg:
                            sgt = [S.sbuf("sgt%d" % i, [128, 512], F32, esg) for i in range(2)]
                            for nb in range(2):
                                w = next_wb()
                                wload(w.v, I["w_in"], l, 0, D, O_GA + nb * 512, 512)
                                for i in range(NT):
                                    bk = next_bank()
                                    for k in range(8):
                                        S.matmul(bk.v, hT[:, k, i * 128:(i + 1) * 128], w[:, k, :], start=(k == 0), stop=(k == 7))
                                    sg = sgt[i % 2]
                                    S.act(sg.v, bk.v, AF.Sigmoid)
                                    S.dma("sp", SGA[i][:, nb * 512:(nb + 1) * 512], sg.v)
                            S.barrier()
                        with ExitStack() as esc:
                            w1d = S.sbuf("w1d", [128, 2, 32, 64], BF16, esc)
                            w2d = S.sbuf("w2d", [64, 2, 128], BF16, esc)
                            posd = S.sbuf("posd", [128, 2, 32], BF16, esc)
                            cbias = S.sbuf("cbias", [64, 2], F32, esc)
                            sz = S.sbuf("sz", [64, 128], BF16, esc)
                            for hf in range(2):
                                for j_ in range(2):
                                    S.dma("pool", w1d[hf * 64:(hf + 1) * 64, j_], I["cmp_w1"].v.map(lambda a: a[l, j_].rearrange("t d e -> d t e")))
                                S.dma("pool", w2d[:, :, hf * 64:(hf + 1) * 64], I["cmp_w2"].v.map(lambda a: a[l].rearrange("j e f -> e j f")))
                            for j_ in range(2):
                                for hf in range(2):
                                    S.dma("pool", posd[hf * 64:(hf + 1) * 64, j_, :], I["cmp_pos"].v.map(lambda a: a[l, j_].rearrange("t d -> d t")), allow_slow_non_contiguous=True)
                            S.memset("dve", kcmpT.v, 0.0)
                            S.memset("dve", vce.v, 0.0)
                            S.memset("dve", vce[:, :, 64:65], 1.0)
                            S.memset("dve", sz.v, 0.0)
                            for g in range(4):
                                S.dma("sp", vce[:, g, 65:97], I["c_ov"].v)
                            dump("w2d", w2d.v, [64, 2, 128], BF16)
                            dump("w1d", w1d.v, [128, 2, 32, 64], BF16)
                            dump("kcT", kcT.v, [128, 2, SEQ], BF16)
                            for j, srcT in ((0, kcT), (1, vcT)):
                                for g in range(4):
                                    gp, e = divmod(g, 2)
                                    hs = slice(e * 64, e * 64 + 64)
                                    bk = next_bank()
                                    for t in range(32):
                                        S.matmul(bk[0:64, 0:127], w1d[hs, j, t, :], srcT[hs, gp, t:t + 16 * 126 + 1:16], start=(t == 0), stop=False)
                                        S.matmul(bk[0:64, 0:127], w1d[hs, j, t, :], posd[hs, j, t:t + 1].map(lambda a: a.to_broadcast([64, 127])), start=False, stop=(t == 31))
                                    S.act(sz[:, 0:127], bk[0:64, 0:127], AF.Silu)
                                    bo = next_bank()
                                    if j == 0:
                                        if g == 0:
                                            dump("sz0", sz.v, [64, 128], BF16)
                                        S.matmul(bo[:, 0:127], w2d[:, 0, :], sz[:, 0:127])
                                        S.copy("dve", kcmpT[hs, g, 0:127], bo[hs, 0:127])
                                    else:
                                        S.matmul(bo[0:127, 0:64], sz[:, 0:127], w2d[:, 1, 0:64])
                                        S.copy("dve", vce[0:127, g, 0:64], bo[0:127, 0:64])
                            S.barrier()
                        S.barrier()
                        st["nwb"] = NWB
                    if l == 0:
                        dump("qT", qT.v, [128, 8, SEQ], BF16)
                        dump("kwT", kwT.v, [128, 4, SEQ], BF16)
                        dump("vwe", vwe.v, [128, NT, 4, 65], BF16)
                        dump("gsig", gsig.v, [128, NT, 48])
                        dump("kcmpT", kcmpT.v, [128, 4, 128], BF16)
                        dump("vce", vce.v, [128, 4, 97], BF16)
                    S.barrier()

                  with ExitStack() as esw:
                    ar = {"off": 0}

                    def take(name, shape, dt):
                        n = 1
                        for d_ in shape[1:]:
                            n *= d_
                        nb = n * (4 if dt == F32 else 2)
                        nb = (nb + 31) // 32 * 32
                        o0 = ar["off"]
                        ar["off"] += nb // 2
                        assert ar["off"] <= 8 * SEQ
                        ap = arena.ap[:, o0:o0 + nb // 2]
                        if dt == F32:
                            ap = ap.bitcast(F32)[:, 0:n]
                        else:
                            ap = ap[:, 0:n]
                        if len(shape) == 3:
                            ap = ap.rearrange("p (a b) -> p a b", a=shape[1])
                        elif len(shape) == 4:
                            ap = ap.rearrange("p (a b c) -> p a b c", a=shape[1], b=shape[2])
                        return Buf("%s_%d" % (name, l), ap)

                    PT = [take("PT%d" % i, [128, 512], BF16) for i in range(3)]
                    ect = [take("ect0", [128, 16 * 128], BF16)] * 2
                    nsa = take("nsa", [128, 16, 64], F32)
                    tmpo = take("tmpo", [128, 4, 64], F32)
                    rl = take("rl", [128, 4], F32)
                    ccf = take("ccf", [128, 4], F32)
                    impn = take("impn", [128, 4, 32], F32)
                    imp = take("imp", [128, 32], F32)
                    score = take("score", [128, 32], F32)
                    work = take("work", [128, 32], F32)
                    m8a = take("m8a", [128, 8], F32)
                    m8b = take("m8b", [128, 8], F32)
                    selb4 = take("selb4", [128, 4, 128], BF16)
                    selbT = take("selbT", [128, 4, 128], BF16)
                    sgat = take("sgat", [128, D], F32)
                    yct = take("yct", [128, D], F32)
                    S.memset("dve", selb4.v, 0.0)
                    yb = hb
                    yT = take("yT", [128, 8, 128], BF16)
                    xo = yct
                    wout = [next_wb(), next_wb()]
                    for nb in range(2):
                        wload(wout[nb].v, I["w_out"], l, 0, D, nb * 512, 512)
                    SB = [banks[0], banks[1], banks[5]]
                    osel, owin, ocmp = banks[2], banks[3], banks[4]
                    LA = 2

                    def v4(view):
                        return view.map(lambda a: a.rearrange("p (r q) -> p r q", r=4))

                    nsa2 = [nsa, S.sbuf("nsa2", [128, 16, 64], F32, esw)]
                    ect = [ect[0], S.sbuf("ect1", [128, 16 * 128], BF16, esw)]
                    selbT2 = [selbT, S.sbuf("selbT2", [128, 4, 128], BF16, esw)]
                    rlc = S.sbuf("rlc", [128, 4], F32, esw)
                    ccc = S.sbuf("ccc", [128, 4], F32, esw)
                    pending = {}
                    cur = {"j": 0}

                    def defer(fn, k):
                        pending.setdefault(cur["j"] + k, []).append(fn)

                    def mk_cmp(qb, g):
                        gp = g // 2
                        ec = ect[qb % 2]
                        qv = qT[:, gp * 4:gp * 4 + 4, qb * 128:(qb + 1) * 128]
                        nsab = nsa2[qb % 2]
                        sbT = selbT2[qb % 2]

                        def qk_c(sb):
                            if g == 0:
                                S.dma("sp", ec.v, ECD[qb].v)
                            S.matmul(sb.v, kcmpT[:, g, :], qv, start=True, stop=False)
                            S.matmul(sb.v, ident.v, ec[:, g * 512:(g + 1) * 512], start=False, stop=True)

                        def ex_c(sb, pt):
                            S.act(pt.v, sb.v, AF.Exp)

                        def pv_c(pt):
                            for r in range(4):
                                S.matmul(ocmp[:, r * 97:(r + 1) * 97], pt[:, r * 128:(r + 1) * 128], vce[:, g, :],
                                         start=True, stop=True, inc=(r == 3))

                        def post_c():
                            oc3 = ocmp.v.map(lambda a: a[:, 0:388].rearrange("p (r c) -> p r c", c=97))
                            S.ts("dve", rlc.v, oc3[:, :, 64], 1e-30, ALU.max)
                            S.recip(rlc.v, rlc.v)
                            S.tt("dve", ccc.v, gsig[:, qb, 4 * g:4 * g + 4], rlc.v, ALU.mult)
                            S.tt("dve", nsab[:, 4 * g:4 * g + 4, :], oc3[:, :, 0:64],
                                 ccc.v.map(lambda a: a.unsqueeze(2).to_broadcast([128, 4, 64])), ALU.mult)
                            if qb >= 8:
                                S.tt("dve", impn.v, oc3[:, :, 65:97], rlc.v.map(lambda a: a.unsqueeze(2).to_broadcast([128, 4, 32])), ALU.mult)
                                S.reduce("dve", imp.v, impn.v.map(lambda a: a.rearrange("p r m -> p m r")), ALU.add)
                                S.tt("dve", score.v, imp.v, cand[:, qb - 8, :], ALU.mult)
                                S.tt("dve", score.v, score.v, fval[:, qb - 8, :], ALU.add)
                                S.op("dve", lambda en: en.max(m8a.ap, score.ap), [score.v], [m8a.v])
                                S.op("dve", lambda en: en.match_replace(work.ap, m8a.ap, score.ap, -2.0), [m8a.v, score.v], [work.v])
                                S.op("dve", lambda en: en.max(m8b.ap, work.ap), [work.v], [m8b.v])
                                S.ts("dve", selb4[:, g, 32 * g:32 * g + 32], score.v, m8b[:, 7:8], ALU.is_lt, NEG, ALU.mult)

                                def pe_part():
                                    tbk = tbank[0]
                                    S.transpose(tbk[:, 0, :], selb4[:, g, :], ident.v)
                                    S.copy("dve", sbT[:, g, :], tbk[:, 0, :])
                                defer(pe_part, 5)

                        return (qk_c, ex_c, pv_c, post_c)

                    def mk_branch(qb, g, kts, kT, ve, obank, tabs, goff, mask):
                        gp = g // 2
                        qv = qT[:, gp * 4:gp * 4 + 4, qb * 128:(qb + 1) * 128]
                        nsab = nsa2[qb % 2]
                        sbT = selbT2[qb % 2]
                        out = []
                        for idx, kt in enumerate(kts):
                            last = idx == len(kts) - 1

                            def qk(sb, kt=kt):
                                tab = (qb - kt) in tabs
                                S.matmul(sb.v, kT[:, g, kt * 128:(kt + 1) * 128], qv, start=True, stop=(not mask and not tab))
                                if mask:
                                    S.matmul(sb.v, expand[:, kt * 128:(kt + 1) * 128],
                                             sbT[:, g, :].map(lambda a: a.unsqueeze(1).to_broadcast([128, 4, 128])), start=False, stop=(not tab))
                                if tab:
                                    S.matmul(sb.v, ident.v, ewin[:, tabs[qb - kt], 4 * g:4 * g + 4, :], start=False, stop=True)

                            def ex(sb, pt, kt=kt):
                                S.act(pt.v, sb.v, AF.Exp)

                            def pv(pt, kt=kt, idx=idx, last=last):
                                for r in range(4):
                                    S.matmul(obank[:, r * 65:(r + 1) * 65], pt[:, r * 128:(r + 1) * 128], ve[:, kt, g, :],
                                             start=(idx == 0 and r == 0), stop=last, inc=(r == 3), skip_group_check=True)

                            def post():
                                o3 = obank.v.map(lambda a: a[:, 0:260].rearrange("p (r c) -> p r c", c=65))
                                S.recip(rl.v, o3[:, :, 64])
                                S.tt("dve", ccf.v, gsig[:, qb, goff + 4 * g:goff + 4 * g + 4], rl.v, ALU.mult)
                                S.tt("dve", tmpo.v, o3[:, :, 0:64], ccf.v.map(lambda a: a.unsqueeze(2).to_broadcast([128, 4, 64])), ALU.mult)
                                S.tt("dve", nsab[:, 4 * g:4 * g + 4, :], nsab[:, 4 * g:4 * g + 4, :], tmpo.v, ALU.add)

                            out.append((qk, ex, pv, post if last else None))
                        return out

                    def mk_asm(qb):
                        nsab = nsa2[qb % 2]

                        def assemble():
                            if l == 0:
                                dump("nsa%d" % qb, nsab.v, [128, 16, 64])
                            x = xt[qb % 2]
                            S.tt("dve", sgat.v, sgat.v, nsab.v.map(lambda a: a.rearrange("p h d -> p (h d)")), ALU.mult)
                            S.tt("dve", yb.v, sgat.v, yct.v, ALU.add)

                            def assemble_b():
                                tbk = tbank[1]
                                for c in range(8):
                                    S.transpose(tbk[:, c, :], yb[:, c * 128:(c + 1) * 128], ident.v, inc=(c == 7))
                                S.copy("act", yT.v, tbk.v)
                                for nb in range(2):
                                    bk = ocmp
                                    for c in range(8):
                                        S.matmul(bk.v, yT[:, c, :], wout[nb][:, c, :], start=(c == 0), stop=(c == 7))
                                    S.tt("dve", xo[:, nb * 512:(nb + 1) * 512], bk.v, x[:, nb * 512:(nb + 1) * 512], ALU.add)
                                S.dma("sp", xs[qb].v, xo.v)

                                def stats_and_prefetch():
                                    S.act(junk.v, xo.v, AF.Square, accum_out=ssnb[0][qb].v)
                                    if qb + 1 < NT:
                                        prefetch_asm(qb + 1)
                                defer(stats_and_prefetch, 3)
                            defer(assemble_b, 4)
                        return assemble

                    def prefetch_asm(qb):
                        S.dma("sp", sgat.v, SGA[qb].v)
                        S.dma("sp", yct.v, YC[qb].v)
                        S.dma("sp", xt[qb % 2].v, src_tiles[qb].v)

                    prefetch_asm(0)
                    S.memset("dve", ssn_hi, 0.0)

                    tiles = []
                    for g in range(4):
                        tiles.append(mk_cmp(0, g))
                    for qb in range(NT):
                        for g in range(4):
                            tiles += mk_branch(qb, g, [kt for kt in range(qb - 4, qb + 1) if kt >= 0], kwT, vwe, owin, {0: 0, 1: 1, 4: 2}, 32, False)
                            if qb + 1 < NT:
                                tiles.append(mk_cmp(qb + 1, g))
                            br = mk_branch(qb, g, list(range(qb + 1)), ksT, vse, osel, {0: 0, 1: 1}, 16, qb >= 8)
                            if g == 3:
                                qk_, ex_, pv_, post_ = br[-1]
                                asm = mk_asm(qb)

                                def post_and_asm(post_=post_, asm=asm):
                                    post_()
                                    asm()
                                br[-1] = (qk_, ex_, pv_, post_and_asm)
                            tiles += br

                    nt = len(tiles)
                    for i in range(nt + LA):
                        if i < nt:
                            tiles[i][0](SB[i % 3])
                        j = i - LA
                        if j >= 0:
                            cur["j"] = j
                            qk_, ex_, pv_, post_ = tiles[j]
                            pt = PT[j % 3]
                            ex_(SB[j % 3], pt)
                            pv_(pt)
                            if post_ is not None:
                                post_()
                            for fn in pending.pop(j, []):
                                fn()
                    while pending:
                        j = min(pending)
                        cur["j"] = j
                        for fn in pending.pop(j):
                            fn()
                    S.barrier(skip_pool=True)
                  if l == 0:
                      dump("x1", multi(xs, xs_ap), [SEQ, D])
                  S.barrier(skip_pool=True)

            if "xattn" in stages:
                with ExitStack() as esx:
                    hT = S.sbuf("hTx", [128, 8, SEQ], BF16, esx)
                    norm_T(hT, xs, "norm_x_g", l, pre=True)
                    q2T = S.sbuf("q2T", [128, 8, SEQ], BF16, esx)
                    kxT = S.sbuf("kxT", [128, 8, MEM], BF16, esx)
                    vxe = S.sbuf("vxe", [128, 2, 4, 257], BF16, esx)
                    PTx = [S.sbuf("PTx%d" % i, [128, 512], BF16, esx) for i in range(4)]
                    on = [S.sbuf("on%d" % i, [128, D], BF16, esx) for i in range(4)]
                    onT = S.sbuf("onT", [128, 8, 128], BF16, esx)
                    xo = S.sbuf("xox", [128, D], F32, esx)
                    rlx = S.sbuf("rlx", [128, 1], F32, esx)
                    for cb in range(2):
                        w = next_wb()
                        wload(w.v, I["xq_w"], l, 0, D, cb * 512, 512)
                        for c4 in range(4):
                            c = cb * 4 + c4

                            def lhs(k, w=w, c4=c4):
                                return w[:, k, c4 * 128:(c4 + 1) * 128]

                            def evq(tb, bk, c=c):
                                if ev_eng() == "act":
                                    S.act(q2T[:, c, tb * 512:(tb + 1) * 512], bk.v, AF.Copy, scale=0.0625)
                                else:
                                    S.ts("dve", q2T[:, c, tb * 512:(tb + 1) * 512], bk.v, 0.0625, ALU.mult)
                            fm_proj(hT, lhs, evq)
                    for cb in range(2):
                        w = next_wb()
                        wload(w.v, I["xkv_w"], l, 0, D, cb * 512, 512)
                        for c4 in range(4):
                            c = cb * 4 + c4
                            bk = next_bank()
                            for k in range(8):
                                S.matmul(bk[:, 0:MEM], w[:, k, c4 * 128:(c4 + 1) * 128], memT[:, k, :], start=(k == 0), stop=(k == 7))
                            S.copy(ev_eng(), kxT[:, c, :], bk[:, 0:MEM])
                    S.memset("dve", vxe[:, :, :, 256:257], 1.0)
                    for cb in range(2):
                        w = next_wb()
                        wload(w.v, I["xkv_w"], l, 0, D, D + cb * 512, 512)
                        for mt in range(2):
                            bk = next_bank()
                            for k in range(8):
                                S.matmul(bk.v, memT[:, k, mt * 128:(mt + 1) * 128], w[:, k, :], start=(k == 0), stop=(k == 7))
                            S.copy(ev_eng(), vxe[:, mt, 2 * cb:2 * cb + 2, 0:256], bk.v.map(lambda a: a.rearrange("p (h d) -> p h d", h=2)))
                    wxo = [next_wb(), next_wb()]
                    for nb in range(2):
                        wload(wxo[nb].v, I["xo_w"], l, 0, D, nb * 512, 512)
                    npt = 0
                    for tb in range(4):
                        for hh in range(4):
                            pts = []
                            for mt in range(2):
                                sb = next_bank()
                                for cch in range(2):
                                    S.matmul(sb.v, kxT[:, 2 * hh + cch, mt * 128:(mt + 1) * 128], q2T[:, 2 * hh + cch, tb * 512:(tb + 1) * 512],
                                             start=(cch == 0), stop=(cch == 1))
                                pt = PTx[npt % 4]
                                npt += 1
                                S.act(pt.v, sb.v, AF.Exp)
                                pts.append(pt)
                            for j in range(4):
                                ob = next_bank()
                                for mt in range(2):
                                    S.matmul(ob[:, 0:257], pts[mt][:, j * 128:(j + 1) * 128], vxe[:, mt, hh, :], start=(mt == 0), stop=(mt == 1))
                                S.recip(rlx.v, ob[:, 256:257])
                                S.ts("dve", on[j][:, hh * 256:(hh + 1) * 256], ob[:, 0:256], rlx.v, ALU.mult)
                        for j in range(4):
                            i = tb * 4 + j
                            tbk = tbank[j % 2]
                            for c in range(8):
                                S.transpose(tbk[:, c, :], on[j][:, c * 128:(c + 1) * 128], ident.v, inc=(c == 7))
                            S.copy("act", onT.v, tbk.v)
                            x = xt[st["xt"] % 2]
                            st["xt"] += 1
                            S.dma("sp", x.v, xs[i].v)
                            for nb in range(2):
                                bk = next_bank()
                                for c in range(8):
                                    S.matmul(bk.v, onT[:, c, :], wxo[nb][:, c, :], start=(c == 0), stop=(c == 7))
                                S.tt("dve", xo[:, nb * 512:(nb + 1) * 512], bk.v, x[:, nb * 512:(nb + 1) * 512], ALU.add)
                            S.act(junk.v, xo.v, AF.Square, accum_out=ssnb[0][i].v)
                            S.dma("sp", xs[i].v, xo.v)
                    S.barrier(skip_pool=True)
                if l == 0:
                    dump("x2", multi(xs, xs_ap), [SEQ, D])

            if "ffn" in stages:
                with ExitStack() as esf:
                    uT = S.sbuf("uT", [128, 22, SEQ], BF16, esf)
                    with ExitStack() as esf1:
                        hT = S.sbuf("hTf", [128, 8, SEQ], BF16, esf1)
                        norm_T(hT, xs, "norm_ffn_g", l, pre=True)
                        sa = [S.sbuf("sa%d" % i, [128, 512], F32, esf1) for i in range(2)]
                        for blk in range(6):
                            c0 = blk * 512
                            ncols = min(512, DFF - c0)
                            wa = next_wb()
                            wbb = next_wb()
                            wload(wa[:, :, 0:ncols], I["ffn_in_w"], l, 0, D, c0, ncols)
                            wload(wbb[:, :, 0:ncols], I["ffn_in_w"], l, 0, D, DFF + c0, ncols)
                            for c4 in range(ncols // 128):
                                c = blk * 4 + c4
                                for tb in range(4):
                                    ba = next_bank()
                                    bb = next_bank()
                                    for k in range(8):
                                        S.matmul(ba.v, wa[:, k, c4 * 128:(c4 + 1) * 128], hT[:, k, tb * 512:(tb + 1) * 512], start=(k == 0), stop=(k == 7))
                                    for k in range(8):
                                        S.matmul(bb.v, wbb[:, k, c4 * 128:(c4 + 1) * 128], hT[:, k, tb * 512:(tb + 1) * 512], start=(k == 0), stop=(k == 7))
                                    sv = sa[tb % 2]
                                    S.act(sv.v, ba.v, AF.Silu)
                                    S.tt("dve", uT[:, c, tb * 512:(tb + 1) * 512], sv.v, bb.v, ALU.mult)
                        S.barrier()
                    wfo = S.sbuf("wfo", [128, 22, 512], BF16, esf)
                    xh = [S.sbuf("xh%d" % i, [128, 512], F32, esf) for i in range(2)]
                    xoh = [S.sbuf("xoh%d" % i, [128, 512], F32, esf) for i in range(2)]
                    for nb in range(2):
                        for (c0, nch) in ((0, 8), (8, 8), (16, 6)):
                            wload(wfo[:, c0:c0 + nch, :], I["ffn_out_w"], l, c0 * 128, nch * 128, nb * 512, 512)
                        for i in range(NT):
                            bk = next_bank()
                            for c in range(22):
                                S.matmul(bk.v, uT[:, c, i * 128:(i + 1) * 128], wfo[:, c, :], start=(c == 0), stop=(c == 21))
                            x = xh[i % 2]
                            S.dma("sp", x.v, xs[i][:, nb * 512:(nb + 1) * 512])
                            xq = xoh[i % 2]
                            S.tt("dve", xq.v, bk.v, x.v, ALU.add)
                            S.act(junk[:, 0:512], xq.v, AF.Square, accum_out=ssnb[nb][i].v)
                            S.dma("sp", xs[i][:, nb * 512:(nb + 1) * 512], xq.v)
                    S.barrier(skip_pool=True)
                if l == 0:
                    dump("x3", multi(xs, xs_ap), [SEQ, D])

        out_toks = []
        with ExitStack() as esz:
            fo = [S.sbuf("fo%d" % i, [128, D], F32, esz) for i in range(2)]
            gb = gbuf[st["g"] % 2]
            st["g"] += 1
            bcast_row(gb.v, I["final_norm_g"], I["final_norm_g"].ap[0:1, :])
            fsrc = xs if n_layers > 0 else xin_t
            full = ("ffn" in stages) and n_layers > 0
            if full:
                batch_rstd()
            for i in range(NT):
                x = xt[st["xt"] % 2]
                sq = ss[st["xt"] % 2]
                rs = rstd[st["xt"] % 2]
                st["xt"] += 1
                S.dma("sp", x.v, fsrc[i].v)
                if full:
                    rs = rsn[:, i:i + 1]
                else:
                    S.act(junk.v, x.v, AF.Square, accum_out=sq.v)
                    S.ts("dve", rs.v, sq.v, 1.0 / D, ALU.mult, EPS, ALU.add)
                    S.act(rs.v, rs.v, AF.Sqrt)
                    S.recip(rs.v, rs.v)
                    rs = rs.v
                S.stt("dve", fo[i % 2].v, x.v, rs, gb.v, ALU.mult, ALU.mult)
                out_toks.append(S.dma("sp", out_t[i].v, fo[i % 2].v))
            S.wait_toks("sp", out_toks)
            S.barrier()
        S.wait_toks("sp", dbg_toks)
        S.emit()
        print("ops", {k: len(v.ops) for k, v in S.eng.items()}, "waits", S.nwaits, "dmas", S.ndma, flush=True)
    return nc, dbg_outs


_CONSTS = None


def make_in_map(inputs, b):
    global _CONSTS
    if _CONSTS is None:
        _CONSTS = host_consts()
    m = {}
    for name, shp in IN_SPECS:
        a = np.asarray(inputs[name])
        if name in ("x", "mem"):
            a = a[b]
        m[name] = np.ascontiguousarray(a, dtype=np.float32).reshape(shp)
    m.update(_CONSTS)
    return m


def kernel(**inputs):
    nc, _ = build()
    in_maps = [make_in_map(inputs, b) for b in range(8)]
    res = run_bass_kernel_spmd(nc, in_maps, core_ids=list(range(8)))
    return np.stack([np.asarray(r["out"], dtype=np.float32) for r in res.results], axis=0)
```

```python
import numpy as np
import ml_dtypes
from contextlib import ExitStack
import concourse.bass as bass
import concourse.mybir as mybir
from concourse.bass_utils import run_bass_kernel_spmd

F32 = mybir.dt.float32
BF16 = mybir.dt.bfloat16
ALU = mybir.AluOpType
AF = mybir.ActivationFunctionType
AX = mybir.AxisListType

SEQ = 2048
D = 1024
NT = 16
DEPTH = 4
NIN = 6704
DFF = 2816
MEM = 256
O_A, O_GT, O_Q, O_KC, O_VC, O_KS, O_VS, O_KW, O_VW, O_GH, O_GC, O_GA = (
    0, 1024, 2048, 3072, 3328, 3584, 3840, 4096, 4352, 4608, 4656, 5680)
EPS = 1e-6
NEG = -30000.0


class View:
    __slots__ = ("bufs", "ap")

    def __init__(self, bufs, ap):
        self.bufs = bufs
        self.ap = ap

    def __getitem__(self, idx):
        return View(self.bufs, self.ap[idx])

    def map(self, f):
        return View(self.bufs, f(self.ap))


class Buf:
    __slots__ = ("name", "ap", "lastw", "readers")

    def __init__(self, name, ap):
        self.name = name
        self.ap = ap
        self.lastw = None
        self.readers = []

    def __getitem__(self, idx):
        return View((self,), self.ap[idx])

    @property
    def v(self):
        return View((self,), self.ap)


def multi(bufs, ap):
    return View(tuple(bufs), ap)


class _Eng:
    def __init__(self, name):
        self.name = name
        self.key = name
        self.epoch = 0
        self.cnt = 0
        self.clock = {}
        self.ops = []
        self.pending = False


class Sched:
    NDMA = 12

    def __init__(self, nc, es):
        self.nc = nc
        self.es = es
        self.eng = {n: _Eng(n) for n in ("pe", "act", "dve", "pool", "sp")}
        self.sems = {}
        for n in self.eng:
            self.sems[n] = es.enter_context(nc.semaphore("s_" + n))
        self.nslots = {"sp": 8, "pool": 6, "act": 0}
        for q, n in self.nslots.items():
            for i in range(n):
                self.sems["d%s%d" % (q, i)] = es.enter_context(nc.semaphore("s_d%s%d" % (q, i)))
        self.ndma_q = {"sp": 0, "pool": 0}
        self.ndma = 0
        self.dlast = {}
        self.tclock = {}
        self.nwaits = 0
        self.nops = 0

    def sbuf(self, name, shape, dt, es=None):
        self.nops += 1
        name = "%s_%d" % (name, self.nops)
        t = (es or self.es).enter_context(self.nc.sbuf_tensor(name, list(shape), dt))
        return Buf(name, t[:])

    def psum(self, name, shape, dt, es=None):
        t = (es or self.es).enter_context(self.nc.psum_tensor(name, list(shape), dt))
        return Buf(name, t[:])

    def _need(self, E, tok, waits):
        s, v = tok
        if E.name == "pe" and s == E.key:
            return
        if E.clock.get(s, 0) >= v:
            return
        if waits.get(s, 0) < v:
            waits[s] = v

    def _apply(self, E, waits):
        for s, v in waits.items():
            tc = self.tclock.get((s, v))
            assert tc is not None, ("wait on unfinished token", s, v, E.name)
            ck = E.clock
            for k2, v2 in tc.items():
                if ck.get(k2, 0) < v2:
                    ck[k2] = v2
        self.nwaits += len(waits)
        return list(waits.items())

    def _deps(self, E, rb, wb):
        waits = {}
        for b in rb:
            if b.lastw is not None:
                self._need(E, b.lastw, waits)
        for b in wb:
            if b.lastw is not None:
                self._need(E, b.lastw, waits)
            for r in b.readers:
                self._need(E, r, waits)
        return self._apply(E, waits)

    def _mark(self, tok, rb, wb):
        for b in rb:
            if tok not in b.readers[-2:]:
                b.readers.append(tok)
        for b in wb:
            b.lastw = tok
            b.readers = []

    def op(self, ename, fn, reads=(), writes=(), inc=True):
        E = self.eng[ename]
        rb = [b for v in reads for b in v.bufs]
        wb = [b for v in writes for b in v.bufs]
        waits = self._deps(E, rb, wb)
        tok = (E.key, E.cnt + 1)
        if inc:
            E.cnt += 1
            c = dict(E.clock)
            c[E.key] = E.cnt
            self.tclock[tok] = c
            E.pending = False
        else:
            E.pending = True
        self._mark(tok, rb, wb)
        self.nops += 1
        E.ops.append((waits, fn, E.key if inc else None, 1))

    def dma(self, qname, out, in_, **kw):
        E = self.eng[qname]
        rb = list(in_.bufs)
        wb = list(out.bufs)
        i = self.ndma_q[qname]
        self.ndma_q[qname] += 1
        self.ndma += 1
        ns = self.nslots[qname]
        s = "d%s%d" % (qname, i % ns)
        val = 16 * (i // ns + 1)
        waits = {}
        for b in rb:
            if b.lastw is not None:
                self._need(E, b.lastw, waits)
        for b in wb:
            if b.lastw is not None:
                self._need(E, b.lastw, waits)
            for r in b.readers:
                self._need(E, r, waits)
        if val > 16:
            self._need(E, (s, val - 16), waits)
        wl = self._apply(E, waits)
        tok = (s, val)
        c = dict(E.clock)
        c[s] = val
        self.tclock[tok] = c
        self.dlast[s] = val
        self._mark(tok, rb, wb)
        oap, iap = out.ap, in_.ap
        E.ops.append((wl, lambda e: e.dma_start(out=oap, in_=iap, **kw), s, 16))
        return tok

    def wait_toks(self, ename, toks):
        E = self.eng[ename]
        waits = {}
        for t in toks:
            self._need(E, t, waits)
        wl = self._apply(E, waits)
        E.ops.append((wl, None, None, 0))

    def barrier(self, skip_pool=False):
        toks = [(E.key, E.cnt) for E in self.eng.values() if E.cnt > 0]
        toks += list(self.dlast.items())
        for n in self.eng:
            if skip_pool and n == "pool":
                continue
            self.wait_toks(n, toks)
        for E in self.eng.values():
            if E.cnt > 1500:
                E.epoch += 1
                E.key = "%s#%d" % (E.name, E.epoch)
                E.cnt = 0
                self.sems[E.key] = self.es.enter_context(self.nc.semaphore("s_%s_%d" % (E.name, E.epoch)))

    def matmul(self, out, lhsT, rhs, start=True, stop=True, inc=None, **kw):
        o, l, r = out.ap, lhsT.ap, rhs.ap
        if inc is None:
            inc = stop
        rd = [lhsT, rhs] + ([] if start else [out])
        self.op("pe", lambda e: e.matmul(o, l, r, start=start, stop=stop, **kw), rd, [out], inc=inc)

    def transpose(self, out, in_, ident, inc=True):
        o, i, d = out.ap, in_.ap, ident.ap
        self.op("pe", lambda e: e.transpose(o, i, d), [in_, ident], [out], inc=inc)

    def act(self, out, in_, func, bias=None, scale=None, accum_out=None):
        kw = {}
        rd = [in_]
        wr = [out]
        if bias is not None:
            if isinstance(bias, View):
                kw["bias"] = bias.ap
                rd.append(bias)
            else:
                kw["bias"] = bias
        if scale is not None:
            if isinstance(scale, View):
                kw["scale"] = scale.ap
                rd.append(scale)
            else:
                kw["scale"] = scale
        if accum_out is not None:
            kw["accum_out"] = accum_out.ap
            wr.append(accum_out)
        o, i = out.ap, in_.ap
        self.op("act", lambda e: e.activation(o, i, func, **kw), rd, wr)

    def tt(self, eng, out, in0, in1, op):
        o, a, b = out.ap, in0.ap, in1.ap
        self.op(eng, lambda e: e.tensor_tensor(o, a, b, op), [in0, in1], [out])

    def ts(self, eng, out, in0, s1, op0, s2=None, op1=None):
        rd = [in0]
        a1, a2 = s1, s2
        if isinstance(s1, View):
            rd.append(s1)
            a1 = s1.ap
        if isinstance(s2, View):
            rd.append(s2)
            a2 = s2.ap
        kw = {}
        if op1 is not None:
            kw["op1"] = op1
        o, i = out.ap, in0.ap
        self.op(eng, lambda e: e.tensor_scalar(o, i, a1, a2, op0, **kw), rd, [out])

    def stt(self, eng, out, in0, scalar, in1, op0, op1):
        rd = [in0, in1]
        sc = scalar
        if isinstance(scalar, View):
            rd.append(scalar)
            sc = scalar.ap
        o, a, b = out.ap, in0.ap, in1.ap
        self.op(eng, lambda e: e.scalar_tensor_tensor(o, a, sc, b, op0, op1), rd, [out])

    def copy(self, eng, out, in_):
        o, i = out.ap, in_.ap
        if eng == "act":
            self.op(eng, lambda e: e.copy(o, i), [in_], [out])
        else:
            self.op(eng, lambda e: e.tensor_copy(o, i), [in_], [out])

    def memset(self, eng, out, val):
        o = out.ap
        self.op(eng, lambda e: e.memset(o, val), [], [out])

    def reduce(self, eng, out, in_, op, axis=AX.X):
        o, i = out.ap, in_.ap
        self.op(eng, lambda e: e.tensor_reduce(o, i, axis, op), [in_], [out])

    def recip(self, out, in_):
        o, i = out.ap, in_.ap
        self.op("dve", lambda e: e.reciprocal(o, i), [in_], [out])

    def emit(self):
        nc = self.nc
        sems = self.sems
        for E in self.eng.values():
            assert not E.pending, E.name
        with nc.Block() as block:
            def mk(E):
                def body(eng):
                    for waits, fn, inc, amt in E.ops:
                        for s, v in waits:
                            eng.wait_ge(sems[s], v)
                        if fn is not None:
                            ins = fn(eng)
                            if inc is not None:
                                ins.then_inc(sems[inc], amt)
                return body
            block.tensor(mk(self.eng["pe"]))
            block.scalar(mk(self.eng["act"]))
            block.vector(mk(self.eng["dve"]))
            block.gpsimd(mk(self.eng["pool"]))
            block.sync(mk(self.eng["sp"]))


def _t5_bucket(n):
    n = np.maximum(n, 0)
    nf = np.maximum(n, 1).astype(np.float32)
    large = 16 + (np.log(nf / np.float32(16)) / np.float32(np.log(8.0)) * 16).astype(np.int32)
    large = np.minimum(large, 31)
    return np.where(n < 16, n, large)


def host_consts():
    bf = ml_dtypes.bfloat16
    c = {}
    c["c_ident"] = np.eye(128, dtype=np.float32).astype(bf)
    c["c_antij"] = np.eye(128, dtype=np.float32)[::-1].copy().astype(bf)
    i = np.arange(768)
    d = i - 127
    ok = (d >= 0) & (d < 512)
    oh = np.zeros((32, 768), np.float32)
    oh[_t5_bucket(d)[ok], i[ok]] = 1.0
    c["c_ohw"] = oh
    c["c_validw"] = np.broadcast_to(ok.astype(np.float32)[None], (16, 768)).copy()
    c["c_maskw"] = ((c["c_validw"] - 1.0) * 30000.0).astype(np.float32)
    dist = np.arange(128)
    oc = np.zeros((32, 128), np.float32)
    oc[_t5_bucket(dist), dist] = 1.0
    c["c_ohc"] = oc
    n = np.arange(128)[:, None]
    m = np.arange(32)[None, :]
    ov = ((16 * n < 64 * m + 64) & (16 * n + 32 > 64 * m)).astype(np.float32)
    ov[127] = 0
    c["c_ov"] = ov.astype(bf)
    ex = (np.arange(2048)[None, :] // 64 == np.arange(32)[:, None]).astype(np.float32)
    c["c_expand"] = np.tile(ex, (4, 1)).astype(bf)
    cand = np.zeros((128, 8, 32), np.float32)
    fval = np.zeros((128, 8, 32), np.float32)
    for qb in range(8, 16):
        t = qb * 128 + np.arange(128)
        blk = (t // 64)[:, None]
        mm = np.arange(32)[None, :]
        forced = (mm == 0) | (mm == blk) | (mm == blk - 1)
        valid = mm <= blk
        cand[:, qb - 8] = (valid & ~forced)
        fval[:, qb - 8] = np.where(forced, 100.0 + mm, np.where(valid, 0.0, -1.0))
    c["c_cand"] = cand
    c["c_fval"] = fval
    return c


CONST_SPECS = [("c_ident", [128, 128], BF16), ("c_antij", [128, 128], BF16), ("c_ohw", [32, 768], F32),
               ("c_validw", [16, 768], F32), ("c_maskw", [16, 768], F32), ("c_ohc", [32, 128], F32), ("c_ov", [128, 32], BF16),
               ("c_expand", [128, 2048], BF16), ("c_cand", [128, 8, 32], F32), ("c_fval", [128, 8, 32], F32)]

IN_SPECS = [("x", [SEQ, D]), ("mem", [MEM, D]), ("norm_mix_g", [DEPTH, D]), ("w_in", [DEPTH, D, NIN]),
            ("conv_dw_w", [DEPTH, 31, D]), ("conv_dw_b", [DEPTH, D]), ("conv_ln_g", [DEPTH, D]),
            ("conv_ln_b", [DEPTH, D]), ("conv_pw_w", [DEPTH, D, D]), ("cmp_pos", [DEPTH, 2, 32, 64]),
            ("cmp_w1", [DEPTH, 2, 32, 64, 64]), ("cmp_w2", [DEPTH, 2, 64, 64]), ("w_out", [DEPTH, D, D]),
            ("norm_x_g", [DEPTH, D]), ("xq_w", [DEPTH, D, D]), ("xkv_w", [DEPTH, D, 2 * D]),
            ("xo_w", [DEPTH, D, D]), ("norm_ffn_g", [DEPTH, D]), ("ffn_in_w", [DEPTH, D, 2 * DFF]),
            ("ffn_out_w", [DEPTH, DFF, D]), ("rel_bias", [32, 16]), ("final_norm_g", [1, D])]


def build(n_layers=DEPTH, stages=("mix", "xattn", "ffn"), dumps=()):
    nc = bass.Bass("TRN2", target_bir_lowering=False)
    I = {}
    for name, shp in IN_SPECS:
        I[name] = Buf(name, nc.dram_tensor(name, shp, F32, kind="ExternalInput").ap())
    for name, shp, dt in CONST_SPECS:
        I[name] = Buf(name, nc.dram_tensor(name, shp, dt, kind="ExternalInput").ap())
    OUT = nc.dram_tensor("out", [SEQ, D], F32, kind="ExternalOutput").ap()
    out_t = [Buf("out%d" % i, OUT[i * 128:(i + 1) * 128, :]) for i in range(NT)]

    def dram(name, shape, dt):
        return nc.dram_tensor(name, list(shape), dt, kind="Internal").ap()

    xs_ap = dram("xs", [SEQ, D], F32)
    xs = [Buf("xs%d" % i, xs_ap[i * 128:(i + 1) * 128, :]) for i in range(NT)]
    xin_t = [Buf("xin%d" % i, I["x"].ap[i * 128:(i + 1) * 128, :]) for i in range(NT)]
    yc_ap = dram("yc", [SEQ, D], F32)
    YC = [Buf("yc%d" % i, yc_ap[i * 128:(i + 1) * 128, :]) for i in range(NT)]
    sga_ap = dram("sga", [SEQ, D], F32)
    SGA = [Buf("sga%d" % i, sga_ap[i * 128:(i + 1) * 128, :]) for i in range(NT)]
    conv_ap = dram("convs", [8, 128, SEQ], F32)
    CONV = [[Buf("cv%d_%d" % (c, t), conv_ap[c, :, t * 512:(t + 1) * 512]) for t in range(4)] for c in range(8)]
    fw_ap = dram("fw", [16, 768], BF16)
    FW = Buf("fw", fw_ap)
    fc_ap = dram("fc", [16, 4096], BF16)
    FC = Buf("fc", fc_ap)
    ec_ap = dram("ecd", [16, 128, 16 * 128], BF16)
    ECD = [Buf("ecd%d" % q, ec_ap[q]) for q in range(16)]

    dbg_toks = []
    dbg_outs = []

    with ExitStack() as es:
        S = Sched(nc, es)

        def dump(name, view, shape, dt=F32):
            if name not in dumps:
                return
            t = nc.dram_tensor("dbg_" + name, list(shape), dt, kind="ExternalOutput").ap()
            dbg_outs.append("dbg_" + name)
            dbg_toks.append(S.dma("sp", Buf("dbg_" + name, t).v, view))

        ident = S.sbuf("ident", [128, 128], BF16)
        antij = S.sbuf("antij", [128, 128], BF16)
        ones = S.sbuf("ones", [128, 128], BF16)
        expand = S.sbuf("expand", [128, 2048], BF16)
        ewin = S.sbuf("ewin", [128, 3, 16, 128], BF16)
        cand = S.sbuf("cand", [128, 8, 32], F32)
        fval = S.sbuf("fval", [128, 8, 32], F32)
        dwT = S.sbuf("dwT", [128, DEPTH, 8, 31], F32)
        dwb = S.sbuf("dwb", [128, DEPTH, 8], F32)
        lng = S.sbuf("lng", [128, DEPTH, 8], F32)
        lnb = S.sbuf("lnb", [128, DEPTH, 8], F32)
        memT = S.sbuf("memT", [128, 8, MEM], BF16)
        gbuf = [S.sbuf("gbuf0", [128, D], F32)] * 2
        xt = [S.sbuf("xt%d" % i, [128, D], F32) for i in range(2)]
        hb = S.sbuf("hb", [128, D], BF16)
        hb2 = [hb, S.sbuf("hbb", [128, D], BF16)]
        junk = S.sbuf("junk", [128, D], BF16)
        ss = [S.sbuf("ss%d" % i, [128, 1], F32) for i in range(2)]
        rstd = [S.sbuf("rstd%d" % i, [128, 1], F32) for i in range(2)]
        ssn_t = S.sbuf("ssn", [128, 2, NT], F32)
        ssnb = [[Buf("ssn%d_%d" % (h_, i_), ssn_t.ap[:, h_, i_:i_ + 1]) for i_ in range(NT)] for h_ in range(2)]
        ssn_all = multi([b_ for r_ in ssnb for b_ in r_], ssn_t.ap)
        ssn_hi = multi(ssnb[1], ssn_t.ap[:, 1, :])
        rsn = S.sbuf("rsn", [128, NT], F32)
        NWB = 4
        wb = [S.sbuf("wb%d" % i, [128, 8, 512], BF16) for i in range(NWB)]
        banks = [S.psum("bk%d" % i, [128, 512], F32) for i in range(6)]
        tbank = [S.psum("tb%d" % i, [128, 8, 128], BF16) for i in range(2)]
        st = {"wb": 0, "bk": 0, "ev": 0, "g": 0, "xt": 0}

        st["nwb"] = NWB

        def next_wb():
            b = wb[st["wb"] % st["nwb"]]
            st["wb"] += 1
            return b

        def next_bank():
            b = banks[st["bk"] % 6]
            st["bk"] += 1
            return b

        def ev_eng():
            st["ev"] += 1
            return "act" if st["ev"] % 2 else "dve"

        def wload(dst, W, l, r0, nr, c0, ncols):
            src = W.ap[l, r0:r0 + nr, c0:c0 + ncols].rearrange("(c p) n -> p c n", p=128)
            S.dma("pool", dst, View((W,), src))

        def bcast_row(dst, src_buf, row_ap):
            S.dma("sp", dst, View((src_buf,), row_ap.partition_broadcast(128)))

        S.dma("sp", ident.v, I["c_ident"].v)
        S.dma("sp", antij.v, I["c_antij"].v)
        S.dma("sp", expand.v, I["c_expand"].v)
        S.dma("sp", cand.v, I["c_cand"].v)
        S.dma("sp", fval.v, I["c_fval"].v)
        S.memset("dve", ones.v, 1.0)
        for l_ in range(DEPTH):
            for c_ in range(8):
                S.dma("sp", dwT[:, l_, c_, :], I["conv_dw_w"].v.map(lambda a: a[l_, :, c_ * 128:(c_ + 1) * 128].rearrange("k p -> p k")),
                      allow_slow_non_contiguous=True)
            for dst, nm in ((dwb, "conv_dw_b"), (lng, "conv_ln_g"), (lnb, "conv_ln_b")):
                S.dma("sp", dst[:, l_, :], I[nm].v.map(lambda a: a[l_].rearrange("(c p) -> p c", p=128)),
                      allow_slow_non_contiguous=True)

        with ExitStack() as es0:
            rbp = S.sbuf("rbp", [32, 16], F32, es0)
            ohw = S.sbuf("ohw", [32, 768], F32, es0)
            ohc = S.sbuf("ohc", [32, 128], F32, es0)
            validw = S.sbuf("validw", [16, 768], F32, es0)
            maskw = S.sbuf("maskw", [16, 768], F32, es0)
            neg31 = S.sbuf("neg31", [16, 1], F32, es0)
            fwf = S.sbuf("fwf", [16, 768], F32, es0)
            fwb = S.sbuf("fwb", [16, 768], BF16, es0)
            fcb = S.sbuf("fcb", [16, 4096], BF16, es0)
            tmph = S.sbuf("tmph", [128, 16, 128], BF16, es0)
            tmpe = [S.sbuf("tmpe%d" % i, [128, 16, 128], BF16, es0) for i in range(2)]
            memf = S.sbuf("memf", [128, D], F32, es0)
            S.dma("sp", rbp.v, I["rel_bias"].v)
            S.dma("sp", ohw.v, I["c_ohw"].v)
            S.dma("sp", ohc.v, I["c_ohc"].v)
            S.dma("sp", validw.v, I["c_validw"].v)
            S.dma("sp", maskw.v, I["c_maskw"].v)
            S.dma("sp", neg31.v, I["rel_bias"].v.map(lambda a: a[31:32, :].rearrange("o h -> h o")),
                  allow_slow_non_contiguous=True)
            S.ts("dve", neg31.v, neg31.v, -1.0, ALU.mult)
            b0, b1, b2 = banks[0], banks[1], banks[2]
            S.matmul(b0[0:16, 0:384], rbp.v, ohw[:, 0:384])
            S.matmul(b1[0:16, 0:384], rbp.v, ohw[:, 384:768])
            S.matmul(b2[0:16, 0:128], rbp.v, ohc.v)
            S.act(fwf[:, 0:384], b0[0:16, 0:384], AF.Identity, bias=neg31.v)
            S.act(fwf[:, 384:768], b1[0:16, 0:384], AF.Identity, bias=neg31.v)
            S.tt("dve", fwf.v, fwf.v, validw.v, ALU.mult)
            S.tt("dve", fwb.v, fwf.v, maskw.v, ALU.add)
            S.memset("dve", fcb[:, 0:2063], NEG)
            S.memset("dve", fcb[:, 2063 + 128:4096], 0.0)
            S.act(fcb[:, 2063:2063 + 128], b2[0:16, 0:128], AF.Identity, bias=neg31.v)
            S.dma("sp", FW.v, fwb.v)
            S.dma("sp", FC.v, fcb.v)
            dump("fw", fwf.v, [16, 768])
            for ti, delta in enumerate((0, 128, 512)):
                src = bass.AP(tensor=fw_ap.tensor, offset=delta, ap=[[1, 128], [768, 16], [1, 128]])
                S.dma("sp", tmph.v, View((FW,), src))
                for g in range(4):
                    bk = next_bank()
                    S.matmul(bk.v, antij.v, tmph[:, 4 * g:4 * g + 4, :])
                    S.copy(ev_eng(), ewin[:, ti, 4 * g:4 * g + 4, :],
                           bk.v.map(lambda a: a.rearrange("p (r q) -> p r q", r=4)))
            for qb in range(16):
                src = bass.AP(tensor=fc_ap.tensor, offset=128 * qb, ap=[[16, 128], [4096, 16], [1, 128]])
                S.dma("sp", tmph.v, View((FC,), src))
                te = tmpe[qb % 2]
                for g in range(4):
                    bk = next_bank()
                    S.matmul(bk.v, antij.v, tmph[:, 4 * g:4 * g + 4, :])
                    S.copy(ev_eng(), te[:, 4 * g:4 * g + 4, :],
                           bk.v.map(lambda a: a.rearrange("p (r q) -> p r q", r=4)))
                S.dma("sp", ECD[qb].v, te.v.map(lambda a: a.rearrange("p h q -> p (h q)")))
            for mt in range(2):
                S.dma("sp", memf.v, I["mem"][mt * 128:(mt + 1) * 128, :])
                S.copy("dve", hb.v, memf.v)
                tb = tbank[mt % 2]
                for c in range(8):
                    S.transpose(tb[:, c, :], hb[:, c * 128:(c + 1) * 128], ident.v, inc=(c == 7))
                S.copy("act", memT[:, :, mt * 128:(mt + 1) * 128], tb.v)
            S.barrier(skip_pool=True)
        dump("ewin", ewin.v, [128, 3, 16, 128], BF16)

        def batch_rstd():
            S.tt("dve", rsn.v, ssn_all.map(lambda a: a[:, 0, :]), ssn_all.map(lambda a: a[:, 1, :]), ALU.add)
            S.ts("dve", rsn.v, rsn.v, 1.0 / D, ALU.mult, EPS, ALU.add)
            S.act(rsn.v, rsn.v, AF.Sqrt)
            S.recip(rsn.v, rsn.v)

        def norm_T(hT, src_tiles, gname, l, pre=False):
            gb = gbuf[st["g"] % 2]
            st["g"] += 1
            if l is None:
                bcast_row(gb.v, I[gname], I[gname].ap[0:1, :])
            else:
                bcast_row(gb.v, I[gname], I[gname].ap[l:l + 1, :])
            if pre:
                batch_rstd()
                for i in range(NT):
                    x = xt[st["xt"] % 2]
                    st["xt"] += 1
                    S.dma("sp", x.v, src_tiles[i].v)
                    hbi = hb2[i % 2]
                    S.stt("dve", hbi.v, x.v, rsn[:, i:i + 1], gb.v, ALU.mult, ALU.mult)
                    tb = tbank[i % 2]
                    for c in range(8):
                        S.transpose(tb[:, c, :], hbi[:, c * 128:(c + 1) * 128], ident.v, inc=(c == 7))
                    S.copy(ev_eng(), hT[:, :, i * 128:(i + 1) * 128], tb.v)
                return
            for i in range(NT):
                x = xt[st["xt"] % 2]
                sq = ss[st["xt"] % 2]
                rs = rstd[st["xt"] % 2]
                st["xt"] += 1
                S.dma("sp", x.v, src_tiles[i].v)
                S.act(junk.v, x.v, AF.Square, accum_out=sq.v)
                S.ts("dve", rs.v, sq.v, 1.0 / D, ALU.mult, EPS, ALU.add)
                S.act(rs.v, rs.v, AF.Sqrt)
                S.recip(rs.v, rs.v)
                S.stt("dve", hb.v, x.v, rs.v, gb.v, ALU.mult, ALU.mult)
                tb = tbank[i % 2]
                for c in range(8):
                    S.transpose(tb[:, c, :], hb[:, c * 128:(c + 1) * 128], ident.v, inc=(c == 7))
                S.copy(ev_eng(), hT[:, :, i * 128:(i + 1) * 128], tb.v)

        def fm_proj(hT, lhs_of_k, evac, nk=8):
            for tb in range(4):
                bk = next_bank()
                for k in range(nk):
                    S.matmul(bk.v, lhs_of_k(k), hT[:, k, tb * 512:(tb + 1) * 512], start=(k == 0), stop=(k == nk - 1))
                evac(tb, bk)

        for l in range(n_layers):
            src_tiles = xin_t if l == 0 else xs
            if "mix" in stages:
                with ExitStack() as esm:
                  if True:
                    esh = esm
                    arena = S.sbuf("arena", [128, 8 * SEQ], BF16, esh)
                    hT = Buf("hT%d" % l, arena.ap.rearrange("p (c t) -> p c t", c=8))
                    norm_T(hT, src_tiles, "norm_mix_g", l, pre=(l > 0))
                    if l == 0:
                        dump("hT", hT.v, [128, 8, SEQ], BF16)
                    with ExitStack() as esa:
                        vTc = [S.sbuf("vTc%d" % i, [128, 30 + SEQ], BF16, esa) for i in range(2)]
                        diag = [S.sbuf("diag%d" % i, [128, 31, 128], BF16, esa) for i in range(2)]
                        sgs = [S.sbuf("sgs%d" % i, [128, 512], F32, esa) for i in range(2)]
                        cvo = [S.sbuf("cvo%d" % i, [128, 512], F32, esa) for i in range(2)]
                        for i in range(2):
                            S.memset("dve", vTc[i][:, 0:30], 0.0)
                        for cc in range(8):
                            w = next_wb()
                            wload(w[:, :, 0:128], I["w_in"], l, 0, D, O_A + cc * 128, 128)
                            wload(w[:, :, 128:256], I["w_in"], l, 0, D, O_GT + cc * 128, 128)
                            vt = vTc[cc % 2]
                            dg = diag[cc % 2]
                            for tb in range(4):
                                ba = next_bank()
                                bg = next_bank()
                                for k in range(8):
                                    S.matmul(ba.v, w[:, k, 0:128], hT[:, k, tb * 512:(tb + 1) * 512], start=(k == 0), stop=(k == 7))
                                for k in range(8):
                                    S.matmul(bg.v, w[:, k, 128:256], hT[:, k, tb * 512:(tb + 1) * 512], start=(k == 0), stop=(k == 7))
                                sg = sgs[tb % 2]
                                S.act(sg.v, bg.v, AF.Sigmoid)
                                S.tt("dve", vt[:, 30 + tb * 512:30 + (tb + 1) * 512], ba.v, sg.v, ALU.mult)
                            S.tt("dve", dg.v,
                                 ident.v.map(lambda a: a.unsqueeze(1).to_broadcast([128, 31, 128])),
                                 dwT[:, l, cc, :].map(lambda a: a.unsqueeze(2).to_broadcast([128, 31, 128])),
                                 ALU.mult)
                            for tb in range(4):
                                bk = next_bank()
                                for k in range(31):
                                    S.matmul(bk.v, dg[:, k, :], vt[:, tb * 512 + k:tb * 512 + k + 512], start=(k == 0), stop=(k == 30))
                                co = cvo[tb % 2]
                                S.act(co.v, bk.v, AF.Identity, bias=dwb[:, l, cc:cc + 1])
                                S.dma("sp", CONV[cc][tb].v, co.v)
                        S.barrier(skip_pool=True)
                    with ExitStack() as esb:
                        cvall2 = [S.sbuf("cvall%d" % i, [128, 8, 512], F32, esb) for i in range(2)]
                        cbf2 = [S.sbuf("cbf%d" % i, [128, 8, 512], BF16, esb) for i in range(2)]
                        sqb2 = [S.sbuf("sqb%d" % i, [128, 8, 512], BF16, esb) for i in range(2)]
                        zT2 = [S.sbuf("zT%d" % i, [128, 8, 512], BF16, esb) for i in range(2)]
                        mean = S.sbuf("mean", [128, 512], F32, esb)
                        msq = S.sbuf("msq", [128, 512], F32, esb)
                        rsd = S.sbuf("rsd", [128, 512], F32, esb)
                        sgc = [S.sbuf("sgc%d" % i, [128, 512], F32, esb) for i in range(2)]
                        yco = [S.sbuf("yco%d" % i, [128, 512], F32, esb) for i in range(2)]
                        wpw = [next_wb(), next_wb()]
                        wgc = [next_wb(), next_wb()]
                        for nb in range(2):
                            wload(wpw[nb].v, I["conv_pw_w"], l, 0, D, nb * 512, 512)
                            wload(wgc[nb].v, I["w_in"], l, 0, D, O_GC + nb * 512, 512)

                        def prep_a(tb):
                            cvall, cbf, sqb = cvall2[tb % 2], cbf2[tb % 2], sqb2[tb % 2]
                            bufs = [CONV[c][tb] for c in range(8)]
                            S.dma("sp", cvall.v, multi(bufs, conv_ap[:, :, tb * 512:(tb + 1) * 512].rearrange("c p t -> p c t")))
                            S.copy("dve", cbf.v, cvall.v)
                            S.act(sqb.v, cvall.v, AF.Square)

                        def prep_b(tb):
                            cvall, cbf, sqb, zT = cvall2[tb % 2], cbf2[tb % 2], sqb2[tb % 2], zT2[tb % 2]
                            b1_ = next_bank()
                            b2_ = next_bank()
                            for c in range(8):
                                S.matmul(b1_.v, ones.v, cbf[:, c, :], start=(c == 0), stop=(c == 7))
                            for c in range(8):
                                S.matmul(b2_.v, ones.v, sqb[:, c, :], start=(c == 0), stop=(c == 7))
                            S.act(mean.v, b1_.v, AF.Copy, scale=1.0 / D)
                            S.tt("dve", msq.v, mean.v, mean.v, ALU.mult)
                            S.stt("dve", rsd.v, b2_.v, 1.0 / D, msq.v, ALU.mult, ALU.subtract)
                            S.ts("dve", rsd.v, rsd.v, EPS, ALU.add)
                            S.act(rsd.v, rsd.v, AF.Sqrt)
                            S.recip(rsd.v, rsd.v)
                            S.tt("dve", cvall.v, cvall.v, mean.v.map(lambda a: a.unsqueeze(1).to_broadcast([128, 8, 512])), ALU.subtract)
                            S.tt("dve", cvall.v, cvall.v, rsd.v.map(lambda a: a.unsqueeze(1).to_broadcast([128, 8, 512])), ALU.mult)
                            for c in range(8):
                                S.act(zT[:, c, :], cvall[:, c, :], AF.Silu, scale=lng[:, l, c:c + 1], bias=lnb[:, l, c:c + 1])

                        def mm_tile(tb, j):
                            zT = zT2[tb % 2]
                            i = tb * 4 + j
                            for nb in range(2):
                                bc = next_bank()
                                bg = next_bank()
                                for c in range(8):
                                    S.matmul(bc.v, zT[:, c, j * 128:(j + 1) * 128], wpw[nb][:, c, :], start=(c == 0), stop=(c == 7))
                                for k in range(8):
                                    S.matmul(bg.v, hT[:, k, i * 128:(i + 1) * 128], wgc[nb][:, k, :], start=(k == 0), stop=(k == 7))
                                sg = sgc[nb]
                                yo = yco[nb]
                                S.act(sg.v, bg.v, AF.Sigmoid)
                                S.tt("dve", yo.v, bc.v, sg.v, ALU.mult)
                                S.dma("sp", YC[i][:, nb * 512:(nb + 1) * 512], yo.v)

                        prep_a(0)
                        prep_b(0)
                        for tb in range(4):
                            if tb + 1 < 4:
                                prep_a(tb + 1)
                            mm_tile(tb, 0)
                            if tb + 1 < 4:
                                prep_b(tb + 1)
                            for j in range(1, 4):
                                mm_tile(tb, j)
                        S.barrier(skip_pool=True)
                    if l == 0:
                        dump("yc", multi(YC, yc_ap), [SEQ, D])
                    esn = esm
                    qT = S.sbuf("qT", [128, 8, SEQ], BF16, esn)
                    kwT = S.sbuf("kwT", [128, 4, SEQ], BF16, esn)
                    ksT = S.sbuf("ksT", [128, 4, SEQ], BF16, esn)
                    S.memset("dve", kwT.v, 0.0)
                    S.memset("dve", ksT.v, 0.0)
                    vse = S.sbuf("vse", [128, NT, 4, 65], BF16, esn)
                    vwe = S.sbuf("vwe", [128, NT, 4, 65], BF16, esn)
                    gsig = S.sbuf("gsig", [128, NT, 48], F32, esn)
                    kcmpT = S.sbuf("kcmpT", [128, 4, 128], BF16, esn)
                    vce = S.sbuf("vce", [128, 4, 97], BF16, esn)
                    with ExitStack() as esp:
                        st["nwb"] = 2
                        kcT = Buf("kcT%d" % l, wb[2].ap.rearrange("p c t -> p (c t)").rearrange("p (a t) -> p a t", a=2))
                        vcT = Buf("vcT%d" % l, wb[3].ap.rearrange("p c t -> p (c t)").rearrange("p (a t) -> p a t", a=2))
                        for gp in range(2):
                            w = next_wb()
                            for e_ in range(2):
                                for r_ in range(4):
                                    wload(w[:, :, r_ * 128 + e_ * 64:r_ * 128 + e_ * 64 + 64], I["w_in"], l, 0, D,
                                          O_Q + gp * 512 + e_ * 256 + r_ * 64, 64)
                            for r in range(4):
                                cidx = gp * 4 + r

                                def lhs(k, w=w, r=r):
                                    return w[:, k, r * 128:(r + 1) * 128]

                                def evq(tb, bk, cidx=cidx):
                                    if ev_eng() == "act":
                                        S.act(qT[:, cidx, tb * 512:(tb + 1) * 512], bk.v, AF.Copy, scale=0.125)
                                    else:
                                        S.ts("dve", qT[:, cidx, tb * 512:(tb + 1) * 512], bk.v, 0.125, ALU.mult)
                                fm_proj(hT, lhs, evq)
                        w = next_wb()
                        wload(w.v, I["w_in"], l, 0, D, O_KC, 512)
                        w2_ = next_wb()
                        wload(w2_[:, :, 0:256], I["w_in"], l, 0, D, O_KS, 256)
                        wload(w2_[:, :, 256:512], I["w_in"], l, 0, D, O_KW, 256)
                        for (wt, c0, dst) in ((w, 0, kcT), (w, 256, vcT), (w2_, 0, ksT), (w2_, 256, kwT)):
                            for gp in range(2):
                                def lhs(k, wt=wt, c0=c0, gp=gp):
                                    return wt[:, k, c0 + gp * 128:c0 + (gp + 1) * 128]

                                def evk(tb, bk, dst=dst, gp=gp):
                                    if dst is kwT or dst is ksT:
                                        en_ = ev_eng()
                                        S.copy(en_, dst[0:64, 2 * gp, tb * 512:(tb + 1) * 512], bk[0:64, :])
                                        S.copy(en_, dst[64:128, 2 * gp + 1, tb * 512:(tb + 1) * 512], bk[64:128, :])
                                    else:
                                        S.copy(ev_eng(), dst[:, gp, tb * 512:(tb + 1) * 512], bk.v)
                                fm_proj(hT, lhs, evk)
                        wtm = [next_wb(), next_wb()]
                        wload(wtm[0][:, :, 0:256], I["w_in"], l, 0, D, O_VS, 256)
                        wload(wtm[1][:, :, 0:304], I["w_in"], l, 0, D, O_VW, 304)
                        S.memset("dve", vse[:, :, :, 64:65], 1.0)
                        S.memset("dve", vwe[:, :, :, 64:65], 1.0)
                        for i in range(NT):
                            ba = next_bank()
                            bb = next_bank()
                            for k in range(8):
                                S.matmul(ba[:, 0:256], hT[:, k, i * 128:(i + 1) * 128], wtm[0][:, k, 0:256], start=(k == 0), stop=(k == 7))
                            for k in range(8):
                                S.matmul(bb[:, 0:304], hT[:, k, i * 128:(i + 1) * 128], wtm[1][:, k, 0:304], start=(k == 0), stop=(k == 7))
                            S.copy("dve", vse[:, i, :, 0:64], ba[:, 0:256].map(lambda a: a.rearrange("p (g d) -> p g d", g=4)))
                            S.copy("act", vwe[:, i, :, 0:64], bb[:, 0:256].map(lambda a: a.rearrange("p (g d) -> p g d", g=4)))
                            S.act(gsig[:, i, :], bb[:, 256:304], AF.Sigmoid)
                        with ExitStack() as esg:
                            sgt = [S.sbuf("sgt%d" % i, [128, 512], F32, esg) for i in range(2)]
                            for nb in range(2):
                                w = next_wb()
                                wload(w.v, I["w_in"], l, 0, D, O_GA + nb * 512, 512)
                                for i in range(NT):
                                    bk = next_bank()
                                    for k in range(8):
                                        S.matmul(bk.v, hT[:, k, i * 128:(i + 1) * 128], w[:, k, :], start=(k == 0), stop=(k == 7))
                                    sg = sgt[i % 2]
                                    S.act(sg.v, bk.v, AF.Sigmoid)
                                    S.dma("sp", SGA[i][:, nb * 512:(nb + 1) * 512], sg.v)
                            S.barrier()
                        with ExitStack() as esc:
                            w1d = S.sbuf("w1d", [128, 2, 32, 64], BF16, esc)
                            w2d = S.sbuf("w2d", [64, 2, 128], BF16, esc)
                            posd = S.sbuf("posd", [128, 2, 32], BF16, esc)
                            cbias = S.sbuf("cbias", [64, 2], F32, esc)
                            sz = S.sbuf("sz", [64, 128], BF16, esc)
                            for hf in range(2):
                                for j_ in range(2):
                                    S.dma("pool", w1d[hf * 64:(hf + 1) * 64, j_], I["cmp_w1"].v.map(lambda a: a[l, j_].rearrange("t d e -> d t e")))
                                S.dma("pool", w2d[:, :, hf * 64:(hf + 1) * 64], I["cmp_w2"].v.map(lambda a: a[l].rearrange("j e f -> e j f")))
                            for j_ in range(2):
                                for hf in range(2):
                                    S.dma("pool", posd[hf * 64:(hf + 1) * 64, j_, :], I["cmp_pos"].v.map(lambda a: a[l, j_].rearrange("t d -> d t")), allow_slow_non_contiguous=True)
                            S.memset("dve", kcmpT.v, 0.0)
                            S.memset("dve", vce.v, 0.0)
                            S.memset("dve", vce[:, :, 64:65], 1.0)
                            S.memset("dve", sz.v, 0.0)
                            for g in range(4):
                                S.dma("sp", vce[:, g, 65:97], I["c_ov"].v)
                            dump("w2d", w2d.v, [64, 2, 128], BF16)
                            dump("w1d", w1d.v, [128, 2, 32, 64], BF16)
                            dump("kcT", kcT.v, [128, 2, SEQ], BF16)
                            for j, srcT in ((0, kcT), (1, vcT)):
                                for g in range(4):
                                    gp, e = divmod(g, 2)
                                    hs = slice(e * 64, e * 64 + 64)
                                    bk = next_bank()
                                    for t in range(32):
                                        S.matmul(bk[0:64, 0:127], w1d[hs, j, t, :], srcT[hs, gp, t:t + 16 * 126 + 1:16], start=(t == 0), stop=False)
                                        S.matmul(bk[0:64, 0:127], w1d[hs, j, t, :], posd[hs, j, t:t + 1].map(lambda a: a.to_broadcast([64, 127])), start=False, stop=(t == 31))
                                    S.act(sz[:, 0:127], bk[0:64, 0:127], AF.Silu)
                                    bo = next_bank()
                                    if j == 0:
                                        if g == 0:
                                            dump("sz0", sz.v, [64, 128], BF16)
                                        S.matmul(bo[:, 0:127], w2d[:, 0, :], sz[:, 0:127])
                                        S.copy("dve", kcmpT[hs, g, 0:127], bo[hs, 0:127])
                                    else:
                                        S.matmul(bo[0:127, 0:64], sz[:, 0:127], w2d[:, 1, 0:64])
                                        S.copy("dve", vce[0:127, g, 0:64], bo[0:127, 0:64])
                            S.barrier()
                        S.barrier()
                        st["nwb"] = NWB
                    if l == 0:
                        dump("qT", qT.v, [128, 8, SEQ], BF16)
                        dump("kwT", kwT.v, [128, 4, SEQ], BF16)
                        dump("vwe", vwe.v, [128, NT, 4, 65], BF16)
                        dump("gsig", gsig.v, [128, NT, 48])
                        dump("kcmpT", kcmpT.v, [128, 4, 128], BF16)
                        dump("vce", vce.v, [128, 4, 97], BF16)
                    S.barrier()

                  with ExitStack() as esw:
                    ar = {"off": 0}

                    def take(name, shape, dt):
                        n = 1
                        for d_ in shape[1:]:
                            n *= d_
                        nb = n * (4 if dt == F32 else 2)
                        nb = (nb + 31) // 32 * 32
                        o0 = ar["off"]
                        ar["off"] += nb // 2
                        assert ar["off"] <= 8 * SEQ
                        ap = arena.ap[:, o0:o0 + nb // 2]
                        if dt == F32:
                            ap = ap.bitcast(F32)[:, 0:n]
                        else:
                            ap = ap[:, 0:n]
                        if len(shape) == 3:
                            ap = ap.rearrange("p (a b) -> p a b", a=shape[1])
                        elif len(shape) == 4:
                            ap = ap.rearrange("p (a b c) -> p a b c", a=shape[1], b=shape[2])
                        return Buf("%s_%d" % (name, l), ap)

                    PT = [take("PT%d" % i, [128, 512], BF16) for i in range(3)]
                    ect = [take("ect0", [128, 16 * 128], BF16)] * 2
                    nsa = take("nsa", [128, 16, 64], F32)
                    tmpo = take("tmpo", [128, 4, 64], F32)
                    rl = take("rl", [128, 4], F32)
                    ccf = take("ccf", [128, 4], F32)
                    impn = take("impn", [128, 4, 32], F32)
                    imp = take("imp", [128, 32], F32)
                    score = take("score", [128, 32], F32)
                    work = take("work", [128, 32], F32)
                    m8a = take("m8a", [128, 8], F32)
                    m8b = take("m8b", [128, 8], F32)
                    selb4 = take("selb4", [128, 4, 128], BF16)
                    selbT = take("selbT", [128, 4, 128], BF16)
                    sgat = take("sgat", [128, D], F32)
                    yct = take("yct", [128, D], F32)
                    S.memset("dve", selb4.v, 0.0)
                    yb = hb
                    yT = take("yT", [128, 8, 128], BF16)
                    xo = yct
                    wout = [next_wb(), next_wb()]
                    for nb in range(2):
                        wload(wout[nb].v, I["w_out"], l, 0, D, nb * 512, 512)
                    SB = [banks[0], banks[1], banks[5]]
                    osel, owin, ocmp = banks[2], banks[3], banks[4]
                    LA = 2

                    def v4(view):
                        return view.map(lambda a: a.rearrange("p (r q) -> p r q", r=4))

                    nsa2 = [nsa, S.sbuf("nsa2", [128, 16, 64], F32, esw)]
                    ect = [ect[0], S.sbuf("ect1", [128, 16 * 128], BF16, esw)]
                    selbT2 = [selbT, S.sbuf("selbT2", [128, 4, 128], BF16, esw)]
                    rlc = S.sbuf("rlc", [128, 4], F32, esw)
                    ccc = S.sbuf("ccc", [128, 4], F32, esw)
                    pending = {}
                    cur = {"j": 0}

                    def defer(fn, k):
                        pending.setdefault(cur["j"] + k, []).append(fn)

                    def mk_cmp(qb, g):
                        gp = g // 2
                        ec = ect[qb % 2]
                        qv = qT[:, gp * 4:gp * 4 + 4, qb * 128:(qb + 1) * 128]
                        nsab = nsa2[qb % 2]
                        sbT = selbT2[qb % 2]

                        def qk_c(sb):
                            if g == 0:
                                S.dma("sp", ec.v, ECD[qb].v)
                            S.matmul(sb.v, kcmpT[:, g, :], qv, start=True, stop=False)
                            S.matmul(sb.v, ident.v, ec[:, g * 512:(g + 1) * 512], start=False, stop=True)

                        def ex_c(sb, pt):
                            S.act(pt.v, sb.v, AF.Exp)

                        def pv_c(pt):
                            for r in range(4):
                                S.matmul(ocmp[:, r * 97:(r + 1) * 97], pt[:, r * 128:(r + 1) * 128], vce[:, g, :],
                                         start=True, stop=True, inc=(r == 3))

                        def post_c():
                            oc3 = ocmp.v.map(lambda a: a[:, 0:388].rearrange("p (r c) -> p r c", c=97))
                            S.ts("dve", rlc.v, oc3[:, :, 64], 1e-30, ALU.max)
                            S.recip(rlc.v, rlc.v)
                            S.tt("dve", ccc.v, gsig[:, qb, 4 * g:4 * g + 4], rlc.v, ALU.mult)
                            S.tt("dve", nsab[:, 4 * g:4 * g + 4, :], oc3[:, :, 0:64],
                                 ccc.v.map(lambda a: a.unsqueeze(2).to_broadcast([128, 4, 64])), ALU.mult)
                            if qb >= 8:
                                S.tt("dve", impn.v, oc3[:, :, 65:97], rlc.v.map(lambda a: a.unsqueeze(2).to_broadcast([128, 4, 32])), ALU.mult)
                                S.reduce("dve", imp.v, impn.v.map(lambda a: a.rearrange("p r m -> p m r")), ALU.add)
                                S.tt("dve", score.v, imp.v, cand[:, qb - 8, :], ALU.mult)
                                S.tt("dve", score.v, score.v, fval[:, qb - 8, :], ALU.add)
                                S.op("dve", lambda en: en.max(m8a.ap, score.ap), [score.v], [m8a.v])
                                S.op("dve", lambda en: en.match_replace(work.ap, m8a.ap, score.ap, -2.0), [m8a.v, score.v], [work.v])
                                S.op("dve", lambda en: en.max(m8b.ap, work.ap), [work.v], [m8b.v])
                                S.ts("dve", selb4[:, g, 32 * g:32 * g + 32], score.v, m8b[:, 7:8], ALU.is_lt, NEG, ALU.mult)

                                def pe_part():
                                    tbk = tbank[0]
                                    S.transpose(tbk[:, 0, :], selb4[:, g, :], ident.v)
                                    S.copy("dve", sbT[:, g, :], tbk[:, 0, :])
                                defer(pe_part, 8)

                        return (qk_c, ex_c, pv_c, post_c)

                    def mk_branch(qb, g, kts, kT, ve, obank, tabs, goff, mask):
                        gp = g // 2
                        qv = qT[:, gp * 4:gp * 4 + 4, qb * 128:(qb + 1) * 128]
                        nsab = nsa2[qb % 2]
                        sbT = selbT2[qb % 2]
                        out = []
                        for idx, kt in enumerate(kts):
                            last = idx == len(kts) - 1

                            def qk(sb, kt=kt):
                                tab = (qb - kt) in tabs
                                S.matmul(sb.v, kT[:, g, kt * 128:(kt + 1) * 128], qv, start=True, stop=(not mask and not tab))
                                if mask:
                                    S.matmul(sb.v, expand[:, kt * 128:(kt + 1) * 128],
                                             sbT[:, g, :].map(lambda a: a.unsqueeze(1).to_broadcast([128, 4, 128])), start=False, stop=(not tab))
                                if tab:
                                    S.matmul(sb.v, ident.v, ewin[:, tabs[qb - kt], 4 * g:4 * g + 4, :], start=False, stop=True)

                            def ex(sb, pt, kt=kt):
                                S.act(pt.v, sb.v, AF.Exp)

                            def pv(pt, kt=kt, idx=idx, last=last):
                                for r in range(4):
                                    S.matmul(obank[:, r * 65:(r + 1) * 65], pt[:, r * 128:(r + 1) * 128], ve[:, kt, g, :],
                                             start=(idx == 0 and r == 0), stop=last, inc=(r == 3), skip_group_check=True)

                            def post():
                                o3 = obank.v.map(lambda a: a[:, 0:260].rearrange("p (r c) -> p r c", c=65))
                                S.recip(rl.v, o3[:, :, 64])
                                S.tt("dve", ccf.v, gsig[:, qb, goff + 4 * g:goff + 4 * g + 4], rl.v, ALU.mult)
                                S.tt("dve", tmpo.v, o3[:, :, 0:64], ccf.v.map(lambda a: a.unsqueeze(2).to_broadcast([128, 4, 64])), ALU.mult)
                                S.tt("dve", nsab[:, 4 * g:4 * g + 4, :], nsab[:, 4 * g:4 * g + 4, :], tmpo.v, ALU.add)

                            out.append((qk, ex, pv, post if last else None))
                        return out

                    def mk_asm(qb):
                        nsab = nsa2[qb % 2]

                        def assemble():
                            if l == 0:
                                dump("nsa%d" % qb, nsab.v, [128, 16, 64])
                            x = xt[qb % 2]
                            S.tt("dve", sgat.v, sgat.v, nsab.v.map(lambda a: a.rearrange("p h d -> p (h d)")), ALU.mult)
                            S.tt("dve", yb.v, sgat.v, yct.v, ALU.add)

                            def assemble_b():
                                tbk = tbank[1]
                                for c in range(8):
                                    S.transpose(tbk[:, c, :], yb[:, c * 128:(c + 1) * 128], ident.v, inc=(c == 7))
                                S.copy("act", yT.v, tbk.v)
                                for nb in range(2):
                                    bk = ocmp
                                    for c in range(8):
                                        S.matmul(bk.v, yT[:, c, :], wout[nb][:, c, :], start=(c == 0), stop=(c == 7))
                                    S.tt("dve", xo[:, nb * 512:(nb + 1) * 512], bk.v, x[:, nb * 512:(nb + 1) * 512], ALU.add)
                                S.dma("sp", xs[qb].v, xo.v)

                                def stats_and_prefetch():
                                    S.act(junk.v, xo.v, AF.Square, accum_out=ssnb[0][qb].v)
                                    if qb + 1 < NT:
                                        prefetch_asm(qb + 1)
                                defer(stats_and_prefetch, 3)
                            defer(assemble_b, 4)
                        return assemble

                    def prefetch_asm(qb):
                        S.dma("sp", sgat.v, SGA[qb].v)
                        S.dma("sp", yct.v, YC[qb].v)
                        S.dma("sp", xt[qb % 2].v, src_tiles[qb].v)

                    prefetch_asm(0)
                    S.memset("dve", ssn_hi, 0.0)

                    tiles = []
                    for g in range(4):
                        tiles.append(mk_cmp(0, g))
                    for qb in range(NT):
                        for g in range(4):
                            tiles += mk_branch(qb, g, [kt for kt in range(qb - 4, qb + 1) if kt >= 0], kwT, vwe, owin, {0: 0, 1: 1, 4: 2}, 32, False)
                            if qb + 1 < NT:
                                tiles.append(mk_cmp(qb + 1, g))
                            br = mk_branch(qb, g, list(range(qb + 1)), ksT, vse, osel, {0: 0, 1: 1}, 16, qb >= 8)
                            if g == 3:
                                qk_, ex_, pv_, post_ = br[-1]
                                asm = mk_asm(qb)

                                def post_and_asm(post_=post_, asm=asm):
                                    post_()
                                    asm()
                                br[-1] = (qk_, ex_, pv_, post_and_asm)
                            tiles += br

                    nt = len(tiles)
                    for i in range(nt + LA):
                        if i < nt:
                            tiles[i][0](SB[i % 3])
                        j = i - LA
                        if j >= 0:
                            cur["j"] = j
                            qk_, ex_, pv_, post_ = tiles[j]
                            pt = PT[j % 3]
                            ex_(SB[j % 3], pt)
                            pv_(pt)
                            if post_ is not None:
                                post_()
                            for fn in pending.pop(j, []):
                                fn()
                    while pending:
                        j = min(pending)
                        cur["j"] = j
                        for fn in pending.pop(j):
                            fn()
                    S.barrier(skip_pool=True)
                  if l == 0:
                      dump("x1", multi(xs, xs_ap), [SEQ, D])
                  S.barrier(skip_pool=True)

            if "xattn" in stages:
                with ExitStack() as esx:
                    hT = S.sbuf("hTx", [128, 8, SEQ], BF16, esx)
                    norm_T(hT, xs, "norm_x_g", l, pre=True)
                    q2T = S.sbuf("q2T", [128, 8, SEQ], BF16, esx)
                    kxT = S.sbuf("kxT", [128, 8, MEM], BF16, esx)
                    vxe = S.sbuf("vxe", [128, 2, 4, 257], BF16, esx)
                    PTx = [S.sbuf("PTx%d" % i, [128, 512], BF16, esx) for i in range(4)]
                    on = [S.sbuf("on%d" % i, [128, D], BF16, esx) for i in range(4)]
                    onT = S.sbuf("onT", [128, 8, 128], BF16, esx)
                    xo = S.sbuf("xox", [128, D], F32, esx)
                    rlx = S.sbuf("rlx", [128, 1], F32, esx)
                    for cb in range(2):
                        w = next_wb()
                        wload(w.v, I["xq_w"], l, 0, D, cb * 512, 512)
                        for c4 in range(4):
                            c = cb * 4 + c4

                            def lhs(k, w=w, c4=c4):
                                return w[:, k, c4 * 128:(c4 + 1) * 128]

                            def evq(tb, bk, c=c):
                                if ev_eng() == "act":
                                    S.act(q2T[:, c, tb * 512:(tb + 1) * 512], bk.v, AF.Copy, scale=0.0625)
                                else:
                                    S.ts("dve", q2T[:, c, tb * 512:(tb + 1) * 512], bk.v, 0.0625, ALU.mult)
                            fm_proj(hT, lhs, evq)
                    for cb in range(2):
                        w = next_wb()
                        wload(w.v, I["xkv_w"], l, 0, D, cb * 512, 512)
                        for c4 in range(4):
                            c = cb * 4 + c4
                            bk = next_bank()
                            for k in range(8):
                                S.matmul(bk[:, 0:MEM], w[:, k, c4 * 128:(c4 + 1) * 128], memT[:, k, :], start=(k == 0), stop=(k == 7))
                            S.copy(ev_eng(), kxT[:, c, :], bk[:, 0:MEM])
                    S.memset("dve", vxe[:, :, :, 256:257], 1.0)
                    for cb in range(2):
                        w = next_wb()
                        wload(w.v, I["xkv_w"], l, 0, D, D + cb * 512, 512)
                        for mt in range(2):
                            bk = next_bank()
                            for k in range(8):
                                S.matmul(bk.v, memT[:, k, mt * 128:(mt + 1) * 128], w[:, k, :], start=(k == 0), stop=(k == 7))
                            S.copy(ev_eng(), vxe[:, mt, 2 * cb:2 * cb + 2, 0:256], bk.v.map(lambda a: a.rearrange("p (h d) -> p h d", h=2)))
                    wxo = [next_wb(), next_wb()]
                    for nb in range(2):
                        wload(wxo[nb].v, I["xo_w"], l, 0, D, nb * 512, 512)
                    npt = 0
                    for tb in range(4):
                        for hh in range(4):
                            pts = []
                            for mt in range(2):
                                sb = next_bank()
                                for cch in range(2):
                                    S.matmul(sb.v, kxT[:, 2 * hh + cch, mt * 128:(mt + 1) * 128], q2T[:, 2 * hh + cch, tb * 512:(tb + 1) * 512],
                                             start=(cch == 0), stop=(cch == 1))
                                pt = PTx[npt % 4]
                                npt += 1
                                S.act(pt.v, sb.v, AF.Exp)
                                pts.append(pt)
                            for j in range(4):
                                ob = next_bank()
                                for mt in range(2):
                                    S.matmul(ob[:, 0:257], pts[mt][:, j * 128:(j + 1) * 128], vxe[:, mt, hh, :], start=(mt == 0), stop=(mt == 1))
                                S.recip(rlx.v, ob[:, 256:257])
                                S.ts("dve", on[j][:, hh * 256:(hh + 1) * 256], ob[:, 0:256], rlx.v, ALU.mult)
                        for j in range(4):
                            i = tb * 4 + j
                            tbk = tbank[j % 2]
                            for c in range(8):
                                S.transpose(tbk[:, c, :], on[j][:, c * 128:(c + 1) * 128], ident.v, inc=(c == 7))
                            S.copy("act", onT.v, tbk.v)
                            x = xt[st["xt"] % 2]
                            st["xt"] += 1
                            S.dma("sp", x.v, xs[i].v)
                            for nb in range(2):
                                bk = next_bank()
                                for c in range(8):
                                    S.matmul(bk.v, onT[:, c, :], wxo[nb][:, c, :], start=(c == 0), stop=(c == 7))
                                S.tt("dve", xo[:, nb * 512:(nb + 1) * 512], bk.v, x[:, nb * 512:(nb + 1) * 512], ALU.add)
                            S.act(junk.v, xo.v, AF.Square, accum_out=ssnb[0][i].v)
                            S.dma("sp", xs[i].v, xo.v)
                    S.barrier(skip_pool=True)
                if l == 0:
                    dump("x2", multi(xs, xs_ap), [SEQ, D])

            if "ffn" in stages:
                with ExitStack() as esf:
                    uT = S.sbuf("uT", [128, 22, SEQ], BF16, esf)
                    with ExitStack() as esf1:
                        hT = S.sbuf("hTf", [128, 8, SEQ], BF16, esf1)
                        norm_T(hT, xs, "norm_ffn_g", l, pre=True)
                        sa = [S.sbuf("sa%d" % i, [128, 512], F32, esf1) for i in range(2)]
                        for blk in range(6):
                            c0 = blk * 512
                            ncols = min(512, DFF - c0)
                            wa = next_wb()
                            wbb = next_wb()
                            wload(wa[:, :, 0:ncols], I["ffn_in_w"], l, 0, D, c0, ncols)
                            wload(wbb[:, :, 0:ncols], I["ffn_in_w"], l, 0, D, DFF + c0, ncols)
                            for c4 in range(ncols // 128):
                                c = blk * 4 + c4
                                for tb in range(4):
                                    ba = next_bank()
                                    bb = next_bank()
                                    for k in range(8):
                                        S.matmul(ba.v, wa[:, k, c4 * 128:(c4 + 1) * 128], hT[:, k, tb * 512:(tb + 1) * 512], start=(k == 0), stop=(k == 7))
                                    for k in range(8):
                                        S.matmul(bb.v, wbb[:, k, c4 * 128:(c4 + 1) * 128], hT[:, k, tb * 512:(tb + 1) * 512], start=(k == 0), stop=(k == 7))
                                    sv = sa[tb % 2]
                                    S.act(sv.v, ba.v, AF.Silu)
                                    S.tt("dve", uT[:, c, tb * 512:(tb + 1) * 512], sv.v, bb.v, ALU.mult)
                        S.barrier()
                    wfo = S.sbuf("wfo", [128, 22, 512], BF16, esf)
                    xh = [S.sbuf("xh%d" % i, [128, 512], F32, esf) for i in range(2)]
                    xoh = [S.sbuf("xoh%d" % i, [128, 512], F32, esf) for i in range(2)]
                    for nb in range(2):
                        for (c0, nch) in ((0, 8), (8, 8), (16, 6)):
                            wload(wfo[:, c0:c0 + nch, :], I["ffn_out_w"], l, c0 * 128, nch * 128, nb * 512, 512)
                        for i in range(NT):
                            bk = next_bank()
                            for c in range(22):
                                S.matmul(bk.v, uT[:, c, i * 128:(i + 1) * 128], wfo[:, c, :], start=(c == 0), stop=(c == 21))
                            x = xh[i % 2]
                            S.dma("sp", x.v, xs[i][:, nb * 512:(nb + 1) * 512])
                            xq = xoh[i % 2]
                            S.tt("dve", xq.v, bk.v, x.v, ALU.add)
                            S.act(junk[:, 0:512], xq.v, AF.Square, accum_out=ssnb[nb][i].v)
                            S.dma("sp", xs[i][:, nb * 512:(nb + 1) * 512], xq.v)
                    S.barrier(skip_pool=True)
                if l == 0:
                    dump("x3", multi(xs, xs_ap), [SEQ, D])

        out_toks = []
        with ExitStack() as esz:
            fo = [S.sbuf("fo%d" % i, [128, D], F32, esz) for i in range(2)]
            gb = gbuf[st["g"] % 2]
            st["g"] += 1
            bcast_row(gb.v, I["final_norm_g"], I["final_norm_g"].ap[0:1, :])
            fsrc = xs if n_layers > 0 else xin_t
            full = ("ffn" in stages) and n_layers > 0
            if full:
                batch_rstd()
            for i in range(NT):
                x = xt[st["xt"] % 2]
                sq = ss[st["xt"] % 2]
                rs = rstd[st["xt"] % 2]
                st["xt"] += 1
                S.dma("sp", x.v, fsrc[i].v)
                if full:
                    rs = rsn[:, i:i + 1]
                else:
                    S.act(junk.v, x.v, AF.Square, accum_out=sq.v)
                    S.ts("dve", rs.v, sq.v, 1.0 / D, ALU.mult, EPS, ALU.add)
                    S.act(rs.v, rs.v, AF.Sqrt)
                    S.recip(rs.v, rs.v)
                    rs = rs.v
                S.stt("dve", fo[i % 2].v, x.v, rs, gb.v, ALU.mult, ALU.mult)
                out_toks.append(S.dma("sp", out_t[i].v, fo[i % 2].v))
            S.wait_toks("sp", out_toks)
            S.barrier()
        S.wait_toks("sp", dbg_toks)
        S.emit()
        print("ops", {k: len(v.ops) for k, v in S.eng.items()}, "waits", S.nwaits, "dmas", S.ndma, flush=True)
    return nc, dbg_outs


_CONSTS = None


def make_in_map(inputs, b):
    global _CONSTS
    if _CONSTS is None:
        _CONSTS = host_consts()
    m = {}
    for name, shp in IN_SPECS:
        a = np.asarray(inputs[name])
        if name in ("x", "mem"):
            a = a[b]
        m[name] = np.ascontiguousarray(a, dtype=np.float32).reshape(shp)
    m.update(_CONSTS)
    return m


def kernel(**inputs):
    nc, _ = build()
    in_maps = [make_in_map(inputs, b) for b in range(8)]
    res = run_bass_kernel_spmd(nc, in_maps, core_ids=list(range(8)))
    return np.stack([np.asarray(r["out"], dtype=np.float32) for r in res.results], axis=0)
```
